# Optimizing a Trainium2 kernel written in Bass

```python
import math
import jax
import jax.numpy as jnp
from jax import lax
import numpy as np

D_MODEL = 1024
BATCH = 16
SEQ = 2048
DEPTH = 2
DEC_BATCH = 8
DEC_SEQ = 64
PAST_LEN = 1024

CHUNK = 64
CONV_WIDTH = 4
EPS = 1e-6
D_LRU = 512
LRU_BLOCKS = 8
LRU_BLOCK_DIM = D_LRU // LRU_BLOCKS
LRU_C = 8.0
N_ATT_HEADS = 8
N_KV_HEADS = 2
ATT_HEAD_DIM = 64
D_ATT = N_ATT_HEADS * ATT_HEAD_DIM
D_KV = N_KV_HEADS * ATT_HEAD_DIM
N_IDX_HEADS = 4
IDX_DIM = 64
TOPK_MAX = 256
Q_BLOCK = 64
NUM_BUCKETS = 32
MAX_DISTANCE = 128
SSD_HEADS = 16
SSD_HEAD_DIM = 64
D_SSD = SSD_HEADS * SSD_HEAD_DIM
SSD_GROUPS = 2
D_STATE = 128
SSD_CONV_DIM = D_SSD + 2 * SSD_GROUPS * D_STATE
SSD_CHUNK = 64
D_MIX = D_LRU + D_ATT + D_SSD
SPLITS = (D_LRU, D_LRU, D_ATT, D_KV, D_KV, D_ATT, N_IDX_HEADS * IDX_DIM, IDX_DIM, N_IDX_HEADS, D_SSD, SSD_CONV_DIM, SSD_HEADS)
D_IN_PROJ = 2 * D_LRU + 2 * D_ATT + 2 * D_KV + N_IDX_HEADS * IDX_DIM + IDX_DIM + N_IDX_HEADS + D_SSD + SSD_CONV_DIM + SSD_HEADS

kernel_name = 'hymba_lru_dsa_ssd_streaming_step'


def rmsnorm(x, g):
    xf = x.astype(jnp.float32)
    y = xf * lax.rsqrt(jnp.mean(xf * xf, axis=-1, keepdims=True) + EPS)
    return (y * g.astype(jnp.float32)).astype(x.dtype)


def causal_conv(x, buf, w, b):
    T = x.shape[1]
    xp = jnp.concatenate([buf.astype(x.dtype), x], axis=1)
    y = b.astype(x.dtype)
    for j in range(CONV_WIDTH):
        y = y + xp[:, j:j + T] * w[j].astype(x.dtype)
    return y, xp[:, T:]


def rg_lru(x, h0, w_a, b_a, w_x, b_x, lam):
    Bn, T, _ = x.shape
    xf = x.astype(jnp.float32)
    xb = xf.reshape(Bn, T, LRU_BLOCKS, LRU_BLOCK_DIM)
    r = jax.nn.sigmoid(jnp.einsum('btnc,ncd->btnd', xb, w_a.astype(jnp.float32)).reshape(Bn, T, D_LRU) + b_a.astype(jnp.float32))
    i = jax.nn.sigmoid(jnp.einsum('btnc,ncd->btnd', xb, w_x.astype(jnp.float32)).reshape(Bn, T, D_LRU) + b_x.astype(jnp.float32))
    log_a = -LRU_C * r * jax.nn.softplus(-lam.astype(jnp.float32))
    a = jnp.exp(log_a)
    b = jnp.sqrt(-jnp.expm1(2.0 * log_a)) * (i * xf)
    b = b.at[:, 0].add(a[:, 0] * h0.astype(jnp.float32))

    def combine(e1, e2):
        a1, b1 = e1
        a2, b2 = e2
        return a1 * a2, a2 * b1 + b2

    _, h = lax.associative_scan(combine, (a, b), axis=1)
    return h, h[:, -1]


def t5_bucket(rel):
    half = NUM_BUCKETS // 2
    max_exact = half // 2
    n = jnp.abs(rel)
    large = max_exact + (jnp.log(jnp.maximum(n, 1).astype(jnp.float32) / max_exact) / math.log(MAX_DISTANCE / max_exact) * (half - max_exact)).astype(jnp.int32)
    large = jnp.minimum(large, half - 1)
    return jnp.where(rel > 0, half, 0) + jnp.where(n < max_exact, n, large)


def sparse_attention(q, k, v, q_idx, k_idx, w_idx, pos_q, pos_k, rel_bias):
    Bn, T = q.shape[:2]
    L = k.shape[1]
    k_top = min(TOPK_MAX, L // 4)
    qb = min(Q_BLOCK, T)
    nb = T // qb
    G = N_ATT_HEADS // N_KV_HEADS
    chunk_k = pos_k // CHUNK

    def to_blocks(a):
        return jnp.moveaxis(a.reshape(Bn, nb, qb, *a.shape[2:]), 1, 0)

    def block(args):
        qq, qi, wi, pq = args
        chunk_q = pq // CHUNK
        s = jnp.einsum('bqhd,bld->bqhl', qi, k_idx).astype(jnp.float32) * (IDX_DIM ** -0.5)
        score = jnp.einsum('bqh,bqhl->bql', wi.astype(jnp.float32), jax.nn.relu(s)) * (N_IDX_HEADS ** -0.5)
        admissible = chunk_k[None, :] <= chunk_q[:, None]
        score = jnp.where(admissible[None], score, -jnp.inf)
        _, idx = lax.top_k(score, k_top)
        ksel = jax.vmap(lambda kk, ii: kk[ii])(k, idx)
        vsel = jax.vmap(lambda vv, ii: vv[ii])(v, idx)
        psel = pos_k[idx]
        valid = (psel // CHUNK) <= chunk_q[None, :, None]
        qg = qq.reshape(Bn, qb, N_KV_HEADS, G, ATT_HEAD_DIM)
        logits = jnp.einsum('bqgrd,bqkgd->bqgrk', qg, ksel).astype(jnp.float32) * (ATT_HEAD_DIM ** -0.5)
        bias = rel_bias[t5_bucket(psel - pq[None, :, None])]
        bias = jnp.moveaxis(bias, 2, 3).reshape(Bn, qb, N_KV_HEADS, G, k_top).astype(jnp.float32)
        logits = jnp.where(valid[:, :, None, None, :], logits + bias, -jnp.inf)
        p = jax.nn.softmax(logits, axis=-1)
        o = jnp.einsum('bqgrk,bqkgd->bqgrd', p.astype(vsel.dtype), vsel)
        return o.reshape(Bn, qb, D_ATT)

    out = lax.map(block, (to_blocks(q), to_blocks(q_idx), to_blocks(w_idx), pos_q.reshape(nb, qb)))
    return jnp.moveaxis(out, 0, 1).reshape(Bn, T, D_ATT)


def ssd_scan(x, dt, a, bm, cm, h0):
    Bn, T = x.shape[:2]
    lc = min(SSD_CHUNK, T)
    nc = T // lc

    def chunks(arr):
        return arr.reshape(Bn, nc, lc, *arr.shape[2:])

    x, dt, bm, cm = chunks(x), chunks(dt), chunks(bm), chunks(cm)
    cum = jnp.cumsum(dt * a, axis=2)
    tril = jnp.tril(jnp.ones((lc, lc), dtype=bool))
    seg = cum[:, :, :, None] - cum[:, :, None, :]
    decay = jnp.exp(jnp.where(tril[None, None, :, :, None, None], seg, -jnp.inf))
    cb = jnp.einsum('bclgn,bcsgn->bclsg', cm, bm)
    wts = decay * cb[..., None] * dt[:, :, None]
    y_diag = jnp.einsum('bclsgr,bcsgrp->bclgrp', wts, x)
    to_end = jnp.exp(cum[:, :, -1:] - cum) * dt
    states = jnp.einsum('bclgn,bclgr,bclgrp->bcgrpn', bm, to_end, x)
    chunk_decay = jnp.exp(cum[:, :, -1])

    def step(h, inp):
        dec, st = inp
        return dec[..., None, None] * h + st, h

    h_last, h_prev = lax.scan(step, h0, (jnp.moveaxis(chunk_decay, 1, 0), jnp.moveaxis(states, 1, 0)))
    h_prev = jnp.moveaxis(h_prev, 0, 1)
    y_off = jnp.einsum('bclgn,bcgrpn,bclgr->bclgrp', cm, h_prev, jnp.exp(cum))
    return (y_diag + y_off).reshape(Bn, T, *x.shape[3:]), h_last


def mixer_layer(x, past, p):
    past_k, past_v, past_ki, lru_buf, lru_h0, ssd_buf, ssd_h0 = past
    Bn, T, _ = x.shape
    P = past_k.shape[1]
    pos_k = jnp.arange(P + T, dtype=jnp.int32)
    pos_q = pos_k[P:]
    h = rmsnorm(x, p['norm_w'])
    proj = jnp.einsum('btd,de->bte', h, p['w_in'].astype(h.dtype))
    offs = np.cumsum(np.array(SPLITS))[:-1].tolist()
    xl, gl, q, k, v, ga, qi, ki, wi, z, xbc, dtr = jnp.split(proj, offs, axis=-1)
    xl, lru_buf_new = causal_conv(xl, lru_buf, p['lru_conv_w'], p['lru_conv_b'])
    hl, lru_h_new = rg_lru(xl, lru_h0, p['lru_w_a'], p['lru_b_a'], p['lru_w_x'], p['lru_b_x'], p['lru_lambda'])
    y_a = hl.astype(x.dtype) * jax.nn.silu(gl)
    q = rmsnorm(q.reshape(Bn, T, N_ATT_HEADS, ATT_HEAD_DIM), p['att_q_norm'])
    k = rmsnorm(k.reshape(Bn, T, N_KV_HEADS, ATT_HEAD_DIM), p['att_k_norm'])
    v = v.reshape(Bn, T, N_KV_HEADS, ATT_HEAD_DIM)
    qi = qi.reshape(Bn, T, N_IDX_HEADS, IDX_DIM)
    k_all = jnp.concatenate([past_k.astype(k.dtype), k], axis=1)
    v_all = jnp.concatenate([past_v.astype(v.dtype), v], axis=1)
    ki_all = jnp.concatenate([past_ki.astype(ki.dtype), ki], axis=1)
    o = sparse_attention(q, k_all, v_all, qi, ki_all, wi, pos_q, pos_k, p['rel_bias'])
    y_b = o * jax.nn.silu(ga)
    R = SSD_HEADS // SSD_GROUPS
    xbc, ssd_buf_new = causal_conv(xbc, ssd_buf, p['ssd_conv_w'], p['ssd_conv_b'])
    xbc = jax.nn.silu(xbc)
    xs, bm, cm = jnp.split(xbc, [D_SSD, D_SSD + SSD_GROUPS * D_STATE], axis=-1)
    dt = jax.nn.softplus(dtr.astype(jnp.float32) + p['ssd_dt_bias'].astype(jnp.float32))
    a = -jnp.exp(p['ssd_a_log'].astype(jnp.float32))
    xs5 = xs.astype(jnp.float32).reshape(Bn, T, SSD_GROUPS, R, SSD_HEAD_DIM)
    y_ssd, ssd_h_new = ssd_scan(
        xs5,
        dt.reshape(Bn, T, SSD_GROUPS, R),
        a.reshape(SSD_GROUPS, R),
        bm.astype(jnp.float32).reshape(Bn, T, SSD_GROUPS, D_STATE),
        cm.astype(jnp.float32).reshape(Bn, T, SSD_GROUPS, D_STATE),
        ssd_h0.astype(jnp.float32).reshape(Bn, SSD_GROUPS, R, SSD_HEAD_DIM, D_STATE))
    y_ssd = y_ssd + p['ssd_d'].astype(jnp.float32).reshape(SSD_GROUPS, R, 1) * xs5
    y_ssd = y_ssd.reshape(Bn, T, D_SSD) * jax.nn.silu(z.astype(jnp.float32))
    y_c = rmsnorm(y_ssd.reshape(Bn, T, SSD_GROUPS, D_SSD // SSD_GROUPS),
                  p['ssd_norm'].reshape(SSD_GROUPS, D_SSD // SSD_GROUPS)).reshape(Bn, T, D_SSD).astype(x.dtype)
    mix = jnp.concatenate([y_a, y_b, y_c], axis=-1)
    x = x + jnp.einsum('bte,ed->btd', mix, p['w_out'].astype(mix.dtype)).astype(x.dtype)
    new_state = (k, v, ki, lru_buf_new, lru_h_new, ssd_buf_new, ssd_h_new.reshape(Bn, SSD_HEADS, SSD_HEAD_DIM, D_STATE))
    return x, new_state


def stack_states(states):
    return [jnp.stack([s[i] for s in states]) for i in range(len(states[0]))]


def setup_inputs(seed: int = 0) -> dict:
    key = jax.random.key(seed)
    ks = jax.random.split(key, 32)
    f32 = jnp.float32

    def nrm(k, shape, scale):
        return scale * jax.random.normal(k, shape, f32)

    u_lam = jax.random.uniform(ks[14], (DEPTH, D_LRU), f32, 0.9, 0.999)
    dt0 = jnp.exp(jax.random.uniform(ks[19], (DEPTH, SSD_HEADS), f32, math.log(1e-3), math.log(1e-1)))
    return {
        'x_prompt': nrm(ks[0], (BATCH, SEQ, D_MODEL), 1.0),
        'x_sample': nrm(ks[1], (DEC_BATCH, DEC_SEQ, D_MODEL), 1.0),
        'cache_att_k': nrm(ks[2], (DEPTH, DEC_BATCH, PAST_LEN, N_KV_HEADS, ATT_HEAD_DIM), 1.0),
        'cache_att_v': nrm(ks[3], (DEPTH, DEC_BATCH, PAST_LEN, N_KV_HEADS, ATT_HEAD_DIM), 1.0),
        'cache_idx_k': nrm(ks[4], (DEPTH, DEC_BATCH, PAST_LEN, IDX_DIM), 1.0),
        'state_lru_conv': nrm(ks[5], (DEPTH, DEC_BATCH, CONV_WIDTH - 1, D_LRU), 1.0),
        'state_lru_h': nrm(ks[6], (DEPTH, DEC_BATCH, D_LRU), 0.5),
        'state_ssd_conv': nrm(ks[7], (DEPTH, DEC_BATCH, CONV_WIDTH - 1, SSD_CONV_DIM), 1.0),
        'state_ssd_h': nrm(ks[8], (DEPTH, DEC_BATCH, SSD_HEADS, SSD_HEAD_DIM, D_STATE), 0.1),
        'norm_w': 1.0 + nrm(ks[9], (DEPTH, D_MODEL), 0.05),
        'w_in': nrm(ks[10], (DEPTH, D_MODEL, D_IN_PROJ), D_MODEL ** -0.5),
        'lru_conv_w': nrm(ks[11], (DEPTH, CONV_WIDTH, D_LRU), CONV_WIDTH ** -0.5),
        'lru_conv_b': nrm(ks[12], (DEPTH, D_LRU), 0.02),
        'lru_w_a': nrm(ks[13], (DEPTH, LRU_BLOCKS, LRU_BLOCK_DIM, LRU_BLOCK_DIM), LRU_BLOCK_DIM ** -0.5),
        'lru_b_a': nrm(ks[15], (DEPTH, D_LRU), 0.02),
        'lru_w_x': nrm(ks[16], (DEPTH, LRU_BLOCKS, LRU_BLOCK_DIM, LRU_BLOCK_DIM), LRU_BLOCK_DIM ** -0.5),
        'lru_b_x': nrm(ks[17], (DEPTH, D_LRU), 0.02),
        'lru_lambda': jnp.log(u_lam / (1.0 - u_lam)),
        'att_q_norm': 1.0 + nrm(ks[18], (DEPTH, ATT_HEAD_DIM), 0.05),
        'att_k_norm': 1.0 + nrm(ks[20], (DEPTH, ATT_HEAD_DIM), 0.05),
        'rel_bias': nrm(ks[21], (NUM_BUCKETS, N_ATT_HEADS), 0.5),
        'ssd_conv_w': nrm(ks[22], (DEPTH, CONV_WIDTH, SSD_CONV_DIM), CONV_WIDTH ** -0.5),
        'ssd_conv_b': nrm(ks[23], (DEPTH, SSD_CONV_DIM), 0.02),
        'ssd_dt_bias': dt0 + jnp.log(-jnp.expm1(-dt0)),
        'ssd_a_log': jnp.log(jax.random.uniform(ks[24], (DEPTH, SSD_HEADS), f32, 1.0, 16.0)),
        'ssd_d': 1.0 + nrm(ks[25], (DEPTH, SSD_HEADS), 0.1),
        'ssd_norm': 1.0 + nrm(ks[26], (DEPTH, D_SSD), 0.05),
        'w_out': nrm(ks[27], (DEPTH, D_MIX, D_MODEL), D_MIX ** -0.5),
    }


def reference(x_prompt, x_sample, cache_att_k, cache_att_v, cache_idx_k, state_lru_conv, state_lru_h,
              state_ssd_conv, state_ssd_h, norm_w, w_in, lru_conv_w, lru_conv_b, lru_w_a, lru_b_a,
              lru_w_x, lru_b_x, lru_lambda, att_q_norm, att_k_norm, rel_bias, ssd_conv_w, ssd_conv_b,
              ssd_dt_bias, ssd_a_log, ssd_d, ssd_norm, w_out):
    dtype = x_prompt.dtype
    nb_p = x_prompt.shape[0]
    hp = x_prompt
    hs = x_sample
    prompt_states = []
    sample_states = []
    for l in range(DEPTH):
        p = {
            'norm_w': norm_w[l], 'w_in': w_in[l],
            'lru_conv_w': lru_conv_w[l], 'lru_conv_b': lru_conv_b[l],
            'lru_w_a': lru_w_a[l], 'lru_b_a': lru_b_a[l], 'lru_w_x': lru_w_x[l], 'lru_b_x': lru_b_x[l],
            'lru_lambda': lru_lambda[l],
            'att_q_norm': att_q_norm[l], 'att_k_norm': att_k_norm[l], 'rel_bias': rel_bias,
            'ssd_conv_w': ssd_conv_w[l], 'ssd_conv_b': ssd_conv_b[l], 'ssd_dt_bias': ssd_dt_bias[l],
            'ssd_a_log': ssd_a_log[l], 'ssd_d': ssd_d[l], 'ssd_norm': ssd_norm[l],
            'w_out': w_out[l],
        }
        empty = (
            jnp.zeros((nb_p, 0, N_KV_HEADS, ATT_HEAD_DIM), dtype),
            jnp.zeros((nb_p, 0, N_KV_HEADS, ATT_HEAD_DIM), dtype),
            jnp.zeros((nb_p, 0, IDX_DIM), dtype),
            jnp.zeros((nb_p, CONV_WIDTH - 1, D_LRU), dtype),
            jnp.zeros((nb_p, D_LRU), dtype),
            jnp.zeros((nb_p, CONV_WIDTH - 1, SSD_CONV_DIM), dtype),
            jnp.zeros((nb_p, SSD_HEADS, SSD_HEAD_DIM, D_STATE), dtype),
        )
        hp, st_p = mixer_layer(hp, empty, p)
        prompt_states.append(st_p)
        past = (cache_att_k[l], cache_att_v[l], cache_idx_k[l], state_lru_conv[l], state_lru_h[l],
                state_ssd_conv[l], state_ssd_h[l])
        hs, st_s = mixer_layer(hs, past, p)
        sample_states.append(st_s)
    att_k_p, att_v_p, idx_k_p, lru_conv_p, lru_h_p, ssd_conv_p, ssd_h_p = stack_states(prompt_states)
    att_k_s, att_v_s, idx_k_s, lru_conv_s, lru_h_s, ssd_conv_s, ssd_h_s = stack_states(sample_states)
    return (hp, hs, att_k_p, att_v_p, idx_k_p, lru_conv_p, lru_h_p, ssd_conv_p, ssd_h_p,
            att_k_s, att_v_s, idx_k_s, lru_conv_s, lru_h_s, ssd_conv_s, ssd_h_s)
```

```python
import numpy as np
from contextlib import ExitStack
import concourse.bass as bass
import concourse.mybir as mybir
from concourse.bass_utils import run_bass_kernel_spmd

F32 = mybir.dt.float32
BF16 = mybir.dt.bfloat16
AF = mybir.ActivationFunctionType
ALU = mybir.AluOpType
AX = mybir.AxisListType


def _region(ap):
    t = ap.tensor
    pat = [(int(s), int(c)) for (s, c) in ap.ap]
    off = int(ap.offset)
    kind = type(t).__name__
    if kind.startswith("DRam"):
        lo = off
        ext = sum((c - 1) * abs(s) for s, c in pat)
        n = 1
        for s, c in pat:
            if s != 0:
                n *= c
        return (t.name, 0, 1, lo, lo + ext + 1, n == ext + 1)
    shp = [int(v) for v in t.shape]
    pstride = 1
    for v in shp[1:]:
        pstride *= v
    p0 = off // pstride
    lo = off % pstride
    npart = pat[0][1]
    ext = sum((c - 1) * abs(s) for s, c in pat[1:])
    n = 1
    for s, c in pat[1:]:
        if s != 0:
            n *= c
    if kind.startswith("PSum"):
        full = (lo == 0 and lo + ext + 1 == pstride and n == ext + 1)
        return (t.name, (p0 // 32) * 32, ((p0 + npart + 31) // 32) * 32, 0, pstride, full and p0 % 32 == 0 and (p0 + npart) % 32 == 0)
    return (t.name, p0, p0 + npart, lo, lo + ext + 1, n == ext + 1)


def _ovl(a, b):
    return a[1] < b[2] and b[1] < a[2] and a[3] < b[4] and b[3] < a[4]


def _covers(a, b):
    return a[5] and a[1] <= b[1] and a[2] >= b[2] and a[3] <= b[3] and a[4] >= b[4]


class _Stop(Exception):
    pass


STOP_AT = [99]
DEBUG_MIX = [False]


CK_OFF = [0]


def drain(gen):
    for _ in gen:
        pass


def interleave(ga_, gb_):
    a_live = b_live = True
    while a_live or b_live:
        if a_live:
            a_live = next(ga_, "end") != "end"
        if b_live:
            b_live = next(gb_, "end") != "end"


def chk(k):
    if k + CK_OFF[0] > STOP_AT[0]:
        raise _Stop()


class Prog:
    def __init__(self, nc):
        self.nc = nc
        self.stack = ExitStack()
        self.ops = []
        self.wr = {}
        self.rd = {}
        self.finished = False

    def sb(self, name, shape, dt):
        return self.stack.enter_context(self.nc.sbuf_tensor("sb_" + name, list(shape), dt))

    def ps(self, name, shape, dt):
        return self.stack.enter_context(self.nc.psum_tensor("ps_" + name, list(shape), dt))

    def _add(self, eng, fn, outs, ins, is_dma=False):
        idx = len(self.ops)
        deps = {}
        rregs = [_region(a) for a in ins]
        wregs = [_region(a) for a in outs]
        for r in rregs:
            for w in self.wr.get(r[0], ()):
                if _ovl(r, w[1]):
                    deps.setdefault(w[0], set()).add("raw")
        for r in wregs:
            for w in self.wr.get(r[0], ()):
                if _ovl(r, w[1]):
                    deps.setdefault(w[0], set()).add("waw")
            for w in self.rd.get(r[0], ()):
                if _ovl(r, w[1]):
                    deps.setdefault(w[0], set()).add("war")
        for r in wregs:
            lw = self.wr.setdefault(r[0], [])
            lw[:] = [w for w in lw if not _covers(r, w[1])]
            lw.append((idx, r))
            lr = self.rd.get(r[0])
            if lr:
                lr[:] = [w for w in lr if not _covers(r, w[1])]
        for r in rregs:
            self.rd.setdefault(r[0], []).append((idx, r))
        self.ops.append(dict(eng=eng, fn=fn, deps=deps, dma=is_dma, sig=False))
        return idx

    def pe(self, fn, outs, ins):
        return self._add("pe", fn, outs, ins)

    def dve(self, fn, outs, ins):
        return self._add("dve", fn, outs, ins)

    def act(self, fn, outs, ins):
        return self._add("act", fn, outs, ins)

    def pool(self, fn, outs, ins):
        return self._add("pool", fn, outs, ins)

    def dma(self, q, out, in_, **kw):
        return self._add(q, lambda e: e.dma_start(out=out, in_=in_, **kw), [out], [in_], is_dma=True)

    def finish(self):
        nc = self.nc
        ops = self.ops
        engs = ["pe", "dve", "act", "pool", "sp"]
        for i, op in enumerate(ops):
            nd = set()
            for j, kinds in op["deps"].items():
                o = ops[j]
                if o["dma"]:
                    nd.add(j)
                    continue
                if o["eng"] == op["eng"] and not op["dma"]:
                    if op["eng"] == "pe":
                        continue
                nd.add(j)
            op["deps"] = nd
            for j in nd:
                ops[j]["sig"] = True
        esem = {e: self.stack.enter_context(nc.semaphore("s_" + e)) for e in engs}
        nds = {"sp": 40, "pool": 16, "act": 8}
        dsem = {q: [self.stack.enter_context(nc.semaphore("d_%s%d" % (q, k))) for k in range(n)] for q, n in nds.items()}
        dcnt = {q: [0] * n for q, n in nds.items()}
        dnext = {q: 0 for q in nds}
        ecnt = {e: 0 for e in engs}
        for i, op in enumerate(ops):
            if op["dma"]:
                q = op["eng"]
                k = dnext[q] % nds[q]
                dnext[q] += 1
                prev = dcnt[q][k]
                dcnt[q][k] += 16
                op["event"] = (dsem[q][k], dcnt[q][k])
                op["prevev"] = (dsem[q][k], prev) if prev > 0 else None
            elif op["sig"]:
                ecnt[op["eng"]] += 1
                op["event"] = (esem[op["eng"]], ecnt[op["eng"]])
        waited = {e: {} for e in engs}
        per_eng = {e: [] for e in engs}
        for i, op in enumerate(ops):
            e = op["eng"]
            need = {}
            for j in op["deps"]:
                s, v = ops[j]["event"]
                key = id(s)
                if need.get(key, (None, 0))[1] < v:
                    need[key] = (s, v)
            if op["dma"] and op["prevev"] is not None:
                s, v = op["prevev"]
                key = id(s)
                if need.get(key, (None, 0))[1] < v:
                    need[key] = (s, v)
            waits = []
            for key, (s, v) in need.items():
                if waited[e].get(key, 0) >= v:
                    continue
                waited[e][key] = v
                waits.append((s, v))
            op["waits"] = waits
            per_eng[e].append(op)
        self.n_waits = sum(len(o["waits"]) for o in ops)
        final = []
        for q in nds:
            for k in range(nds[q]):
                if dcnt[q][k] > 0:
                    final.append((dsem[q][k], dcnt[q][k]))

        def emit(ename, e):
            for op in per_eng[ename]:
                for (s, v) in op["waits"]:
                    e.wait_ge(s, v)
                ins = op["fn"](e)
                if op["dma"]:
                    ins.then_inc(op["event"][0], 16)
                elif op["sig"]:
                    ins.then_inc(op["event"][0], 1)
            if ename == "sp":
                for (s, v) in final:
                    e.wait_ge(s, v)

        with nc.Block() as block:
            @block.tensor
            def _(e):
                emit("pe", e)

            @block.vector
            def _(e):
                emit("dve", e)

            @block.scalar
            def _(e):
                emit("act", e)

            @block.gpsimd
            def _(e):
                emit("pool", e)

            @block.sync
            def _(e):
                emit("sp", e)
        self.stack.close()
        self.finished = True


D_MODEL = 1024
D_IN = 5204
D_MIX = 2048
EPS = 1e-6
NEG = -30000.0
C_XL, C_GL, C_Q, C_K, C_GA, C_QI, C_KI, C_Z, C_XBC, C_DT = 0, 512, 1024, 1536, 1792, 2304, 2560, 2628, 3652, 5188
NPP = 100
NPB = 1208
NCST = 1152
PB_GQ, PB_GK, PB_DTB, PB_ALOG, PB_D, PB_GSSD, PB_RB15 = 0, 64, 128, 144, 160, 176, 1200
PP_GN, PP_LCW, PP_LCB, PP_LBA, PP_LBX, PP_LAM, PP_SCW, PP_SCB = 0, 8, 24, 28, 32, 36, 40, 88


def build_program(NSP=2, TP=2048, SAMPLE=True, PS=1024, TS=64, DEPTH=2, TG=256, NBIS=16):
    nc = bass.Bass("TRN2", target_bir_lowering=False)
    pg = Prog(nc)
    dt_in = lambda name, shape: nc.dram_tensor(name, list(shape), F32, kind="ExternalInput").ap()
    dt_out = lambda name, shape: nc.dram_tensor(name, list(shape), F32, kind="ExternalOutput").ap()
    dt_tmp = lambda name, shape: nc.dram_tensor(name, list(shape), F32, kind="Internal").ap()
    I = {}
    I["xp"] = dt_in("xp", [NSP, TP, D_MODEL])
    I["w_in"] = dt_in("w_in", [DEPTH, D_MODEL, D_IN])
    I["w_out"] = dt_in("w_out", [DEPTH, D_MIX, D_MODEL])
    I["pp"] = dt_in("pp", [DEPTH, 128, NPP])
    I["pb"] = dt_in("pb", [DEPTH, 128, NPB])
    I["wabd"] = dt_in("wabd", [DEPTH, 128, 2 * 4 * 128])
    I["rb"] = dt_in("rb", [32, 8])
    I["cst"] = dt_in("cst", [128, NCST])
    O = {}
    O["yp"] = dt_out("yp", [NSP, TP, D_MODEL])
    O["akp"] = dt_out("akp", [DEPTH, NSP, TP, 128])
    O["avp"] = dt_out("avp", [DEPTH, NSP, TP, 128])
    O["ikp"] = dt_out("ikp", [DEPTH, NSP, TP, 64])
    O["lcp"] = dt_out("lcp", [DEPTH, NSP, 3, 512])
    O["lhp"] = dt_out("lhp", [DEPTH, NSP, 512])
    O["scp"] = dt_out("scp", [DEPTH, NSP, 3, 1536])
    O["shp"] = dt_out("shp", [DEPTH, NSP, 1024, 128])
    xmid_p = dt_tmp("xmid_p", [NSP, TP, D_MODEL])
    if DEBUG_MIX[0]:
        O["dbg"] = nc.dram_tensor("dbg", [16, 128, TP], BF16, kind="ExternalOutput").ap()
    vecd = dt_tmp("vecd", [8, 384])
    if SAMPLE:
        I["xs"] = dt_in("xs", [TS, D_MODEL])
        I["ck"] = dt_in("ck", [DEPTH, PS, 128])
        I["cv"] = dt_in("cv", [DEPTH, PS, 128])
        I["cki"] = dt_in("cki", [DEPTH, PS, 64])
        I["slc"] = dt_in("slc", [DEPTH, 3, 512])
        I["slh"] = dt_in("slh", [DEPTH, 512])
        I["ssc"] = dt_in("ssc", [DEPTH, 3, 1536])
        I["ssh"] = dt_in("ssh", [DEPTH, 1024, 128])
        O["ys"] = dt_out("ys", [TS, D_MODEL])
        O["aks"] = dt_out("aks", [DEPTH, TS, 128])
        O["avs"] = dt_out("avs", [DEPTH, TS, 128])
        O["iks"] = dt_out("iks", [DEPTH, TS, 64])
        O["lcs"] = dt_out("lcs", [DEPTH, 3, 512])
        O["lhs"] = dt_out("lhs", [DEPTH, 512])
        O["scs"] = dt_out("scs", [DEPTH, 3, 1536])
        O["shs"] = dt_out("shs", [DEPTH, 1024, 128])
        xmid_s = dt_tmp("xmid_s", [TS, D_MODEL])

    LMAX = max(TP, (PS + TS) if SAMPLE else 0)
    LMAX = ((LMAX + 127) // 128) * 128
    NKB = LMAX // 128
    sb, ps = pg.sb, pg.ps
    win = sb("win", [128, 8, D_IN], BF16)
    wout = sb("wout", [128, 16, D_MODEL], BF16)
    ppt = sb("ppt", [128, NPP], F32)
    pbt = sb("pbt", [128, NPB], F32)
    wabd = sb("wabdb", [128, 8, 128], BF16)
    cst = sb("cst", [128, 768], F32)
    ident = cst[:, 0:128]
    tri = cst[:, 128:256]
    astr = cst[:, 256:384]
    dmask = cst[:, 640:768]
    cbf = sb("cbf", [128, 6, 128], BF16)
    trib, astrb = cbf[:, 4, :], cbf[:, 5, :]
    RB = sb("RB", [128, 2, 512], BF16)
    JK = sb("JK", [128, LMAX], mybir.dt.uint8)
    dAs = sb("dAs", [128, 3, 16], BF16)
    identb, j128b, j64b, onesb = cbf[:, 0, :], cbf[:, 1, :], cbf[:, 2, :], cbf[:, 3, :]
    hk = sb("hk", [128, 2, 8, 128], BF16)
    p2row = sb("p2row", [128, NBIS + 1], F32)
    rbt = sb("rbt", [32, 8], F32)
    rb15row = sb("rb15row", [1, 8, 128], BF16)
    vecs = sb("vecs", [8, 384], F32)
    drv = sb("drv", [128, 64], F32)
    c1 = drv[:, 0:4]
    aneg = drv[:, 4:20]
    gq8 = sb("gq8", [128, 64], F32)
    hT = sb("hT", [128, 8, TG], BF16)
    mixT = sb("mixT", [128, 16, TG], BF16)
    khT = sb("khT", [128, LMAX], BF16)
    vtok = sb("vtok", [128, NKB, 128], BF16)
    kiT = sb("kiT", [128, LMAX], BF16)
    qhT = sb("qhT", [128, 4, TG], BF16)
    qiT = sb("qiT", [128, 2, TG], BF16)
    gaT = sb("gaT", [128, 4, TG], BF16)
    lhist = sb("lhist", [128, 4, 3], F32)
    shist = sb("shist", [128, 12, 3], F32)
    lh = sb("lh", [128, 4], F32)
    xbcT = sb("xbcT", [128, 12, TG], BF16)
    sz = sb("sz", [128, 2, 1024], BF16)
    hst = sb("hst", [128, 1024], F32)
    hsb = sb("hsb", [128, 1024], BF16)
    sm = sb("sm", [128, 256], F32)
    kiw = sb("kiw", [128, 2, 68], F32)
    dtt = sb("dtt", [128, 2, 16], F32)
    dAt = sb("dAt", [128, 2, 16], F32)
    A32 = sb("A32", [128, 3328], F32)
    AB = sb("AB", [128, 4608], BF16)
    pA = ps("pA", [128, 512], F32)
    pB = ps("pB", [128, 512], F32)
    pC = ps("pC", [128, 512], F32)
    pD = ps("pD", [128, 512], F32)
    pE = ps("pE", [128, 512], F32)
    pF = ps("pF", [128, 512], F32)
    pG = ps("pG", [128, 512], F32)
    pH = ps("pH", [128, 512], F32)

    act, dve, pe, pool, dma = pg.act, pg.dve, pg.pe, pg.pool, pg.dma

    def A_(fn, out, ins):
        return act(fn, [out] if not isinstance(out, list) else out, ins)

    def acopy(out, in_):
        act(lambda e: e.activation(out=out, in_=in_, func=AF.Copy), [out], [in_])

    def afunc(out, in_, func, bias=None, scale=None, accum=None):
        kw = {}
        reads = [in_]
        outs = [out]
        if bias is not None:
            kw["bias"] = bias
            if not isinstance(bias, float):
                reads.append(bias)
        if scale is not None:
            kw["scale"] = scale
            if not isinstance(scale, float):
                reads.append(scale)
        if accum is not None:
            kw["accum_out"] = accum
            outs.append(accum)
        act(lambda e: e.activation(out=out, in_=in_, func=func, **kw), outs, reads)

    def vtt(out, a, b, op):
        dve(lambda e: e.tensor_tensor(out=out, in0=a, in1=b, op=op), [out], [a, b])

    def vts(out, a, s1, op0, s2=None, op1=None, accum=None):
        reads = [a] + [s for s in (s1, s2) if s is not None and not isinstance(s, float)]
        outs = [out] + ([accum] if accum is not None else [])
        kw = {}
        if op1 is not None:
            kw["op1"] = op1
        if accum is not None:
            kw["accum_out"] = accum
        dve(lambda e: e.tensor_scalar(out=out, in0=a, scalar1=s1, scalar2=s2, op0=op0, **kw), outs, reads)

    def vstt(out, a, s, b, op0, op1):
        reads = [a, b] + ([s] if not isinstance(s, float) else [])
        dve(lambda e: e.scalar_tensor_tensor(out=out, in0=a, scalar=s, in1=b, op0=op0, op1=op1), [out], reads)

    def vcopy(out, in_):
        dve(lambda e: e.tensor_copy(out=out, in_=in_), [out], [in_])

    def vscan(out, d0, d1, init):
        reads = [d0, d1] + ([init] if not isinstance(init, float) else [])
        dve(lambda e: e.tensor_tensor_scan(out=out, data0=d0, data1=d1, initial=init, op0=ALU.mult, op1=ALU.add), [out], reads)

    def vreduce(out, in_, op, absval=False):
        if absval:
            dve(lambda e: e.tensor_reduce(out=out, in_=in_, axis=AX.X, op=op, apply_absolute_value=True), [out], [in_])
        else:
            dve(lambda e: e.tensor_reduce(out=out, in_=in_, axis=AX.X, op=op), [out], [in_])

    def vmemset(ap, val):
        dve(lambda e: e.memset(ap, val), [ap], [])

    def ptt(out, a, b, op):
        pool(lambda e: e.tensor_tensor(out=out, in0=a, in1=b, op=op), [out], [a, b])

    def split3(dst, src32, tmpa, tmpb):
        vcopy(dst[:, 0, :], src32)
        vtt(tmpa, src32, dst[:, 0, :], ALU.subtract)
        vcopy(dst[:, 1, :], tmpa)
        vtt(tmpb, tmpa, dst[:, 1, :], ALU.subtract)
        vcopy(dst[:, 2, :], tmpb)

    def rstd_pow(out, ss, n, tmp):
        afunc(tmp, ss, AF.Ln, scale=1.0 / n, bias=EPS)
        afunc(out, tmp, AF.Exp, scale=-0.5)

    def sigm(buf, src, scale=-1.0, bias=None):
        afunc(buf, src, AF.Exp, scale=scale, bias=bias)
        afunc(buf, buf, AF.Ln, bias=1.0)
        afunc(buf, buf, AF.Exp, scale=-1.0)

    def vrecip(out, in_):
        dve(lambda e: e.reciprocal(out=out, in_=in_), [out], [in_])

    def mm(out, lhsT, rhs, start, stop=True):
        pe(lambda e: e.matmul(out, lhsT=lhsT, rhs=rhs, start=start, stop=stop), [out], [lhsT, rhs])

    def tr(out, in_, idt):
        pe(lambda e: e.transpose(out, in_, idt), [out], [in_, idt])

    def bc(ap, axis, n):
        a = ap.unsqueeze(axis)
        shp = list(a.shape)
        shp[axis] = n
        return a.broadcast_to(shp)

    try:
      _body(locals())
    except _Stop:
      pass
    pg.finish()
    return nc


def _body(env):
    globals().update({k: v for k, v in env.items() if not k.startswith("__")})
    dma("sp", cst[:], I["cst"][:, 0:768])
    dma("sp", A32[0:32, 0:384], I["cst"][0:32, 768:1152])
    dma("sp", rbt[:], I["rb"])
    vcopy(cbf[:, 0, :], cst[:, 0:128])
    vcopy(cbf[:, 1, :], cst[:, 384:512])
    vcopy(cbf[:, 2, :], cst[:, 512:640])
    vmemset(cbf[:, 3, :], 1.0)
    vcopy(cbf[:, 4, :], cst[:, 128:256])
    vcopy(cbf[:, 5, :], cst[:, 256:384])
    mm(pA[0:8, 0:384], rbt[:, :], A32[0:32, 0:384], True)
    vcopy(vecs[:], pA[0:8, 0:384])
    vcopy(rbt[0:8, 0:1], vecs[:, 0:1])
    vts(vecs[:], vecs[:], rbt[0:8, 0:1], ALU.subtract)
    dma("sp", vecd, vecs[:])
    hktmp = A32[:, 0:1024].rearrange("p (h s) -> p h s", h=8)
    for k in range(NBIS + 1):
        vmemset(p2row[:, k:k + 1], 2.0 ** -k)
    for si, (TTq, dl) in enumerate([(128, 0), (128, -128)]):
        base = dl - TTq + 256
        src = bass.AP(tensor=vecd.tensor, offset=base, ap=[[1, TTq], [384, 8], [1, 128]])
        dma("sp", hktmp[:TTq], src)
        vcopy(hk[:TTq, si, :, :], hktmp[:TTq])

    CK_OFF[0] = 0
    chk(1)
    seqs = []
    for q in range(NSP):
        seqs.append(dict(kind="p", idx=q, T=TP, P=0))
    if SAMPLE:
        seqs.append(dict(kind="s", idx=0, T=TS, P=PS))

    for l in range(DEPTH):
        for kc in range(8):
            for hf in range(2):
                c0, c1_ = hf * 2602, (hf + 1) * 2602
                dma("pool", win[:, kc, c0:c1_], I["w_in"][l, kc * 128:(kc + 1) * 128, c0:c1_])
        for ec in range(16):
            dma("pool", wout[:, ec, :], I["w_out"][l, ec * 128:(ec + 1) * 128, :])
        dma("sp", ppt[:], I["pp"][l])
        dma("sp", pbt[:], I["pb"][l])
        dma("pool", wabd[:].rearrange("p a b -> p (a b)"), I["wabd"][l])
        afunc(drv[:, 20:24], ppt[:, PP_LAM:PP_LAM + 4], AF.Exp, scale=-1.0)
        afunc(drv[:, 24:28], drv[:, 20:24], AF.Ln, bias=1.0)
        vts(c1, drv[:, 24:28], -8.0, ALU.mult)
        afunc(drv[:, 28:44], pbt[:, PB_ALOG:PB_ALOG + 16], AF.Exp)
        vts(aneg, drv[:, 28:44], -1.0, ALU.mult)
        vts(gq8[:], pbt[:, PB_GQ:PB_GQ + 64], 0.125, ALU.mult)
        vts(drv[:, 48:52], ppt[:, PP_LBA:PP_LBA + 4], -1.0, ALU.mult)
        vts(drv[:, 52:56], ppt[:, PP_LBX:PP_LBX + 4], -1.0, ALU.mult)
        vcopy(rb15row[0:1, :, :], bc(pbt[0:1, PB_RB15:PB_RB15 + 8], 2, 128))

        chk(2)
        for sq in seqs:
            T, P0 = sq["T"], sq["P"]
            isS = sq["kind"] == "s"
            qi_ = sq["idx"]
            L = P0 + T
            ktop = min(256, L // 4)
            if isS:
                xin = I["xs"] if l == 0 else xmid_s
                xout = O["ys"] if l == DEPTH - 1 else xmid_s
                o_ak, o_av, o_ik = O["aks"][l], O["avs"][l], O["iks"][l]
                o_lc, o_lh, o_sc, o_sh = O["lcs"][l], O["lhs"][l], O["scs"][l], O["shs"][l]
            else:
                xin = I["xp"][qi_] if l == 0 else xmid_p[qi_]
                xout = O["yp"][qi_] if l == DEPTH - 1 else xmid_p[qi_]
                o_ak, o_av, o_ik = O["akp"][l, qi_], O["avp"][l, qi_], O["ikp"][l, qi_]
                o_lc, o_lh, o_sc, o_sh = O["lcp"][l, qi_], O["lhp"][l, qi_], O["scp"][l, qi_], O["shp"][l, qi_]
            if isS:
                for g in range(4):
                    dma("sp", lhist[:, g, :], I["slc"][l].rearrange("j (g p) -> p g j", p=128)[:, g, :], allow_slow_non_contiguous=True)
                for g in range(12):
                    dma("sp", shist[:, g, :], I["ssc"][l].rearrange("j (g p) -> p g j", p=128)[:, g, :], allow_slow_non_contiguous=True)
                dma("sp", lh[:], I["slh"][l].rearrange("(g p) -> p g", p=128), allow_slow_non_contiguous=True)
                stt = A32[:, 0:1024].rearrange("p (c n) -> p c n", c=8)
                dma("sp", stt, I["ssh"][l].rearrange("(c p) n -> p c n", p=128))
                sp3 = AB[:, 0:3072].rearrange("p (k n) -> p k n", k=3)
                split3(sp3, A32[:, 0:1024], A32[:, 1024:2048], A32[:, 2048:3072])
                for c in range(8):
                    pbank = pA if c < 4 else pB
                    for k in range(3):
                        mm(pbank[:, (c % 4) * 128:(c % 4 + 1) * 128], sp3[:, k, c * 128:(c + 1) * 128], identb, c % 4 == 0 and k == 0, c % 4 == 3 and k == 2)
                vcopy(hst[:, 0:512], pA[:, :])
                vcopy(hst[:, 512:1024], pB[:, :])
                acopy(hsb[:, :], hst[:, :])
                nb = P0 // 128
                ckb = AB[:, 0:nb * 128].rearrange("p (c n) -> p c n", c=nb)
                kid = AB[:, nb * 128:2 * nb * 128].rearrange("p (c n) -> p c n", c=nb)
                dma("pool", ckb, I["ck"][l].rearrange("(c p) n -> p c n", p=128))
                dma("pool", vtok[:, 0:nb, :], I["cv"][l].rearrange("(c p) n -> p c n", p=128))
                dma("pool", kid[:, :, 0:64], I["cki"][l].rearrange("(c p) n -> p c n", p=128))
                dma("pool", kid[:, :, 64:128], I["cki"][l].rearrange("(c p) n -> p c n", p=128))
                for c in range(nb):
                    mm(pC[:, (c % 4) * 128:(c % 4 + 1) * 128], ckb[:, c, :], identb, c % 4 == 0, c % 4 == 3 or c == nb - 1)
                    mm(pD[:, (c % 4) * 128:(c % 4 + 1) * 128], kid[:, c, :], identb, c % 4 == 0, c % 4 == 3 or c == nb - 1)
                    if c % 4 == 3 or c == nb - 1:
                        c0 = (c // 4) * 4
                        n_ = c - c0 + 1
                        vcopy(khT[:, c0 * 128:(c0 + n_) * 128], pC[:, 0:n_ * 128])
                        acopy(kiT[:, c0 * 128:(c0 + n_) * 128], pD[:, 0:n_ * 128])
            else:
                vmemset(lhist[:], 0.0)
                vmemset(shist[:], 0.0)
                vmemset(lh[:], 0.0)
                vmemset(hst[:], 0.0)
                vmemset(hsb[:], 0.0)

            CK_OFF[0] = 10 if isS else 0
            chk(3)
            ngroups = (T + TG - 1) // TG
            for gi in range(ngroups):
                t0 = gi * TG
                NV = min(TG, T - t0)
                NT = ((NV + 127) // 128) * 128
                TT = 128
                ntile = NT // TT
                for ti in range(ntile):
                    r0 = t0 + ti * TT
                    TV = min(TT, NV - ti * TT)
                    xt32 = A32[:TT, 0:1024]
                    if TV < TT:
                        vmemset(A32[TV:TT, 0:1024], 0.0)
                    dma("sp", A32[:TV, 0:1024], xin[r0:r0 + TV, :])
                    junk = AB[:TT, 0:1024]
                    xn = AB[:TT, 1024:2048]
                    ssq = sm[:TT, ti:ti + 1]
                    afunc(junk, xt32, AF.Square, accum=ssq)
                    rstd_pow(sm[:TT, 4 + ti:5 + ti], ssq, D_MODEL, sm[:TT, 2 + ti:3 + ti])
                    vts(xn, xt32, sm[:TT, 4 + ti:5 + ti], ALU.mult)
                    for kc in range(8):
                        pb_ = pA if kc < 4 else pB
                        mm(pb_[:, (kc % 4) * TT:(kc % 4 + 1) * TT], xn[:, kc * 128:(kc + 1) * 128], identb[:TT, :TT], kc % 4 == 0, kc % 4 == 3)
                    for hf in range(2):
                        pb_ = pA if hf == 0 else pB
                        vtt(hT[:, hf * 4:(hf + 1) * 4, ti * TT:(ti + 1) * TT], pb_[:, 0:4 * TT].rearrange("p (c t) -> p c t", c=4),
                            bc(ppt[:, PP_GN + hf * 4:PP_GN + hf * 4 + 4], 2, TT), ALU.mult)

                chk(4)
                def proj_fm(pbank, c0, M, prow=0):
                    for kc in range(8):
                        mm(pbank[prow:prow + M, :NT], win[:, kc, c0:c0 + M], hT[:, kc, :NT], kc == 0, kc == 7)

                def conv(pbank, hist, g, wcol, bcol, out32, roff=0, ppt=ppt):
                    raw = A32[:, roff:roff + 3 + NT]
                    vcopy(raw[:, 0:3], hist[:, g, :])
                    acopy(raw[:, 3:3 + NT], pbank[:, :NT])
                    vcopy(hist[:, g, :], raw[:, NV:NV + 3])
                    vts(out32, raw[:, 0:NT], ppt[:, wcol:wcol + 1], ALU.mult, ppt[:, bcol:bcol + 1], ALU.add)
                    for j in range(1, 4):
                        vstt(out32, raw[:, j:j + NT], ppt[:, wcol + j:wcol + j + 1], out32, ALU.mult, ALU.add)
                    return raw

                def gen_lru(g):
                    lbase = (g % 2) * 1664
                    f = lambda k: A32[:, lbase + 260 + k * 256:lbase + 260 + k * 256 + NT]
                    gC, gD = (pC, pD) if g % 2 == 0 else (pG, pH)
                    xc, rr, ii, aa, s_ = [f(k) for k in range(5)]
                    gx, bb, hseq, sg = ii, s_, xc, rr
                    xcb = AB[:, 2048 + (g % 2) * 256:2048 + (g % 2) * 256 + NT]
                    pb_ = pA if g % 2 == 0 else pB
                    proj_fm(pb_, C_XL + g * 128, 128)
                    raw = conv(pb_, lhist, g, PP_LCW + g * 4, PP_LCB + g, xc, lbase)
                    if gi == ngroups - 1:
                        dma("sp", o_lc.rearrange("j (g p) -> p g j", p=128)[:, g, :], raw[:, NV:NV + 3], allow_slow_non_contiguous=True)
                    acopy(xcb, xc)
                    yield
                    mm(gC[:, :NT], wabd[:, g, :], xcb, True)
                    mm(gD[:, :NT], wabd[:, 4 + g, :], xcb, True)
                    sigm(rr, gC[:, :NT], -1.0, drv[:, 48 + g:49 + g])
                    yield
                    sigm(ii, gD[:, :NT], -1.0, drv[:, 52 + g:53 + g])
                    yield
                    afunc(aa, rr, AF.Exp, scale=c1[:, g:g + 1])
                    afunc(s_, aa, AF.Square)
                    afunc(s_, s_, AF.Ln, scale=-1.0, bias=1.0)
                    afunc(s_, s_, AF.Exp, scale=0.5)
                    yield
                    vtt(gx, ii, xc, ALU.mult)
                    vtt(bb, s_, gx, ALU.mult)
                    vscan(hseq, aa, bb, lh[:, g:g + 1])
                    vcopy(lh[:, g:g + 1], hseq[:, NV - 1:NV])
                    yield
                    pg_ = pE if g % 2 == 0 else pF
                    proj_fm(pg_, C_GL + g * 128, 128)
                    sigm(sg, pg_[:, :NT])
                    yield
                    vtt(sg, sg, pg_[:, :NT], ALU.mult)
                    vtt(mixT[:, g, :NT], hseq, sg, ALU.mult)
                    yield

                interleave(gen_lru(0), gen_lru(1))
                interleave(gen_lru(2), gen_lru(3))
                if gi == ngroups - 1:
                    dma("sp", o_lh.rearrange("(g p) -> p g", p=128), lh[:], allow_slow_non_contiguous=True)

                chk(5)
                def gen_ga(g):
                    pb_ = pA if g % 2 == 0 else pB
                    proj_fm(pb_, C_GA + g * 128, 128)
                    gth = A32[:, 512 + (g % 2) * 256:512 + (g % 2) * 256 + NT]
                    yield
                    sigm(gth, pb_[:, :NT])
                    yield
                    vtt(gaT[:, g, :NT], gth, pb_[:, :NT], ALU.mult)
                    yield

                interleave(gen_ga(0), gen_ga(1))
                interleave(gen_ga(2), gen_ga(3))
                for g in range(2):
                    pb_ = pE if g % 2 == 0 else pF
                    proj_fm(pb_, C_QI + g * 128, 128)
                    vcopy(qiT[:, g, :NT], pb_[:, :NT])
                proj_fm(pA, C_KI, 64, 0)
                proj_fm(pA, C_KI, 64, 64)
                acopy(kiT[:, P0 + t0:P0 + t0 + NT], pA[:, :NT])

                def gen_conv(g):
                    pb_ = pE if g % 2 == 0 else pF
                    proj_fm(pb_, C_XBC + g * 128, 128)
                    roff = (g % 2) * 1280
                    acc = A32[:, roff + 512:roff + 512 + NT]
                    raw = conv(pb_, shist, g, PP_SCW + g * 4, PP_SCB + g, acc, roff)
                    if gi == ngroups - 1:
                        dma("sp", o_sc.rearrange("j (g p) -> p g j", p=128)[:, g, :], raw[:, NV:NV + 3], allow_slow_non_contiguous=True)
                    yield
                    cth = A32[:, roff + 768:roff + 768 + NT]
                    sigm(cth, acc)
                    yield
                    vtt(xbcT[:, g, :NT], cth, acc, ALU.mult)
                    yield

                for g2 in range(0, 12, 2):
                    interleave(gen_conv(g2), gen_conv(g2 + 1))

                chk(6)
                for ti in range(ntile):
                    tsl = slice(ti * TT, (ti + 1) * TT)
                    r0 = t0 + ti * TT
                    TV = min(TT, NV - ti * TT)
                    kb = (P0 + r0) // 128
                    if ti == 1:
                        chk(6.9)
                    koff = (P0 + r0) % 128
                    for kc in range(8):
                        mm(pC[:TT, 0:512], hT[:, kc, tsl], win[:, kc, C_Q:C_Q + 512], kc == 0, kc == 7)
                    for (o0, c0, n_) in ((0, C_K, 256), (256, C_KI, 68), (324, C_DT, 16)):
                        for kc in range(8):
                            mm(pD[:TT, o0:o0 + n_], hT[:, kc, tsl], win[:, kc, c0:c0 + n_], kc == 0, kc == 7)
                    for hf in range(2):
                        pz = pE if hf == 0 else pF
                        for kc in range(8):
                            mm(pz[:TT, :], hT[:, kc, tsl], win[:, kc, C_Z + hf * 512:C_Z + (hf + 1) * 512], kc == 0, kc == 7)
                        zth = A32[:TT, 512 + hf * 512:1024 + hf * 512]
                        sigm(zth, pz[:TT, :])
                        vtt(sz[:TT, ti, hf * 512:(hf + 1) * 512], zth, pz[:TT, :], ALU.mult)
                    sqt = A32[:TT, 512:1024]
                    qn = A32[:TT, 1024:1536]
                    kv32 = A32[:TT, 1536:1792]
                    qhat = AB[:TT, 2304:2816]
                    khat = AB[:TT, 2816:2944]
                    chk(6.2 if ti == 0 else 6.92)
                    afunc(sqt, pC[:TT, :], AF.Square)
                    vreduce(sm[:TT, 8:16], sqt.rearrange("p (h d) -> p h d", h=8), ALU.add)
                    rstd_pow(sm[:TT, 24:32], sm[:TT, 8:16], 64, sm[:TT, 16:24])
                    vtt(qn.rearrange("p (h d) -> p h d", h=8), pC[:TT, :].rearrange("p (h d) -> p h d", h=8), bc(sm[:TT, 24:32], 2, 64), ALU.mult)
                    vtt(qhat.rearrange("p (m g d) -> p g m d", m=4, g=2), qn.rearrange("p (g m d) -> p g m d", g=2, m=4),
                        bc(bc(gq8[:TT, :], 1, 4), 1, 2), ALU.mult)
                    chk(6.4 if ti == 0 else 6.94)
                    afunc(sqt[:, 0:128], pD[:TT, 0:128], AF.Square)
                    vreduce(sm[:TT, 32:34], sqt[:, 0:128].rearrange("p (h d) -> p h d", h=2), ALU.add)
                    rstd_pow(sm[:TT, 36:38], sm[:TT, 32:34], 64, sm[:TT, 34:36])
                    vtt(qn[:, 0:128].rearrange("p (h d) -> p h d", h=2), pD[:TT, 0:128].rearrange("p (h d) -> p h d", h=2), bc(sm[:TT, 36:38], 2, 64), ALU.mult)
                    vtt(kv32[:, 0:128].rearrange("p (h d) -> p h d", h=2), qn[:, 0:128].rearrange("p (h d) -> p h d", h=2),
                        bc(pbt[:TT, PB_GK:PB_GK + 64], 1, 2), ALU.mult)
                    acopy(khat, kv32[:, 0:128])
                    acopy(kv32[:, 128:256], pD[:TT, 128:256])
                    acopy(vtok[koff:koff + TT, kb, :], pD[:TT, 128:256])
                    dma("sp", o_ak[r0:r0 + TV, :], kv32[:TV, 0:128])
                    dma("sp", o_av[r0:r0 + TV, :], kv32[:TV, 128:256])
                    vcopy(kiw[:TT, ti, :], pD[:TT, 256:324])
                    dma("sp", o_ik[r0:r0 + TV, :], kiw[:TV, ti, 0:64])
                    chk(6.6 if ti == 0 else 6.96)
                    vtt(sm[:TT, 40:56], pD[:TT, 324:340], pbt[:TT, PB_DTB:PB_DTB + 16], ALU.add)
                    afunc(sm[:TT, 56:72], sm[:TT, 40:56], AF.Exp)
                    afunc(dtt[:TT, ti, :], sm[:TT, 56:72], AF.Ln, bias=1.0)
                    if TV < TT:
                        vmemset(dtt[TV:TT, ti, :], 0.0)
                    vtt(dAt[:TT, ti, :], dtt[:TT, ti, :], aneg[:TT, :], ALU.mult)
                    chk(6.8 if ti == 0 else 6.98)
                    mm(pE[:, 0:TT], khat, identb[:TT, :TT], True)
                    vcopy(khT[:, P0 + r0:P0 + r0 + TT], pE[:, 0:TT])
                    for m in range(4):
                        mm(pF[:, m * TT:(m + 1) * TT], qhat[:, m * 128:(m + 1) * 128], identb[:TT, :TT], m == 0, m == 3)
                    vcopy(qhT[:, :, tsl], pF[:, 0:4 * TT].rearrange("p (c t) -> p c t", c=4))

                chk(7)
                for ti in range(ntile):
                    tsl = slice(ti * TT, (ti + 1) * TT)
                    if ti == 1:
                        chk(7.9)
                    dA = dAt[:TT, ti, :]
                    dtv = dtt[:TT, ti, :]
                    split3(dAs[:TT], dA, sm[:TT, 176:192], sm[:TT, 192:208])
                    for k in range(3):
                        mm(pF[:TT, 0:16], trib[:TT, :TT], dAs[:TT, k, :], k == 0, k == 2)
                    for k in range(3):
                        mm(pF[:TT, 16:32], astrb[:TT, :TT], dAs[:TT, k, :], k == 0, k == 2)
                    for k in range(3):
                        mm(pF[:, 32:48], onesb[:TT, :], dAs[:TT, k, :], k == 0, k == 2)
                    ecum = sm[:TT, 80:96]
                    toend = sm[:TT, 96:112]
                    dec = sm[:, 112:128]
                    afunc(sm[:TT, 80:112], pF[:TT, 0:32], AF.Exp)
                    afunc(dec, pF[:, 32:48], AF.Exp)
                    chk(7.1)
                    for c in range(8):
                        pb_ = pC if c < 4 else pD
                        mm(pb_[:TT, (c % 4) * 128:(c % 4 + 1) * 128], xbcT[:, c, tsl], identb, c % 4 == 0, c % 4 == 3)
                    xt_ = AB[:TT, 0:1024]
                    x2_ = AB[:TT, 1024:2048]
                    xD = A32[:TT, 0:1024]
                    for hf in range(2):
                        pTv = (pC if hf == 0 else pD)[:TT, :].rearrange("p (h d) -> p h d", h=8)
                        hs_ = slice(hf * 512, (hf + 1) * 512)
                        vtt(xt_[:, hs_].rearrange("p (h d) -> p h d", h=8), pTv, bc(dtv[:, hf * 8:(hf + 1) * 8], 2, 64), ALU.mult)
                        vtt(xD[:, hs_].rearrange("p (h d) -> p h d", h=8), pTv, bc(pbt[:TT, PB_D + hf * 8:PB_D + hf * 8 + 8], 2, 64), ALU.mult)
                    vtt(x2_.rearrange("p (h d) -> p h d", h=16), xt_.rearrange("p (h d) -> p h d", h=16), bc(toend, 2, 64), ALU.mult)
                    def gen_ssd(g, ti=ti, tsl=tsl, xt_=xt_, x2_=x2_, xD=xD, ecum=ecum, dec=dec):
                        eb = 2048 + g * 1280
                        E = AB[:TT, eb:eb + 8 * TT].rearrange("p (h l) -> p h l", h=8)
                        WT = E
                        G = AB[:TT, eb + 1024:eb + 1024 + TT]
                        bmtok = AB[:TT, eb + 1152:eb + 1280]
                        ycb = AB[:TT, eb:eb + 512]
                        t1 = A32[:TT, 1024 + g * 1024:1536 + g * 1024]
                        yz = A32[:TT, 1536 + g * 1024:2048 + g * 1024]
                        dbk = (pA, pB) if g == 0 else (pG, pH)
                        cbk = pC if g == 0 else pF
                        ydk = pD if g == 0 else pG
                        yok = pE if g == 0 else pH
                        ytk = dbk[0]
                        hpb = 512 // TT
                        for bk in range(8 // hpb):
                            pb_ = dbk[bk % 2]
                            for k in range(2):
                                Rk = RB[:TT, (g * 2 + bk + k) % 2, 0:hpb * TT]
                                h0_ = g * 8 + bk * hpb
                                vtt(Rk.rearrange("p (h l) -> p h l", h=hpb), bc(dAs[:TT, k, h0_:h0_ + hpb], 2, TT), bc(trib[:TT, :TT], 1, hpb), ALU.mult)
                                mm(pb_[:TT, 0:hpb * TT], astrb[:TT, :TT], Rk, k == 0, k == 1)
                            afunc(E[:, bk * hpb:(bk + 1) * hpb, :], pb_[:TT, 0:hpb * TT].rearrange("p (h l) -> p h l", h=hpb), AF.Exp)
                            yield
                        mm(cbk[:TT, :TT], xbcT[:, 8 + g, tsl], xbcT[:, 10 + g, tsl], True)
                        vtt(G, cbk[:TT, :TT], tri[:TT, :TT], ALU.mult)
                        vtt(WT, E, bc(G, 1, 8), ALU.mult)
                        yield
                        for hh in range(8):
                            h_ = g * 8 + hh
                            mm(ydk[:TT, hh * 64:(hh + 1) * 64], WT[:, hh, :], xt_[:, h_ * 64:(h_ + 1) * 64], hh == 0, hh == 7)
                        mm(yok[:TT, :], xbcT[:, 10 + g, tsl], hsb[:, g * 512:(g + 1) * 512], True)
                        yield
                        vtt(t1.rearrange("p (h d) -> p h d", h=8), yok[:TT, :].rearrange("p (h d) -> p h d", h=8), bc(ecum[:, g * 8:(g + 1) * 8], 2, 64), ALU.mult)
                        vtt(t1, t1, ydk[:TT, :], ALU.add)
                        vtt(t1, t1, xD[:, g * 512:(g + 1) * 512], ALU.add)
                        vtt(yz, t1, sz[:TT, ti, g * 512:(g + 1) * 512], ALU.mult)
                        yield
                        afunc(t1, yz, AF.Square, accum=sm[:TT, 128 + g:129 + g])
                        rstd_pow(sm[:TT, 132 + g:133 + g], sm[:TT, 128 + g:129 + g], 512, sm[:TT, 130 + g:131 + g])
                        vts(t1, yz, sm[:TT, 132 + g:133 + g], ALU.mult)
                        vtt(ycb, t1, pbt[:TT, PB_GSSD + g * 512:PB_GSSD + (g + 1) * 512], ALU.mult)
                        yield
                        for c in range(4):
                            mm(ytk[:, c * TT:(c + 1) * TT], ycb[:, c * 128:(c + 1) * 128], identb[:TT, :TT], c == 0, c == 3)
                        vcopy(mixT[:, 8 + g * 4:12 + g * 4, tsl], ytk[:, 0:4 * TT].rearrange("p (c t) -> p c t", c=4))
                        yield
                        mm(cbk[:TT, 0:128], xbcT[:, 8 + g, tsl], identb, True)
                        acopy(bmtok, cbk[:TT, 0:128])
                        mm(cbk[:, :], bmtok, x2_[:, g * 512:(g + 1) * 512], True)
                        hv = hst[:, g * 512:(g + 1) * 512]
                        vtt(hv.rearrange("p (h d) -> p h d", h=8), hv.rearrange("p (h d) -> p h d", h=8), bc(dec[:, g * 8:(g + 1) * 8], 2, 64), ALU.mult)
                        vtt(hv, hv, cbk[:, :], ALU.add)
                        acopy(hsb[:, g * 512:(g + 1) * 512], hv)
                        yield

                    interleave(gen_ssd(0), gen_ssd(1))
                if gi == ngroups - 1:
                    stt = A32[:, 0:1024].rearrange("p (c n) -> p c n", c=8)
                    sp3 = AB[:, 0:3072].rearrange("p (k n) -> p k n", k=3)
                    split3(sp3, hst[:, :], A32[:, 1024:2048], A32[:, 2048:3072])
                    for c in range(8):
                        pbank = pA if c < 4 else pB
                        for k in range(3):
                            mm(pbank[:, (c % 4) * 128:(c % 4 + 1) * 128], sp3[:, k, c * 128:(c + 1) * 128], identb, c % 4 == 0 and k == 0, c % 4 == 3 and k == 2)
                    vcopy(stt[:, 0:4, :], pA[:, :].rearrange("p (c n) -> p c n", c=4))
                    vcopy(stt[:, 4:8, :], pB[:, :].rearrange("p (c n) -> p c n", c=4))
                    dma("sp", o_sh.rearrange("(c p) n -> p c n", p=128), stt)

                chk(8)
                def p4_vars(ti):
                    q0 = P0 + t0 + ti * TT
                    Lk = q0 + TT
                    return (slice(ti * TT, (ti + 1) * TT), q0, Lk, (Lk + 127) // 128, A32[:TT, 0:Lk], AB[:TT, 0:Lk], JK[:TT, 0:Lk])

                def gen_topk(ti):
                    tsl, q0, Lk, nkb, score, negm, junkb = p4_vars(ti)
                    for c0 in range(0, Lk, 512):
                        c1_ = min(Lk, c0 + 512)
                        for hh in range(4):
                            pb_ = pA if hh % 2 == 0 else pB
                            pr = (hh % 2) * 64
                            mm(pb_[:TT, 0:c1_ - c0], qiT[pr:pr + 64, hh // 2, tsl], kiT[pr:pr + 64, c0:c1_], True)
                            rl = A32[:TT, 2048 + (hh % 2) * 512:2560 + (hh % 2) * 512]
                            afunc(rl[:, 0:c1_ - c0], pb_[:TT, 0:c1_ - c0], AF.Relu)
                            if hh == 0:
                                vts(score[:, c0:c1_], rl[:, 0:c1_ - c0], kiw[:TT, ti, 64:65], ALU.mult)
                            else:
                                vstt(score[:, c0:c1_], rl[:, 0:c1_ - c0], kiw[:TT, ti, 64 + hh:65 + hh], score[:, c0:c1_], ALU.mult, ALU.add)
                            yield
                    lo = sm[:TT, 140:141]
                    if Lk > ktop:
                        amax = sm[:TT, 141:142]
                        hw = sm[:TT, 144:144 + NBIS + 1]
                        vreduce(amax, score, ALU.max, absval=True)
                        vtt(score[:, Lk - 128:Lk], score[:, Lk - 128:Lk], dmask[:, :], ALU.add)
                        vts(amax, amax, 1.001, ALU.mult, 1e-3, ALU.add)
                        vts(hw, p2row[:TT, :], amax, ALU.mult)
                        mid = sm[:TT, 142:143]
                        cnt = sm[:TT, 143:144]
                        ind = sm[:TT, 170:171]
                        vmemset(mid, 0.0)
                        yield
                        for it in range(NBIS):
                            vts(junkb, score, mid, ALU.is_gt, 0.0, ALU.add, accum=cnt)
                            vstt(ind, cnt, float(ktop) - 0.5, hw[:, it:it + 1], ALU.is_gt, ALU.mult)
                            vstt(mid, ind, hw[:, it + 1:it + 2], mid, ALU.subtract, ALU.add)
                            yield
                        vtt(lo, mid, hw[:, NBIS:NBIS + 1], ALU.subtract)
                    else:
                        vtt(score[:, Lk - 128:Lk], score[:, Lk - 128:Lk], dmask[:, :], ALU.add)
                        vmemset(lo, NEG / 2)

                def do_mask(ti):
                    tsl, q0, Lk, nkb, score, negm, junkb = p4_vars(ti)
                    lo = sm[:TT, 140:141]
                    if Lk > ktop:
                        c0 = sm[:TT, 171:172]
                        mrem = sm[:TT, 172:173]
                        vts(junkb, score, 0.0, ALU.is_gt, 0.0, ALU.add, accum=c0)
                        vts(mrem, c0, -1.0, ALU.mult, float(ktop), ALU.add)
                        vts(negm, score, 0.0, ALU.is_equal)
                        vscan(negm, onesb[:TT, 0:1].broadcast_to([TT, Lk]), negm, 0.0)
                        vts(negm, negm, mrem, ALU.is_gt)
                        vstt(negm, score, 0.0, negm, ALU.is_equal, ALU.mult)
                        vstt(negm, score, lo, negm, ALU.is_le, ALU.max)
                        vts(negm, negm, NEG, ALU.mult)
                    else:
                        vts(negm, score, lo, ALU.is_le, NEG, ALU.mult)

                def gen_attn(ti):
                    tsl, q0, Lk, nkb, score, negm, junkb = p4_vars(ti)
                    for g in range(2):
                        for jb in range(nkb):
                            S = min(128, Lk - jb * 128)
                            pl = pC if jb % 2 == 0 else pD
                            plv = pl[:S, 0:4 * TT].rearrange("p (m t) -> p m t", m=4)
                            mm(plv, negm[:, jb * 128:jb * 128 + S], bc(identb[:TT, :TT], 1, 4), True, False)
                            dl = jb * 128 - q0
                            mm(plv, khT[g * 64:(g + 1) * 64, jb * 128:jb * 128 + S], qhT[g * 64:(g + 1) * 64, :, tsl], False, dl <= -256)
                            if dl > -256:
                                si = 0 if dl == 0 else 1
                                for m in range(4):
                                    mm(plv[:, m, :], hk[:TT, si, g * 4 + m, :S], j128b[:TT, :TT], False, m == 3)
                            ET = AB[:S, 2048 + (jb % 2) * 512:2048 + (jb % 2) * 512 + 4 * TT].rearrange("p (m t) -> p m t", m=4)
                            afunc(ET, plv, AF.Exp)
                            pov = pE[:, 0:2 * TT].rearrange("p (a t) -> p a t", a=2)
                            pdv = pF[:, 0:2 * TT].rearrange("p (a t) -> p a t", a=2)
                            for par in range(2):
                                mm(pov[par * 64:(par + 1) * 64], vtok[:S, jb, g * 64:(g + 1) * 64], ET[:, par::2, :], jb == 0, jb == nkb - 1)
                                mm(pdv[par * 64:(par + 1) * 64], onesb[:S, 0:64], ET[:, par::2, :], jb == 0, jb == nkb - 1)
                            yield
                        rden = A32[:, 3072:3072 + 2 * TT]
                        vrecip(rden, pF[:, 0:2 * TT])
                        vtt(rden, pE[:, 0:2 * TT], rden, ALU.mult)
                        vtt(mixT[:, 4 + 2 * g:6 + 2 * g, tsl], rden.rearrange("p (a t) -> p a t", a=2), gaT[:, 2 * g:2 * g + 2, tsl], ALU.mult)
                        yield

                drain(gen_topk(0))
                do_mask(0)
                if ntile == 2:
                    interleave(gen_attn(0), gen_topk(1))
                    do_mask(1)
                    drain(gen_attn(1))
                else:
                    drain(gen_attn(0))

                chk(9)
                if DEBUG_MIX[0] and not isS and qi_ == 0 and l == 0:
                    dma("sp", O["dbg"][:, :, t0:t0 + NT].rearrange("c p t -> p c t"), mixT[:, :, :NT])
                for ti in range(ntile):
                    tsl = slice(ti * TT, (ti + 1) * TT)
                    r0 = t0 + ti * TT
                    yo = A32[:TT, 0:1024]
                    TV = min(TT, NV - ti * TT)
                    xr = A32[:TT, 1024:2048]
                    if TV < TT:
                        vmemset(A32[TV:TT, 1024:2048], 0.0)
                    dma("sp", A32[:TV, 1024:2048], xin[r0:r0 + TV, :])
                    for hf in range(2):
                        pb_ = pA if hf == 0 else pB
                        for ec in range(16):
                            mm(pb_[:TT, :], mixT[:, ec, tsl], wout[:, ec, hf * 512:(hf + 1) * 512], ec == 0, ec == 15)
                        vtt(yo[:, hf * 512:(hf + 1) * 512], pb_[:TT, :], xr[:, hf * 512:(hf + 1) * 512], ALU.add)
                    dma("sp", xout[r0:r0 + TV, :], yo[:TV])


def _t5_bucket_np(rel):
    import math
    import jax
    import jax.numpy as jnp
    with jax.default_device(jax.devices("cpu")[0]):
        return _t5_bucket_cpu(rel, math, jnp)


def _t5_bucket_cpu(rel, math, jnp):
    rel = jnp.asarray(rel, dtype=jnp.int32)
    half, max_exact = 16, 8
    n = jnp.abs(rel)
    large = max_exact + (jnp.log(jnp.maximum(n, 1).astype(jnp.float32) / max_exact) / math.log(128 / max_exact) * (half - max_exact)).astype(jnp.int32)
    large = jnp.minimum(large, half - 1)
    return np.asarray(jnp.where(rel > 0, half, 0) + jnp.where(n < max_exact, n, large))


def _consts():
    c = np.zeros((128, NCST), np.float32)
    i = np.arange(128)
    c[:, 0:128] = np.eye(128)
    c[:, 128:256] = (i[:, None] <= i[None, :])
    c[:, 256:384] = (i[:, None] > i[None, :])
    c[:, 384:512] = (i[:, None] == 127 - i[None, :])
    c[:64, 512:576] = (i[:64, None] == 63 - i[None, :64])
    c[:, 640:768] = np.where((i[None, :] // 64) > (i[:, None] // 64), NEG, 0.0)
    bk = _t5_bucket_np(np.arange(384) - 255)
    c[0:32, 768:1152] = (np.arange(32)[:, None] == bk[None, :])
    return c


_PROG_CACHE = {}


def _run(inputs, NCORES, NSP, TP, SAMPLE, PS, TS, DEPTH):
    key = (NSP, TP, SAMPLE, PS, TS, DEPTH)
    f32 = np.float32
    g = lambda k: np.ascontiguousarray(np.asarray(inputs[k], dtype=f32))
    w_in, w_out = g("w_in"), g("w_out")
    cst = _consts()
    pp = np.zeros((DEPTH, 128, NPP), f32)
    pb = np.zeros((DEPTH, 128, NPB), f32)
    wabd = np.zeros((DEPTH, 128, 2, 4, 128), f32)
    fm = lambda v, n: v.reshape(n, 128).T
    for l in range(DEPTH):
        pp[l, :, PP_GN:PP_GN + 8] = fm(g("norm_w")[l], 8)
        lcw = g("lru_conv_w")[l]
        pp[l, :, PP_LCW:PP_LCW + 16] = lcw.reshape(4, 4, 128).transpose(2, 1, 0).reshape(128, 16)
        pp[l, :, PP_LCB:PP_LCB + 4] = fm(g("lru_conv_b")[l], 4)
        pp[l, :, PP_LBA:PP_LBA + 4] = fm(g("lru_b_a")[l], 4)
        pp[l, :, PP_LBX:PP_LBX + 4] = fm(g("lru_b_x")[l], 4)
        pp[l, :, PP_LAM:PP_LAM + 4] = fm(g("lru_lambda")[l], 4)
        scw = g("ssd_conv_w")[l]
        pp[l, :, PP_SCW:PP_SCW + 48] = scw.reshape(4, 12, 128).transpose(2, 1, 0).reshape(128, 48)
        pp[l, :, PP_SCB:PP_SCB + 12] = fm(g("ssd_conv_b")[l], 12)
        pb[l, :, PB_GQ:PB_GQ + 64] = g("att_q_norm")[l][None, :]
        pb[l, :, PB_GK:PB_GK + 64] = g("att_k_norm")[l][None, :]
        pb[l, :, PB_DTB:PB_DTB + 16] = g("ssd_dt_bias")[l][None, :]
        pb[l, :, PB_ALOG:PB_ALOG + 16] = g("ssd_a_log")[l][None, :]
        pb[l, :, PB_D:PB_D + 16] = g("ssd_d")[l][None, :]
        pb[l, :, PB_GSSD:PB_GSSD + 1024] = g("ssd_norm")[l][None, :]
        pb[l, :, PB_RB15:PB_RB15 + 8] = g("rel_bias")[15][None, :]
        for a, nm in enumerate(("lru_w_a", "lru_w_x")):
            w = g(nm)[l]
            for gg in range(4):
                wabd[l, 0:64, a, gg, 0:64] = w[2 * gg]
                wabd[l, 64:128, a, gg, 64:128] = w[2 * gg + 1]
    wabd = wabd.reshape(DEPTH, 128, 1024)
    xp = g("x_prompt")
    in_maps = []
    for c in range(NCORES):
        m = {"xp": xp[c * NSP:(c + 1) * NSP], "w_in": w_in, "w_out": w_out, "pp": pp, "pb": pb, "wabd": wabd,
             "rb": g("rel_bias"), "cst": cst}
        if SAMPLE:
            m["xs"] = g("x_sample")[c]
            m["ck"] = g("cache_att_k")[:, c].reshape(DEPTH, PS, 128)
            m["cv"] = g("cache_att_v")[:, c].reshape(DEPTH, PS, 128)
            m["cki"] = g("cache_idx_k")[:, c]
            m["slc"] = g("state_lru_conv")[:, c]
            m["slh"] = g("state_lru_h")[:, c]
            m["ssc"] = g("state_ssd_conv")[:, c]
            m["ssh"] = g("state_ssd_h")[:, c].reshape(DEPTH, 1024, 128)
        in_maps.append({k: np.ascontiguousarray(v) for k, v in m.items()})
    if key not in _PROG_CACHE:
        _PROG_CACHE[key] = build_program(NSP=NSP, TP=TP, SAMPLE=SAMPLE, PS=PS, TS=TS, DEPTH=DEPTH)
    nc = _PROG_CACHE[key]
    res = run_bass_kernel_spmd(nc, in_maps, core_ids=list(range(NCORES)))
    R = res.results
    cat = lambda k, ax: np.concatenate([np.asarray(r[k]) for r in R], axis=ax)
    stk = lambda k, ax: np.stack([np.asarray(r[k]) for r in R], axis=ax)
    B = NCORES * NSP
    outs = [cat("yp", 0)]
    if SAMPLE:
        outs.append(stk("ys", 0))
    outs += [cat("akp", 1).reshape(DEPTH, B, TP, 2, 64), cat("avp", 1).reshape(DEPTH, B, TP, 2, 64), cat("ikp", 1),
             cat("lcp", 1), cat("lhp", 1), cat("scp", 1), cat("shp", 1).reshape(DEPTH, B, 16, 64, 128)]
    if SAMPLE:
        outs += [stk("aks", 1).reshape(DEPTH, NCORES, TS, 2, 64), stk("avs", 1).reshape(DEPTH, NCORES, TS, 2, 64), stk("iks", 1),
                 stk("lcs", 1), stk("lhs", 1), stk("scs", 1), stk("shs", 1).reshape(DEPTH, NCORES, 16, 64, 128)]
    return tuple(np.ascontiguousarray(o, dtype=np.float32) for o in outs)


def kernel(**inputs):
    return _run(inputs, 8, 2, 2048, True, 1024, 64, 2)
```

```python
import numpy as np
from contextlib import ExitStack
import concourse.bass as bass
import concourse.mybir as mybir
from concourse.bass_utils import run_bass_kernel_spmd

F32 = mybir.dt.float32
BF16 = mybir.dt.bfloat16
AF = mybir.ActivationFunctionType
ALU = mybir.AluOpType
AX = mybir.AxisListType


def _region(ap):
    t = ap.tensor
    pat = [(int(s), int(c)) for (s, c) in ap.ap]
    off = int(ap.offset)
    kind = type(t).__name__
    if kind.startswith("DRam"):
        lo = off
        ext = sum((c - 1) * abs(s) for s, c in pat)
        n = 1
        for s, c in pat:
            if s != 0:
                n *= c
        return (t.name, 0, 1, lo, lo + ext + 1, n == ext + 1)
    shp = [int(v) for v in t.shape]
    pstride = 1
    for v in shp[1:]:
        pstride *= v
    p0 = off // pstride
    lo = off % pstride
    npart = pat[0][1]
    ext = sum((c - 1) * abs(s) for s, c in pat[1:])
    n = 1
    for s, c in pat[1:]:
        if s != 0:
            n *= c
    if kind.startswith("PSum"):
        full = (lo == 0 and lo + ext + 1 == pstride and n == ext + 1)
        return (t.name, (p0 // 32) * 32, ((p0 + npart + 31) // 32) * 32, 0, pstride, full and p0 % 32 == 0 and (p0 + npart) % 32 == 0)
    return (t.name, p0, p0 + npart, lo, lo + ext + 1, n == ext + 1)


def _ovl(a, b):
    return a[1] < b[2] and b[1] < a[2] and a[3] < b[4] and b[3] < a[4]


def _covers(a, b):
    return a[5] and a[1] <= b[1] and a[2] >= b[2] and a[3] <= b[3] and a[4] >= b[4]


class _Stop(Exception):
    pass


STOP_AT = [99]
DEBUG_MIX = [False]


CK_OFF = [0]


def drain(gen):
    for _ in gen:
        pass


def interleave(ga_, gb_):
    a_live = b_live = True
    while a_live or b_live:
        if a_live:
            a_live = next(ga_, "end") != "end"
        if b_live:
            b_live = next(gb_, "end") != "end"


def chk(k):
    if k + CK_OFF[0] > STOP_AT[0]:
        raise _Stop()


class Prog:
    def __init__(self, nc):
        self.nc = nc
        self.stack = ExitStack()
        self.ops = []
        self.wr = {}
        self.rd = {}
        self.finished = False

    def sb(self, name, shape, dt):
        return self.stack.enter_context(self.nc.sbuf_tensor("sb_" + name, list(shape), dt))

    def ps(self, name, shape, dt):
        return self.stack.enter_context(self.nc.psum_tensor("ps_" + name, list(shape), dt))

    def _add(self, eng, fn, outs, ins, is_dma=False):
        idx = len(self.ops)
        deps = {}
        rregs = [_region(a) for a in ins]
        wregs = [_region(a) for a in outs]
        for r in rregs:
            for w in self.wr.get(r[0], ()):
                if _ovl(r, w[1]):
                    deps.setdefault(w[0], set()).add("raw")
        for r in wregs:
            for w in self.wr.get(r[0], ()):
                if _ovl(r, w[1]):
                    deps.setdefault(w[0], set()).add("waw")
            for w in self.rd.get(r[0], ()):
                if _ovl(r, w[1]):
                    deps.setdefault(w[0], set()).add("war")
        for r in wregs:
            lw = self.wr.setdefault(r[0], [])
            lw[:] = [w for w in lw if not _covers(r, w[1])]
            lw.append((idx, r))
            lr = self.rd.get(r[0])
            if lr:
                lr[:] = [w for w in lr if not _covers(r, w[1])]
        for r in rregs:
            self.rd.setdefault(r[0], []).append((idx, r))
        self.ops.append(dict(eng=eng, fn=fn, deps=deps, dma=is_dma, sig=False))
        return idx

    def pe(self, fn, outs, ins):
        return self._add("pe", fn, outs, ins)

    def dve(self, fn, outs, ins):
        return self._add("dve", fn, outs, ins)

    def act(self, fn, outs, ins):
        return self._add("act", fn, outs, ins)

    def pool(self, fn, outs, ins):
        return self._add("pool", fn, outs, ins)

    def dma(self, q, out, in_, **kw):
        return self._add(q, lambda e: e.dma_start(out=out, in_=in_, **kw), [out], [in_], is_dma=True)

    def finish(self):
        nc = self.nc
        ops = self.ops
        engs = ["pe", "dve", "act", "pool", "sp"]
        for i, op in enumerate(ops):
            nd = set()
            for j, kinds in op["deps"].items():
                o = ops[j]
                if o["dma"]:
                    nd.add(j)
                    continue
                if o["eng"] == op["eng"] and not op["dma"]:
                    if op["eng"] == "pe":
                        continue
                nd.add(j)
            op["deps"] = nd
            for j in nd:
                ops[j]["sig"] = True
        esem = {e: self.stack.enter_context(nc.semaphore("s_" + e)) for e in engs}
        nds = {"sp": 40, "pool": 16, "act": 8}
        dsem = {q: [self.stack.enter_context(nc.semaphore("d_%s%d" % (q, k))) for k in range(n)] for q, n in nds.items()}
        dcnt = {q: [0] * n for q, n in nds.items()}
        dnext = {q: 0 for q in nds}
        ecnt = {e: 0 for e in engs}
        for i, op in enumerate(ops):
            if op["dma"]:
                q = op["eng"]
                k = dnext[q] % nds[q]
                dnext[q] += 1
                prev = dcnt[q][k]
                dcnt[q][k] += 16
                op["event"] = (dsem[q][k], dcnt[q][k])
                op["prevev"] = (dsem[q][k], prev) if prev > 0 else None
            elif op["sig"]:
                ecnt[op["eng"]] += 1
                op["event"] = (esem[op["eng"]], ecnt[op["eng"]])
        waited = {e: {} for e in engs}
        per_eng = {e: [] for e in engs}
        for i, op in enumerate(ops):
            e = op["eng"]
            need = {}
            for j in op["deps"]:
                s, v = ops[j]["event"]
                key = id(s)
                if need.get(key, (None, 0))[1] < v:
                    need[key] = (s, v)
            if op["dma"] and op["prevev"] is not None:
                s, v = op["prevev"]
                key = id(s)
                if need.get(key, (None, 0))[1] < v:
                    need[key] = (s, v)
            waits = []
            for key, (s, v) in need.items():
                if waited[e].get(key, 0) >= v:
                    continue
                waited[e][key] = v
                waits.append((s, v))
            op["waits"] = waits
            per_eng[e].append(op)
        self.n_waits = sum(len(o["waits"]) for o in ops)
        final = []
        for q in nds:
            for k in range(nds[q]):
                if dcnt[q][k] > 0:
                    final.append((dsem[q][k], dcnt[q][k]))

        def emit(ename, e):
            for op in per_eng[ename]:
                for (s, v) in op["waits"]:
                    e.wait_ge(s, v)
                ins = op["fn"](e)
                if op["dma"]:
                    ins.then_inc(op["event"][0], 16)
                elif op["sig"]:
                    ins.then_inc(op["event"][0], 1)
            if ename == "sp":
                for (s, v) in final:
                    e.wait_ge(s, v)

        with nc.Block() as block:
            @block.tensor
            def _(e):
                emit("pe", e)

            @block.vector
            def _(e):
                emit("dve", e)

            @block.scalar
            def _(e):
                emit("act", e)

            @block.gpsimd
            def _(e):
                emit("pool", e)

            @block.sync
            def _(e):
                emit("sp", e)
        self.stack.close()
        self.finished = True


D_MODEL = 1024
D_IN = 5204
D_MIX = 2048
EPS = 1e-6
NEG = -30000.0
C_XL, C_GL, C_Q, C_K, C_GA, C_QI, C_KI, C_Z, C_XBC, C_DT = 0, 512, 1024, 1536, 1792, 2304, 2560, 2628, 3652, 5188
NPP = 100
NPB = 1208
NCST = 1152
PB_GQ, PB_GK, PB_DTB, PB_ALOG, PB_D, PB_GSSD, PB_RB15 = 0, 64, 128, 144, 160, 176, 1200
PP_GN, PP_LCW, PP_LCB, PP_LBA, PP_LBX, PP_LAM, PP_SCW, PP_SCB = 0, 8, 24, 28, 32, 36, 40, 88


def build_program(NSP=2, TP=2048, SAMPLE=True, PS=1024, TS=64, DEPTH=2, TG=256, NBIS=16):
    nc = bass.Bass("TRN2", target_bir_lowering=False)
    pg = Prog(nc)
    dt_in = lambda name, shape: nc.dram_tensor(name, list(shape), F32, kind="ExternalInput").ap()
    dt_out = lambda name, shape: nc.dram_tensor(name, list(shape), F32, kind="ExternalOutput").ap()
    dt_tmp = lambda name, shape: nc.dram_tensor(name, list(shape), F32, kind="Internal").ap()
    I = {}
    I["xp"] = dt_in("xp", [NSP, TP, D_MODEL])
    I["w_in"] = dt_in("w_in", [DEPTH, D_MODEL, D_IN])
    I["w_out"] = dt_in("w_out", [DEPTH, D_MIX, D_MODEL])
    I["pp"] = dt_in("pp", [DEPTH, 128, NPP])
    I["pb"] = dt_in("pb", [DEPTH, 128, NPB])
    I["wabd"] = dt_in("wabd", [DEPTH, 128, 2 * 4 * 128])
    I["rb"] = dt_in("rb", [32, 8])
    I["cst"] = dt_in("cst", [128, NCST])
    O = {}
    O["yp"] = dt_out("yp", [NSP, TP, D_MODEL])
    O["akp"] = dt_out("akp", [DEPTH, NSP, TP, 128])
    O["avp"] = dt_out("avp", [DEPTH, NSP, TP, 128])
    O["ikp"] = dt_out("ikp", [DEPTH, NSP, TP, 64])
    O["lcp"] = dt_out("lcp", [DEPTH, NSP, 3, 512])
    O["lhp"] = dt_out("lhp", [DEPTH, NSP, 512])
    O["scp"] = dt_out("scp", [DEPTH, NSP, 3, 1536])
    O["shp"] = dt_out("shp", [DEPTH, NSP, 1024, 128])
    xmid_p = dt_tmp("xmid_p", [NSP, TP, D_MODEL])
    if DEBUG_MIX[0]:
        O["dbg"] = nc.dram_tensor("dbg", [16, 128, TP], BF16, kind="ExternalOutput").ap()
    vecd = dt_tmp("vecd", [8, 384])
    if SAMPLE:
        I["xs"] = dt_in("xs", [TS, D_MODEL])
        I["ck"] = dt_in("ck", [DEPTH, PS, 128])
        I["cv"] = dt_in("cv", [DEPTH, PS, 128])
        I["cki"] = dt_in("cki", [DEPTH, PS, 64])
        I["slc"] = dt_in("slc", [DEPTH, 3, 512])
        I["slh"] = dt_in("slh", [DEPTH, 512])
        I["ssc"] = dt_in("ssc", [DEPTH, 3, 1536])
        I["ssh"] = dt_in("ssh", [DEPTH, 1024, 128])
        O["ys"] = dt_out("ys", [TS, D_MODEL])
        O["aks"] = dt_out("aks", [DEPTH, TS, 128])
        O["avs"] = dt_out("avs", [DEPTH, TS, 128])
        O["iks"] = dt_out("iks", [DEPTH, TS, 64])
        O["lcs"] = dt_out("lcs", [DEPTH, 3, 512])
        O["lhs"] = dt_out("lhs", [DEPTH, 512])
        O["scs"] = dt_out("scs", [DEPTH, 3, 1536])
        O["shs"] = dt_out("shs", [DEPTH, 1024, 128])
        xmid_s = dt_tmp("xmid_s", [TS, D_MODEL])

    LMAX = max(TP, (PS + TS) if SAMPLE else 0)
    LMAX = ((LMAX + 127) // 128) * 128
    NKB = LMAX // 128
    sb, ps = pg.sb, pg.ps
    win = sb("win", [128, 8, D_IN], BF16)
    wout = sb("wout", [128, 16, D_MODEL], BF16)
    ppt = sb("ppt", [128, NPP], F32)
    pbt = sb("pbt", [128, NPB], F32)
    wabd = sb("wabdb", [128, 8, 128], BF16)
    cst = sb("cst", [128, 768], F32)
    ident = cst[:, 0:128]
    tri = cst[:, 128:256]
    astr = cst[:, 256:384]
    dmask = cst[:, 640:768]
    cbf = sb("cbf", [128, 6, 128], BF16)
    trib, astrb = cbf[:, 4, :], cbf[:, 5, :]
    RB = sb("RB", [128, 2, 512], BF16)
    JK = sb("JK", [128, LMAX], mybir.dt.uint8)
    dAs = sb("dAs", [128, 3, 16], BF16)
    identb, j128b, j64b, onesb = cbf[:, 0, :], cbf[:, 1, :], cbf[:, 2, :], cbf[:, 3, :]
    hk = sb("hk", [128, 2, 8, 128], BF16)
    p2row = sb("p2row", [128, NBIS + 1], F32)
    rbt = sb("rbt", [32, 8], F32)
    rb15row = sb("rb15row", [1, 8, 128], BF16)
    vecs = sb("vecs", [8, 384], F32)
    drv = sb("drv", [128, 64], F32)
    c1 = drv[:, 0:4]
    aneg = drv[:, 4:20]
    gq8 = sb("gq8", [128, 64], F32)
    hT = sb("hT", [128, 8, TG], BF16)
    mixT = sb("mixT", [128, 16, TG], BF16)
    khT = sb("khT", [128, LMAX], BF16)
    vtok = sb("vtok", [128, NKB, 128], BF16)
    kiT = sb("kiT", [128, LMAX], BF16)
    qhT = sb("qhT", [128, 4, TG], BF16)
    qiT = sb("qiT", [128, 2, TG], BF16)
    gaT = sb("gaT", [128, 4, TG], BF16)
    lhist = sb("lhist", [128, 4, 3], F32)
    shist = sb("shist", [128, 12, 3], F32)
    lh = sb("lh", [128, 4], F32)
    xbcT = sb("xbcT", [128, 12, TG], BF16)
    sz = sb("sz", [128, 2, 1024], BF16)
    hst = sb("hst", [128, 1024], F32)
    hsb = sb("hsb", [128, 1024], BF16)
    sm = sb("sm", [128, 256], F32)
    smb = sb("smb", [128, 2, 80], F32)
    kiw = sb("kiw", [128, 2, 68], F32)
    dtt = sb("dtt", [128, 2, 16], F32)
    dAt = sb("dAt", [128, 2, 16], F32)
    A32 = sb("A32", [128, 3328], F32)
    AB = sb("AB", [128, 4608], BF16)
    pA = ps("pA", [128, 512], F32)
    pB = ps("pB", [128, 512], F32)
    pC = ps("pC", [128, 512], F32)
    pD = ps("pD", [128, 512], F32)
    pE = ps("pE", [128, 512], F32)
    pF = ps("pF", [128, 512], F32)
    pG = ps("pG", [128, 512], F32)
    pH = ps("pH", [128, 512], F32)

    act, dve, pe, pool, dma = pg.act, pg.dve, pg.pe, pg.pool, pg.dma

    def A_(fn, out, ins):
        return act(fn, [out] if not isinstance(out, list) else out, ins)

    def acopy(out, in_):
        act(lambda e: e.activation(out=out, in_=in_, func=AF.Copy), [out], [in_])

    def afunc(out, in_, func, bias=None, scale=None, accum=None):
        kw = {}
        reads = [in_]
        outs = [out]
        if bias is not None:
            kw["bias"] = bias
            if not isinstance(bias, float):
                reads.append(bias)
        if scale is not None:
            kw["scale"] = scale
            if not isinstance(scale, float):
                reads.append(scale)
        if accum is not None:
            kw["accum_out"] = accum
            outs.append(accum)
        act(lambda e: e.activation(out=out, in_=in_, func=func, **kw), outs, reads)

    def vtt(out, a, b, op):
        dve(lambda e: e.tensor_tensor(out=out, in0=a, in1=b, op=op), [out], [a, b])

    def vts(out, a, s1, op0, s2=None, op1=None, accum=None):
        reads = [a] + [s for s in (s1, s2) if s is not None and not isinstance(s, float)]
        outs = [out] + ([accum] if accum is not None else [])
        kw = {}
        if op1 is not None:
            kw["op1"] = op1
        if accum is not None:
            kw["accum_out"] = accum
        dve(lambda e: e.tensor_scalar(out=out, in0=a, scalar1=s1, scalar2=s2, op0=op0, **kw), outs, reads)

    def vstt(out, a, s, b, op0, op1):
        reads = [a, b] + ([s] if not isinstance(s, float) else [])
        dve(lambda e: e.scalar_tensor_tensor(out=out, in0=a, scalar=s, in1=b, op0=op0, op1=op1), [out], reads)

    def vcopy(out, in_):
        dve(lambda e: e.tensor_copy(out=out, in_=in_), [out], [in_])

    def vscan(out, d0, d1, init):
        reads = [d0, d1] + ([init] if not isinstance(init, float) else [])
        dve(lambda e: e.tensor_tensor_scan(out=out, data0=d0, data1=d1, initial=init, op0=ALU.mult, op1=ALU.add), [out], reads)

    def vreduce(out, in_, op, absval=False):
        if absval:
            dve(lambda e: e.tensor_reduce(out=out, in_=in_, axis=AX.X, op=op, apply_absolute_value=True), [out], [in_])
        else:
            dve(lambda e: e.tensor_reduce(out=out, in_=in_, axis=AX.X, op=op), [out], [in_])

    def vmemset(ap, val):
        dve(lambda e: e.memset(ap, val), [ap], [])

    def ptt(out, a, b, op):
        pool(lambda e: e.tensor_tensor(out=out, in0=a, in1=b, op=op), [out], [a, b])

    def split3(dst, src32, tmpa, tmpb):
        vcopy(dst[:, 0, :], src32)
        vtt(tmpa, src32, dst[:, 0, :], ALU.subtract)
        vcopy(dst[:, 1, :], tmpa)
        vtt(tmpb, tmpa, dst[:, 1, :], ALU.subtract)
        vcopy(dst[:, 2, :], tmpb)

    def rstd_pow(out, ss, n, tmp):
        afunc(tmp, ss, AF.Ln, scale=1.0 / n, bias=EPS)
        afunc(out, tmp, AF.Exp, scale=-0.5)

    def sigm(buf, src, scale=-1.0, bias=None):
        afunc(buf, src, AF.Exp, scale=scale, bias=bias)
        afunc(buf, buf, AF.Ln, bias=1.0)
        afunc(buf, buf, AF.Exp, scale=-1.0)

    def vrecip(out, in_):
        dve(lambda e: e.reciprocal(out=out, in_=in_), [out], [in_])

    def mm(out, lhsT, rhs, start, stop=True):
        pe(lambda e: e.matmul(out, lhsT=lhsT, rhs=rhs, start=start, stop=stop), [out], [lhsT, rhs])

    def tr(out, in_, idt):
        pe(lambda e: e.transpose(out, in_, idt), [out], [in_, idt])

    def bc(ap, axis, n):
        a = ap.unsqueeze(axis)
        shp = list(a.shape)
        shp[axis] = n
        return a.broadcast_to(shp)

    try:
      _body(locals())
    except _Stop:
      pass
    pg.finish()
    return nc


def _body(env):
    globals().update({k: v for k, v in env.items() if not k.startswith("__")})
    dma("sp", cst[:], I["cst"][:, 0:768])
    dma("sp", A32[0:32, 0:384], I["cst"][0:32, 768:1152])
    dma("sp", rbt[:], I["rb"])
    vcopy(cbf[:, 0, :], cst[:, 0:128])
    vcopy(cbf[:, 1, :], cst[:, 384:512])
    vcopy(cbf[:, 2, :], cst[:, 512:640])
    vmemset(cbf[:, 3, :], 1.0)
    vcopy(cbf[:, 4, :], cst[:, 128:256])
    vcopy(cbf[:, 5, :], cst[:, 256:384])
    mm(pA[0:8, 0:384], rbt[:, :], A32[0:32, 0:384], True)
    vcopy(vecs[:], pA[0:8, 0:384])
    vcopy(rbt[0:8, 0:1], vecs[:, 0:1])
    vts(vecs[:], vecs[:], rbt[0:8, 0:1], ALU.subtract)
    dma("sp", vecd, vecs[:])
    hktmp = A32[:, 0:1024].rearrange("p (h s) -> p h s", h=8)
    for k in range(NBIS + 1):
        vmemset(p2row[:, k:k + 1], 2.0 ** -k)
    for si, (TTq, dl) in enumerate([(128, 0), (128, -128)]):
        base = dl - TTq + 256
        src = bass.AP(tensor=vecd.tensor, offset=base, ap=[[1, TTq], [384, 8], [1, 128]])
        dma("sp", hktmp[:TTq], src)
        vcopy(hk[:TTq, si, :, :], hktmp[:TTq])

    CK_OFF[0] = 0
    chk(1)
    seqs = []
    for q in range(NSP):
        seqs.append(dict(kind="p", idx=q, T=TP, P=0))
    if SAMPLE:
        seqs.append(dict(kind="s", idx=0, T=TS, P=PS))

    for l in range(DEPTH):
        for kc in range(8):
            for hf in range(2):
                c0, c1_ = hf * 2602, (hf + 1) * 2602
                dma("pool", win[:, kc, c0:c1_], I["w_in"][l, kc * 128:(kc + 1) * 128, c0:c1_])
        for ec in range(16):
            dma("pool", wout[:, ec, :], I["w_out"][l, ec * 128:(ec + 1) * 128, :])
        dma("sp", ppt[:], I["pp"][l])
        dma("sp", pbt[:], I["pb"][l])
        dma("pool", wabd[:].rearrange("p a b -> p (a b)"), I["wabd"][l])
        afunc(drv[:, 20:24], ppt[:, PP_LAM:PP_LAM + 4], AF.Exp, scale=-1.0)
        afunc(drv[:, 24:28], drv[:, 20:24], AF.Ln, bias=1.0)
        vts(c1, drv[:, 24:28], -8.0, ALU.mult)
        afunc(drv[:, 28:44], pbt[:, PB_ALOG:PB_ALOG + 16], AF.Exp)
        vts(aneg, drv[:, 28:44], -1.0, ALU.mult)
        vts(gq8[:], pbt[:, PB_GQ:PB_GQ + 64], 0.125, ALU.mult)
        vts(drv[:, 48:52], ppt[:, PP_LBA:PP_LBA + 4], -1.0, ALU.mult)
        vts(drv[:, 52:56], ppt[:, PP_LBX:PP_LBX + 4], -1.0, ALU.mult)
        vcopy(rb15row[0:1, :, :], bc(pbt[0:1, PB_RB15:PB_RB15 + 8], 2, 128))

        chk(2)
        for sq in seqs:
            T, P0 = sq["T"], sq["P"]
            isS = sq["kind"] == "s"
            qi_ = sq["idx"]
            L = P0 + T
            ktop = min(256, L // 4)
            if isS:
                xin = I["xs"] if l == 0 else xmid_s
                xout = O["ys"] if l == DEPTH - 1 else xmid_s
                o_ak, o_av, o_ik = O["aks"][l], O["avs"][l], O["iks"][l]
                o_lc, o_lh, o_sc, o_sh = O["lcs"][l], O["lhs"][l], O["scs"][l], O["shs"][l]
            else:
                xin = I["xp"][qi_] if l == 0 else xmid_p[qi_]
                xout = O["yp"][qi_] if l == DEPTH - 1 else xmid_p[qi_]
                o_ak, o_av, o_ik = O["akp"][l, qi_], O["avp"][l, qi_], O["ikp"][l, qi_]
                o_lc, o_lh, o_sc, o_sh = O["lcp"][l, qi_], O["lhp"][l, qi_], O["scp"][l, qi_], O["shp"][l, qi_]
            if isS:
                for g in range(4):
                    dma("sp", lhist[:, g, :], I["slc"][l].rearrange("j (g p) -> p g j", p=128)[:, g, :], allow_slow_non_contiguous=True)
                for g in range(12):
                    dma("sp", shist[:, g, :], I["ssc"][l].rearrange("j (g p) -> p g j", p=128)[:, g, :], allow_slow_non_contiguous=True)
                dma("sp", lh[:], I["slh"][l].rearrange("(g p) -> p g", p=128), allow_slow_non_contiguous=True)
                stt = A32[:, 0:1024].rearrange("p (c n) -> p c n", c=8)
                dma("sp", stt, I["ssh"][l].rearrange("(c p) n -> p c n", p=128))
                sp3 = AB[:, 0:3072].rearrange("p (k n) -> p k n", k=3)
                split3(sp3, A32[:, 0:1024], A32[:, 1024:2048], A32[:, 2048:3072])
                for c in range(8):
                    pbank = pA if c < 4 else pB
                    for k in range(3):
                        mm(pbank[:, (c % 4) * 128:(c % 4 + 1) * 128], sp3[:, k, c * 128:(c + 1) * 128], identb, c % 4 == 0 and k == 0, c % 4 == 3 and k == 2)
                vcopy(hst[:, 0:512], pA[:, :])
                vcopy(hst[:, 512:1024], pB[:, :])
                acopy(hsb[:, :], hst[:, :])
                nb = P0 // 128
                ckb = AB[:, 0:nb * 128].rearrange("p (c n) -> p c n", c=nb)
                kid = AB[:, nb * 128:2 * nb * 128].rearrange("p (c n) -> p c n", c=nb)
                dma("pool", ckb, I["ck"][l].rearrange("(c p) n -> p c n", p=128))
                dma("pool", vtok[:, 0:nb, :], I["cv"][l].rearrange("(c p) n -> p c n", p=128))
                dma("pool", kid[:, :, 0:64], I["cki"][l].rearrange("(c p) n -> p c n", p=128))
                dma("pool", kid[:, :, 64:128], I["cki"][l].rearrange("(c p) n -> p c n", p=128))
                for c in range(nb):
                    mm(pC[:, (c % 4) * 128:(c % 4 + 1) * 128], ckb[:, c, :], identb, c % 4 == 0, c % 4 == 3 or c == nb - 1)
                    mm(pD[:, (c % 4) * 128:(c % 4 + 1) * 128], kid[:, c, :], identb, c % 4 == 0, c % 4 == 3 or c == nb - 1)
                    if c % 4 == 3 or c == nb - 1:
                        c0 = (c // 4) * 4
                        n_ = c - c0 + 1
                        vcopy(khT[:, c0 * 128:(c0 + n_) * 128], pC[:, 0:n_ * 128])
                        acopy(kiT[:, c0 * 128:(c0 + n_) * 128], pD[:, 0:n_ * 128])
            else:
                vmemset(lhist[:], 0.0)
                vmemset(shist[:], 0.0)
                vmemset(lh[:], 0.0)
                vmemset(hst[:], 0.0)
                vmemset(hsb[:], 0.0)

            CK_OFF[0] = 10 if isS else 0
            chk(3)
            ngroups = (T + TG - 1) // TG
            for gi in range(ngroups):
                t0 = gi * TG
                NV = min(TG, T - t0)
                NT = ((NV + 127) // 128) * 128
                TT = 128
                ntile = NT // TT
                for ti in range(ntile):
                    r0 = t0 + ti * TT
                    TV = min(TT, NV - ti * TT)
                    xt32 = A32[:TT, 0:1024]
                    if TV < TT:
                        vmemset(A32[TV:TT, 0:1024], 0.0)
                    dma("sp", A32[:TV, 0:1024], xin[r0:r0 + TV, :])
                    junk = AB[:TT, 0:1024]
                    xn = AB[:TT, 1024:2048]
                    ssq = sm[:TT, ti:ti + 1]
                    afunc(junk, xt32, AF.Square, accum=ssq)
                    rstd_pow(sm[:TT, 4 + ti:5 + ti], ssq, D_MODEL, sm[:TT, 2 + ti:3 + ti])
                    vts(xn, xt32, sm[:TT, 4 + ti:5 + ti], ALU.mult)
                    for kc in range(8):
                        pb_ = pA if kc < 4 else pB
                        mm(pb_[:, (kc % 4) * TT:(kc % 4 + 1) * TT], xn[:, kc * 128:(kc + 1) * 128], identb[:TT, :TT], kc % 4 == 0, kc % 4 == 3)
                    for hf in range(2):
                        pb_ = pA if hf == 0 else pB
                        vtt(hT[:, hf * 4:(hf + 1) * 4, ti * TT:(ti + 1) * TT], pb_[:, 0:4 * TT].rearrange("p (c t) -> p c t", c=4),
                            bc(ppt[:, PP_GN + hf * 4:PP_GN + hf * 4 + 4], 2, TT), ALU.mult)

                chk(4)
                def proj_fm(pbank, c0, M, prow=0):
                    for kc in range(8):
                        mm(pbank[prow:prow + M, :NT], win[:, kc, c0:c0 + M], hT[:, kc, :NT], kc == 0, kc == 7)

                def conv(pbank, hist, g, wcol, bcol, out32, roff=0, ppt=ppt):
                    raw = A32[:, roff:roff + 3 + NT]
                    vcopy(raw[:, 0:3], hist[:, g, :])
                    acopy(raw[:, 3:3 + NT], pbank[:, :NT])
                    vcopy(hist[:, g, :], raw[:, NV:NV + 3])
                    vts(out32, raw[:, 0:NT], ppt[:, wcol:wcol + 1], ALU.mult, ppt[:, bcol:bcol + 1], ALU.add)
                    for j in range(1, 4):
                        vstt(out32, raw[:, j:j + NT], ppt[:, wcol + j:wcol + j + 1], out32, ALU.mult, ALU.add)
                    return raw

                def gen_lru(g):
                    lbase = (g % 2) * 1664
                    f = lambda k: A32[:, lbase + 260 + k * 256:lbase + 260 + k * 256 + NT]
                    gC, gD = (pC, pD) if g % 2 == 0 else (pG, pH)
                    xc, rr, ii, aa, s_ = [f(k) for k in range(5)]
                    gx, bb, hseq, sg = ii, s_, xc, rr
                    xcb = AB[:, 2048 + (g % 2) * 256:2048 + (g % 2) * 256 + NT]
                    pb_ = pA if g % 2 == 0 else pB
                    proj_fm(pb_, C_XL + g * 128, 128)
                    raw = conv(pb_, lhist, g, PP_LCW + g * 4, PP_LCB + g, xc, lbase)
                    if gi == ngroups - 1:
                        dma("sp", o_lc.rearrange("j (g p) -> p g j", p=128)[:, g, :], raw[:, NV:NV + 3], allow_slow_non_contiguous=True)
                    acopy(xcb, xc)
                    yield
                    mm(gC[:, :NT], wabd[:, g, :], xcb, True)
                    mm(gD[:, :NT], wabd[:, 4 + g, :], xcb, True)
                    sigm(rr, gC[:, :NT], -1.0, drv[:, 48 + g:49 + g])
                    yield
                    sigm(ii, gD[:, :NT], -1.0, drv[:, 52 + g:53 + g])
                    yield
                    afunc(aa, rr, AF.Exp, scale=c1[:, g:g + 1])
                    afunc(s_, aa, AF.Square)
                    afunc(s_, s_, AF.Ln, scale=-1.0, bias=1.0)
                    afunc(s_, s_, AF.Exp, scale=0.5)
                    yield
                    vtt(gx, ii, xc, ALU.mult)
                    vtt(bb, s_, gx, ALU.mult)
                    vscan(hseq, aa, bb, lh[:, g:g + 1])
                    vcopy(lh[:, g:g + 1], hseq[:, NV - 1:NV])
                    yield
                    pg_ = pE if g % 2 == 0 else pF
                    proj_fm(pg_, C_GL + g * 128, 128)
                    sigm(sg, pg_[:, :NT])
                    yield
                    vtt(sg, sg, pg_[:, :NT], ALU.mult)
                    vtt(mixT[:, g, :NT], hseq, sg, ALU.mult)
                    yield

                interleave(gen_lru(0), gen_lru(1))
                interleave(gen_lru(2), gen_lru(3))
                if gi == ngroups - 1:
                    dma("sp", o_lh.rearrange("(g p) -> p g", p=128), lh[:], allow_slow_non_contiguous=True)

                chk(5)
                for g in range(4):
                    pb_ = pA if g % 2 == 0 else pB
                    proj_fm(pb_, C_GA + g * 128, 128)
                    gth = A32[:, 512 + (g % 2) * 256:512 + (g % 2) * 256 + NT]
                    sigm(gth, pb_[:, :NT])
                    vtt(gaT[:, g, :NT], gth, pb_[:, :NT], ALU.mult)
                for g in range(2):
                    pb_ = pE if g % 2 == 0 else pF
                    proj_fm(pb_, C_QI + g * 128, 128)
                    vcopy(qiT[:, g, :NT], pb_[:, :NT])
                proj_fm(pA, C_KI, 64, 0)
                proj_fm(pA, C_KI, 64, 64)
                acopy(kiT[:, P0 + t0:P0 + t0 + NT], pA[:, :NT])

                def gen_conv(g):
                    pb_ = pE if g % 2 == 0 else pF
                    proj_fm(pb_, C_XBC + g * 128, 128)
                    roff = (g % 2) * 1280
                    acc = A32[:, roff + 512:roff + 512 + NT]
                    raw = conv(pb_, shist, g, PP_SCW + g * 4, PP_SCB + g, acc, roff)
                    if gi == ngroups - 1:
                        dma("sp", o_sc.rearrange("j (g p) -> p g j", p=128)[:, g, :], raw[:, NV:NV + 3], allow_slow_non_contiguous=True)
                    yield
                    cth = A32[:, roff + 768:roff + 768 + NT]
                    sigm(cth, acc)
                    yield
                    vtt(xbcT[:, g, :NT], cth, acc, ALU.mult)
                    yield

                for g2 in range(0, 12, 2):
                    interleave(gen_conv(g2), gen_conv(g2 + 1))

                chk(6)
                def gen_p2b(ti):
                    tsl = slice(ti * TT, (ti + 1) * TT)
                    r0 = t0 + ti * TT
                    TV = min(TT, NV - ti * TT)
                    kb = (P0 + r0) // 128
                    koff = (P0 + r0) % 128
                    qC, qD, qE, qF = (pC, pD, pE, pF) if ti % 2 == 0 else (pA, pB, pG, pH)
                    tb = (ti % 2) * 1536
                    for kc in range(8):
                        mm(qC[:TT, 0:512], hT[:, kc, tsl], win[:, kc, C_Q:C_Q + 512], kc == 0, kc == 7)
                    for (o0, c0, n_) in ((0, C_K, 256), (256, C_KI, 68), (324, C_DT, 16)):
                        for kc in range(8):
                            mm(qD[:TT, o0:o0 + n_], hT[:, kc, tsl], win[:, kc, c0:c0 + n_], kc == 0, kc == 7)
                    yield
                    for hf in range(2):
                        pz = qE if hf == 0 else qF
                        for kc in range(8):
                            mm(pz[:TT, :], hT[:, kc, tsl], win[:, kc, C_Z + hf * 512:C_Z + (hf + 1) * 512], kc == 0, kc == 7)
                        zth = A32[:TT, tb + 512 + hf * 512:tb + 1024 + hf * 512]
                        sigm(zth, pz[:TT, :])
                        vtt(sz[:TT, ti, hf * 512:(hf + 1) * 512], zth, pz[:TT, :], ALU.mult)
                        yield
                    sqt = A32[:TT, tb + 512:tb + 1024]
                    qn = A32[:TT, tb + 1024:tb + 1536]
                    kv32 = A32[:TT, tb + 1536:tb + 1792]
                    qhat = AB[:TT, 2304 + (ti % 2) * 640:2816 + (ti % 2) * 640]
                    khat = AB[:TT, 2816 + (ti % 2) * 640:2944 + (ti % 2) * 640]
                    afunc(sqt, qC[:TT, :], AF.Square)
                    vreduce(smb[:TT, ti % 2, 8:16], sqt.rearrange("p (h d) -> p h d", h=8), ALU.add)
                    rstd_pow(smb[:TT, ti % 2, 24:32], smb[:TT, ti % 2, 8:16], 64, smb[:TT, ti % 2, 16:24])
                    vtt(qn.rearrange("p (h d) -> p h d", h=8), qC[:TT, :].rearrange("p (h d) -> p h d", h=8), bc(smb[:TT, ti % 2, 24:32], 2, 64), ALU.mult)
                    vtt(qhat.rearrange("p (m g d) -> p g m d", m=4, g=2), qn.rearrange("p (g m d) -> p g m d", g=2, m=4),
                        bc(bc(gq8[:TT, :], 1, 4), 1, 2), ALU.mult)
                    yield
                    afunc(sqt[:, 0:128], qD[:TT, 0:128], AF.Square)
                    vreduce(smb[:TT, ti % 2, 32:34], sqt[:, 0:128].rearrange("p (h d) -> p h d", h=2), ALU.add)
                    rstd_pow(smb[:TT, ti % 2, 36:38], smb[:TT, ti % 2, 32:34], 64, smb[:TT, ti % 2, 34:36])
                    vtt(qn[:, 0:128].rearrange("p (h d) -> p h d", h=2), qD[:TT, 0:128].rearrange("p (h d) -> p h d", h=2), bc(smb[:TT, ti % 2, 36:38], 2, 64), ALU.mult)
                    vtt(kv32[:, 0:128].rearrange("p (h d) -> p h d", h=2), qn[:, 0:128].rearrange("p (h d) -> p h d", h=2),
                        bc(pbt[:TT, PB_GK:PB_GK + 64], 1, 2), ALU.mult)
                    acopy(khat, kv32[:, 0:128])
                    acopy(kv32[:, 128:256], qD[:TT, 128:256])
                    acopy(vtok[koff:koff + TT, kb, :], qD[:TT, 128:256])
                    dma("sp", o_ak[r0:r0 + TV, :], kv32[:TV, 0:128])
                    dma("sp", o_av[r0:r0 + TV, :], kv32[:TV, 128:256])
                    vcopy(kiw[:TT, ti, :], qD[:TT, 256:324])
                    dma("sp", o_ik[r0:r0 + TV, :], kiw[:TV, ti, 0:64])
                    yield
                    vtt(smb[:TT, ti % 2, 40:56], qD[:TT, 324:340], pbt[:TT, PB_DTB:PB_DTB + 16], ALU.add)
                    afunc(smb[:TT, ti % 2, 56:72], smb[:TT, ti % 2, 40:56], AF.Exp)
                    afunc(dtt[:TT, ti, :], smb[:TT, ti % 2, 56:72], AF.Ln, bias=1.0)
                    if TV < TT:
                        vmemset(dtt[TV:TT, ti, :], 0.0)
                    vtt(dAt[:TT, ti, :], dtt[:TT, ti, :], aneg[:TT, :], ALU.mult)
                    yield
                    mm(qE[:, 0:TT], khat, identb[:TT, :TT], True)
                    vcopy(khT[:, P0 + r0:P0 + r0 + TT], qE[:, 0:TT])
                    for m in range(4):
                        mm(qF[:, m * TT:(m + 1) * TT], qhat[:, m * 128:(m + 1) * 128], identb[:TT, :TT], m == 0, m == 3)
                    vcopy(qhT[:, :, tsl], qF[:, 0:4 * TT].rearrange("p (c t) -> p c t", c=4))
                    yield

                if ntile == 2:
                    interleave(gen_p2b(0), gen_p2b(1))
                else:
                    drain(gen_p2b(0))

                chk(7)
                for ti in range(ntile):
                    tsl = slice(ti * TT, (ti + 1) * TT)
                    if ti == 1:
                        chk(7.9)
                    dA = dAt[:TT, ti, :]
                    dtv = dtt[:TT, ti, :]
                    split3(dAs[:TT], dA, sm[:TT, 176:192], sm[:TT, 192:208])
                    for k in range(3):
                        mm(pF[:TT, 0:16], trib[:TT, :TT], dAs[:TT, k, :], k == 0, k == 2)
                    for k in range(3):
                        mm(pF[:TT, 16:32], astrb[:TT, :TT], dAs[:TT, k, :], k == 0, k == 2)
                    for k in range(3):
                        mm(pF[:, 32:48], onesb[:TT, :], dAs[:TT, k, :], k == 0, k == 2)
                    ecum = sm[:TT, 80:96]
                    toend = sm[:TT, 96:112]
                    dec = sm[:, 112:128]
                    afunc(sm[:TT, 80:112], pF[:TT, 0:32], AF.Exp)
                    afunc(dec, pF[:, 32:48], AF.Exp)
                    chk(7.1)
                    for c in range(8):
                        pb_ = pC if c < 4 else pD
                        mm(pb_[:TT, (c % 4) * 128:(c % 4 + 1) * 128], xbcT[:, c, tsl], identb, c % 4 == 0, c % 4 == 3)
                    xt_ = AB[:TT, 0:1024]
                    x2_ = AB[:TT, 1024:2048]
                    xD = A32[:TT, 0:1024]
                    for hf in range(2):
                        pTv = (pC if hf == 0 else pD)[:TT, :].rearrange("p (h d) -> p h d", h=8)
                        hs_ = slice(hf * 512, (hf + 1) * 512)
                        vtt(xt_[:, hs_].rearrange("p (h d) -> p h d", h=8), pTv, bc(dtv[:, hf * 8:(hf + 1) * 8], 2, 64), ALU.mult)
                        vtt(xD[:, hs_].rearrange("p (h d) -> p h d", h=8), pTv, bc(pbt[:TT, PB_D + hf * 8:PB_D + hf * 8 + 8], 2, 64), ALU.mult)
                    vtt(x2_.rearrange("p (h d) -> p h d", h=16), xt_.rearrange("p (h d) -> p h d", h=16), bc(toend, 2, 64), ALU.mult)
                    def gen_ssd(g, ti=ti, tsl=tsl, xt_=xt_, x2_=x2_, xD=xD, ecum=ecum, dec=dec):
                        eb = 2048 + g * 1280
                        E = AB[:TT, eb:eb + 8 * TT].rearrange("p (h l) -> p h l", h=8)
                        WT = E
                        G = AB[:TT, eb + 1024:eb + 1024 + TT]
                        bmtok = AB[:TT, eb + 1152:eb + 1280]
                        ycb = AB[:TT, eb:eb + 512]
                        t1 = A32[:TT, 1024 + g * 1024:1536 + g * 1024]
                        yz = A32[:TT, 1536 + g * 1024:2048 + g * 1024]
                        dbk = (pA, pB) if g == 0 else (pG, pH)
                        cbk = pC if g == 0 else pF
                        ydk = pD if g == 0 else pG
                        yok = pE if g == 0 else pH
                        ytk = dbk[0]
                        hpb = 512 // TT
                        for bk in range(8 // hpb):
                            pb_ = dbk[bk % 2]
                            for k in range(2):
                                Rk = RB[:TT, (g * 2 + bk + k) % 2, 0:hpb * TT]
                                h0_ = g * 8 + bk * hpb
                                vtt(Rk.rearrange("p (h l) -> p h l", h=hpb), bc(dAs[:TT, k, h0_:h0_ + hpb], 2, TT), bc(trib[:TT, :TT], 1, hpb), ALU.mult)
                                mm(pb_[:TT, 0:hpb * TT], astrb[:TT, :TT], Rk, k == 0, k == 1)
                            afunc(E[:, bk * hpb:(bk + 1) * hpb, :], pb_[:TT, 0:hpb * TT].rearrange("p (h l) -> p h l", h=hpb), AF.Exp)
                            yield
                        mm(cbk[:TT, :TT], xbcT[:, 8 + g, tsl], xbcT[:, 10 + g, tsl], True)
                        vtt(G, cbk[:TT, :TT], tri[:TT, :TT], ALU.mult)
                        vtt(WT, E, bc(G, 1, 8), ALU.mult)
                        yield
                        for hh in range(8):
                            h_ = g * 8 + hh
                            mm(ydk[:TT, hh * 64:(hh + 1) * 64], WT[:, hh, :], xt_[:, h_ * 64:(h_ + 1) * 64], hh == 0, hh == 7)
                        mm(yok[:TT, :], xbcT[:, 10 + g, tsl], hsb[:, g * 512:(g + 1) * 512], True)
                        yield
                        vtt(t1.rearrange("p (h d) -> p h d", h=8), yok[:TT, :].rearrange("p (h d) -> p h d", h=8), bc(ecum[:, g * 8:(g + 1) * 8], 2, 64), ALU.mult)
                        vtt(t1, t1, ydk[:TT, :], ALU.add)
                        vtt(t1, t1, xD[:, g * 512:(g + 1) * 512], ALU.add)
                        vtt(yz, t1, sz[:TT, ti, g * 512:(g + 1) * 512], ALU.mult)
                        yield
                        afunc(t1, yz, AF.Square, accum=sm[:TT, 128 + g:129 + g])
                        rstd_pow(sm[:TT, 132 + g:133 + g], sm[:TT, 128 + g:129 + g], 512, sm[:TT, 130 + g:131 + g])
                        vts(t1, yz, sm[:TT, 132 + g:133 + g], ALU.mult)
                        vtt(ycb, t1, pbt[:TT, PB_GSSD + g * 512:PB_GSSD + (g + 1) * 512], ALU.mult)
                        yield
                        for c in range(4):
                            mm(ytk[:, c * TT:(c + 1) * TT], ycb[:, c * 128:(c + 1) * 128], identb[:TT, :TT], c == 0, c == 3)
                        vcopy(mixT[:, 8 + g * 4:12 + g * 4, tsl], ytk[:, 0:4 * TT].rearrange("p (c t) -> p c t", c=4))
                        yield
                        mm(cbk[:TT, 0:128], xbcT[:, 8 + g, tsl], identb, True)
                        acopy(bmtok, cbk[:TT, 0:128])
                        mm(cbk[:, :], bmtok, x2_[:, g * 512:(g + 1) * 512], True)
                        hv = hst[:, g * 512:(g + 1) * 512]
                        vtt(hv.rearrange("p (h d) -> p h d", h=8), hv.rearrange("p (h d) -> p h d", h=8), bc(dec[:, g * 8:(g + 1) * 8], 2, 64), ALU.mult)
                        vtt(hv, hv, cbk[:, :], ALU.add)
                        acopy(hsb[:, g * 512:(g + 1) * 512], hv)
                        yield

                    interleave(gen_ssd(0), gen_ssd(1))
                if gi == ngroups - 1:
                    stt = A32[:, 0:1024].rearrange("p (c n) -> p c n", c=8)
                    sp3 = AB[:, 0:3072].rearrange("p (k n) -> p k n", k=3)
                    split3(sp3, hst[:, :], A32[:, 1024:2048], A32[:, 2048:3072])
                    for c in range(8):
                        pbank = pA if c < 4 else pB
                        for k in range(3):
                            mm(pbank[:, (c % 4) * 128:(c % 4 + 1) * 128], sp3[:, k, c * 128:(c + 1) * 128], identb, c % 4 == 0 and k == 0, c % 4 == 3 and k == 2)
                    vcopy(stt[:, 0:4, :], pA[:, :].rearrange("p (c n) -> p c n", c=4))
                    vcopy(stt[:, 4:8, :], pB[:, :].rearrange("p (c n) -> p c n", c=4))
                    dma("sp", o_sh.rearrange("(c p) n -> p c n", p=128), stt)

                chk(8)
                def p4_vars(ti):
                    q0 = P0 + t0 + ti * TT
                    Lk = q0 + TT
                    return (slice(ti * TT, (ti + 1) * TT), q0, Lk, (Lk + 127) // 128, A32[:TT, 0:Lk], AB[:TT, 0:Lk], JK[:TT, 0:Lk])

                def gen_topk(ti):
                    tsl, q0, Lk, nkb, score, negm, junkb = p4_vars(ti)
                    for c0 in range(0, Lk, 512):
                        c1_ = min(Lk, c0 + 512)
                        for hh in range(4):
                            pb_ = pA if hh % 2 == 0 else pB
                            pr = (hh % 2) * 64
                            mm(pb_[:TT, 0:c1_ - c0], qiT[pr:pr + 64, hh // 2, tsl], kiT[pr:pr + 64, c0:c1_], True)
                            rl = A32[:TT, 2048 + (hh % 2) * 512:2560 + (hh % 2) * 512]
                            afunc(rl[:, 0:c1_ - c0], pb_[:TT, 0:c1_ - c0], AF.Relu)
                            if hh == 0:
                                vts(score[:, c0:c1_], rl[:, 0:c1_ - c0], kiw[:TT, ti, 64:65], ALU.mult)
                            else:
                                vstt(score[:, c0:c1_], rl[:, 0:c1_ - c0], kiw[:TT, ti, 64 + hh:65 + hh], score[:, c0:c1_], ALU.mult, ALU.add)
                            yield
                    lo = sm[:TT, 140:141]
                    if Lk > ktop:
                        amax = sm[:TT, 141:142]
                        hw = sm[:TT, 144:144 + NBIS + 1]
                        vreduce(amax, score, ALU.max, absval=True)
                        vtt(score[:, Lk - 128:Lk], score[:, Lk - 128:Lk], dmask[:, :], ALU.add)
                        vts(amax, amax, 1.001, ALU.mult, 1e-3, ALU.add)
                        vts(hw, p2row[:TT, :], amax, ALU.mult)
                        mid = sm[:TT, 142:143]
                        cnt = sm[:TT, 143:144]
                        ind = sm[:TT, 170:171]
                        vmemset(mid, 0.0)
                        yield
                        for it in range(NBIS):
                            vts(junkb, score, mid, ALU.is_gt, 0.0, ALU.add, accum=cnt)
                            vstt(ind, cnt, float(ktop) - 0.5, hw[:, it:it + 1], ALU.is_gt, ALU.mult)
                            vstt(mid, ind, hw[:, it + 1:it + 2], mid, ALU.subtract, ALU.add)
                            yield
                        vtt(lo, mid, hw[:, NBIS:NBIS + 1], ALU.subtract)
                    else:
                        vtt(score[:, Lk - 128:Lk], score[:, Lk - 128:Lk], dmask[:, :], ALU.add)
                        vmemset(lo, NEG / 2)

                def do_mask(ti):
                    tsl, q0, Lk, nkb, score, negm, junkb = p4_vars(ti)
                    lo = sm[:TT, 140:141]
                    if Lk > ktop:
                        c0 = sm[:TT, 171:172]
                        mrem = sm[:TT, 172:173]
                        vts(junkb, score, 0.0, ALU.is_gt, 0.0, ALU.add, accum=c0)
                        vts(mrem, c0, -1.0, ALU.mult, float(ktop), ALU.add)
                        vts(negm, score, 0.0, ALU.is_equal)
                        vscan(negm, onesb[:TT, 0:1].broadcast_to([TT, Lk]), negm, 0.0)
                        vts(negm, negm, mrem, ALU.is_gt)
                        vstt(negm, score, 0.0, negm, ALU.is_equal, ALU.mult)
                        vstt(negm, score, lo, negm, ALU.is_le, ALU.max)
                        vts(negm, negm, NEG, ALU.mult)
                    else:
                        vts(negm, score, lo, ALU.is_le, NEG, ALU.mult)

                def gen_attn(ti):
                    tsl, q0, Lk, nkb, score, negm, junkb = p4_vars(ti)
                    for g in range(2):
                        for jb in range(nkb):
                            S = min(128, Lk - jb * 128)
                            pl = pC if jb % 2 == 0 else pD
                            plv = pl[:S, 0:4 * TT].rearrange("p (m t) -> p m t", m=4)
                            mm(plv, negm[:, jb * 128:jb * 128 + S], bc(identb[:TT, :TT], 1, 4), True, False)
                            dl = jb * 128 - q0
                            mm(plv, khT[g * 64:(g + 1) * 64, jb * 128:jb * 128 + S], qhT[g * 64:(g + 1) * 64, :, tsl], False, dl <= -256)
                            if dl > -256:
                                si = 0 if dl == 0 else 1
                                for m in range(4):
                                    mm(plv[:, m, :], hk[:TT, si, g * 4 + m, :S], j128b[:TT, :TT], False, m == 3)
                            ET = AB[:S, 2048 + (jb % 2) * 512:2048 + (jb % 2) * 512 + 4 * TT].rearrange("p (m t) -> p m t", m=4)
                            afunc(ET, plv, AF.Exp)
                            pov = pE[:, 0:2 * TT].rearrange("p (a t) -> p a t", a=2)
                            pdv = pF[:, 0:2 * TT].rearrange("p (a t) -> p a t", a=2)
                            for par in range(2):
                                mm(pov[par * 64:(par + 1) * 64], vtok[:S, jb, g * 64:(g + 1) * 64], ET[:, par::2, :], jb == 0, jb == nkb - 1)
                                mm(pdv[par * 64:(par + 1) * 64], onesb[:S, 0:64], ET[:, par::2, :], jb == 0, jb == nkb - 1)
                            yield
                        rden = A32[:, 3072:3072 + 2 * TT]
                        vrecip(rden, pF[:, 0:2 * TT])
                        vtt(rden, pE[:, 0:2 * TT], rden, ALU.mult)
                        vtt(mixT[:, 4 + 2 * g:6 + 2 * g, tsl], rden.rearrange("p (a t) -> p a t", a=2), gaT[:, 2 * g:2 * g + 2, tsl], ALU.mult)
                        yield

                drain(gen_topk(0))
                do_mask(0)
                if ntile == 2:
                    interleave(gen_attn(0), gen_topk(1))
                    do_mask(1)
                    drain(gen_attn(1))
                else:
                    drain(gen_attn(0))

                chk(9)
                if DEBUG_MIX[0] and not isS and qi_ == 0 and l == 0:
                    dma("sp", O["dbg"][:, :, t0:t0 + NT].rearrange("c p t -> p c t"), mixT[:, :, :NT])
                for ti in range(ntile):
                    tsl = slice(ti * TT, (ti + 1) * TT)
                    r0 = t0 + ti * TT
                    yo = A32[:TT, 0:1024]
                    TV = min(TT, NV - ti * TT)
                    xr = A32[:TT, 1024:2048]
                    if TV < TT:
                        vmemset(A32[TV:TT, 1024:2048], 0.0)
                    dma("sp", A32[:TV, 1024:2048], xin[r0:r0 + TV, :])
                    for hf in range(2):
                        pb_ = pA if hf == 0 else pB
                        for ec in range(16):
                            mm(pb_[:TT, :], mixT[:, ec, tsl], wout[:, ec, hf * 512:(hf + 1) * 512], ec == 0, ec == 15)
                        vtt(yo[:, hf * 512:(hf + 1) * 512], pb_[:TT, :], xr[:, hf * 512:(hf + 1) * 512], ALU.add)
                    dma("sp", xout[r0:r0 + TV, :], yo[:TV])


def _t5_bucket_np(rel):
    import math
    import jax
    import jax.numpy as jnp
    with jax.default_device(jax.devices("cpu")[0]):
        return _t5_bucket_cpu(rel, math, jnp)


def _t5_bucket_cpu(rel, math, jnp):
    rel = jnp.asarray(rel, dtype=jnp.int32)
    half, max_exact = 16, 8
    n = jnp.abs(rel)
    large = max_exact + (jnp.log(jnp.maximum(n, 1).astype(jnp.float32) / max_exact) / math.log(128 / max_exact) * (half - max_exact)).astype(jnp.int32)
    large = jnp.minimum(large, half - 1)
    return np.asarray(jnp.where(rel > 0, half, 0) + jnp.where(n < max_exact, n, large))


def _consts():
    c = np.zeros((128, NCST), np.float32)
    i = np.arange(128)
    c[:, 0:128] = np.eye(128)
    c[:, 128:256] = (i[:, None] <= i[None, :])
    c[:, 256:384] = (i[:, None] > i[None, :])
    c[:, 384:512] = (i[:, None] == 127 - i[None, :])
    c[:64, 512:576] = (i[:64, None] == 63 - i[None, :64])
    c[:, 640:768] = np.where((i[None, :] // 64) > (i[:, None] // 64), NEG, 0.0)
    bk = _t5_bucket_np(np.arange(384) - 255)
    c[0:32, 768:1152] = (np.arange(32)[:, None] == bk[None, :])
    return c


_PROG_CACHE = {}


def _run(inputs, NCORES, NSP, TP, SAMPLE, PS, TS, DEPTH):
    key = (NSP, TP, SAMPLE, PS, TS, DEPTH)
    f32 = np.float32
    g = lambda k: np.ascontiguousarray(np.asarray(inputs[k], dtype=f32))
    w_in, w_out = g("w_in"), g("w_out")
    cst = _consts()
    pp = np.zeros((DEPTH, 128, NPP), f32)
    pb = np.zeros((DEPTH, 128, NPB), f32)
    wabd = np.zeros((DEPTH, 128, 2, 4, 128), f32)
    fm = lambda v, n: v.reshape(n, 128).T
    for l in range(DEPTH):
        pp[l, :, PP_GN:PP_GN + 8] = fm(g("norm_w")[l], 8)
        lcw = g("lru_conv_w")[l]
        pp[l, :, PP_LCW:PP_LCW + 16] = lcw.reshape(4, 4, 128).transpose(2, 1, 0).reshape(128, 16)
        pp[l, :, PP_LCB:PP_LCB + 4] = fm(g("lru_conv_b")[l], 4)
        pp[l, :, PP_LBA:PP_LBA + 4] = fm(g("lru_b_a")[l], 4)
        pp[l, :, PP_LBX:PP_LBX + 4] = fm(g("lru_b_x")[l], 4)
        pp[l, :, PP_LAM:PP_LAM + 4] = fm(g("lru_lambda")[l], 4)
        scw = g("ssd_conv_w")[l]
        pp[l, :, PP_SCW:PP_SCW + 48] = scw.reshape(4, 12, 128).transpose(2, 1, 0).reshape(128, 48)
        pp[l, :, PP_SCB:PP_SCB + 12] = fm(g("ssd_conv_b")[l], 12)
        pb[l, :, PB_GQ:PB_GQ + 64] = g("att_q_norm")[l][None, :]
        pb[l, :, PB_GK:PB_GK + 64] = g("att_k_norm")[l][None, :]
        pb[l, :, PB_DTB:PB_DTB + 16] = g("ssd_dt_bias")[l][None, :]
        pb[l, :, PB_ALOG:PB_ALOG + 16] = g("ssd_a_log")[l][None, :]
        pb[l, :, PB_D:PB_D + 16] = g("ssd_d")[l][None, :]
        pb[l, :, PB_GSSD:PB_GSSD + 1024] = g("ssd_norm")[l][None, :]
        pb[l, :, PB_RB15:PB_RB15 + 8] = g("rel_bias")[15][None, :]
        for a, nm in enumerate(("lru_w_a", "lru_w_x")):
            w = g(nm)[l]
            for gg in range(4):
                wabd[l, 0:64, a, gg, 0:64] = w[2 * gg]
                wabd[l, 64:128, a, gg, 64:128] = w[2 * gg + 1]
    wabd = wabd.reshape(DEPTH, 128, 1024)
    xp = g("x_prompt")
    in_maps = []
    for c in range(NCORES):
        m = {"xp": xp[c * NSP:(c + 1) * NSP], "w_in": w_in, "w_out": w_out, "pp": pp, "pb": pb, "wabd": wabd,
             "rb": g("rel_bias"), "cst": cst}
        if SAMPLE:
            m["xs"] = g("x_sample")[c]
            m["ck"] = g("cache_att_k")[:, c].reshape(DEPTH, PS, 128)
            m["cv"] = g("cache_att_v")[:, c].reshape(DEPTH, PS, 128)
            m["cki"] = g("cache_idx_k")[:, c]
            m["slc"] = g("state_lru_conv")[:, c]
            m["slh"] = g("state_lru_h")[:, c]
            m["ssc"] = g("state_ssd_conv")[:, c]
            m["ssh"] = g("state_ssd_h")[:, c].reshape(DEPTH, 1024, 128)
        in_maps.append({k: np.ascontiguousarray(v) for k, v in m.items()})
    if key not in _PROG_CACHE:
        _PROG_CACHE[key] = build_program(NSP=NSP, TP=TP, SAMPLE=SAMPLE, PS=PS, TS=TS, DEPTH=DEPTH)
    nc = _PROG_CACHE[key]
    res = run_bass_kernel_spmd(nc, in_maps, core_ids=list(range(NCORES)))
    R = res.results
    cat = lambda k, ax: np.concatenate([np.asarray(r[k]) for r in R], axis=ax)
    stk = lambda k, ax: np.stack([np.asarray(r[k]) for r in R], axis=ax)
    B = NCORES * NSP
    outs = [cat("yp", 0)]
    if SAMPLE:
        outs.append(stk("ys", 0))
    outs += [cat("akp", 1).reshape(DEPTH, B, TP, 2, 64), cat("avp", 1).reshape(DEPTH, B, TP, 2, 64), cat("ikp", 1),
             cat("lcp", 1), cat("lhp", 1), cat("scp", 1), cat("shp", 1).reshape(DEPTH, B, 16, 64, 128)]
    if SAMPLE:
        outs += [stk("aks", 1).reshape(DEPTH, NCORES, TS, 2, 64), stk("avs", 1).reshape(DEPTH, NCORES, TS, 2, 64), stk("iks", 1),
                 stk("lcs", 1), stk("lhs", 1), stk("scs", 1), stk("shs", 1).reshape(DEPTH, NCORES, 16, 64, 128)]
    return tuple(np.ascontiguousarray(o, dtype=np.float32) for o in outs)


def kernel(**inputs):
    return _run(inputs, 8, 2, 2048, True, 1024, 64, 2)
```

```python
import numpy as np
from contextlib import ExitStack
import concourse.bass as bass
import concourse.mybir as mybir
from concourse.bass_utils import run_bass_kernel_spmd

F32 = mybir.dt.float32
BF16 = mybir.dt.bfloat16
AF = mybir.ActivationFunctionType
ALU = mybir.AluOpType
AX = mybir.AxisListType


def _region(ap):
    t = ap.tensor
    pat = [(int(s), int(c)) for (s, c) in ap.ap]
    off = int(ap.offset)
    kind = type(t).__name__
    if kind.startswith("DRam"):
        lo = off
        ext = sum((c - 1) * abs(s) for s, c in pat)
        n = 1
        for s, c in pat:
            if s != 0:
                n *= c
        return (t.name, 0, 1, lo, lo + ext + 1, n == ext + 1)
    shp = [int(v) for v in t.shape]
    pstride = 1
    for v in shp[1:]:
        pstride *= v
    p0 = off // pstride
    lo = off % pstride
    npart = pat[0][1]
    ext = sum((c - 1) * abs(s) for s, c in pat[1:])
    n = 1
    for s, c in pat[1:]:
        if s != 0:
            n *= c
    if kind.startswith("PSum"):
        full = (lo == 0 and lo + ext + 1 == pstride and n == ext + 1)
        return (t.name, (p0 // 32) * 32, ((p0 + npart + 31) // 32) * 32, 0, pstride, full and p0 % 32 == 0 and (p0 + npart) % 32 == 0)
    return (t.name, p0, p0 + npart, lo, lo + ext + 1, n == ext + 1)


def _ovl(a, b):
    return a[1] < b[2] and b[1] < a[2] and a[3] < b[4] and b[3] < a[4]


def _covers(a, b):
    return a[5] and a[1] <= b[1] and a[2] >= b[2] and a[3] <= b[3] and a[4] >= b[4]


class _Stop(Exception):
    pass


STOP_AT = [99]
DEBUG_MIX = [False]


CK_OFF = [0]


def drain(gen):
    for _ in gen:
        pass


def interleave(ga_, gb_):
    a_live = b_live = True
    while a_live or b_live:
        if a_live:
            a_live = next(ga_, "end") != "end"
        if b_live:
            b_live = next(gb_, "end") != "end"


def chk(k):
    if k + CK_OFF[0] > STOP_AT[0]:
        raise _Stop()


class Prog:
    def __init__(self, nc):
        self.nc = nc
        self.stack = ExitStack()
        self.ops = []
        self.wr = {}
        self.rd = {}
        self.finished = False

    def sb(self, name, shape, dt):
        return self.stack.enter_context(self.nc.sbuf_tensor("sb_" + name, list(shape), dt))

    def ps(self, name, shape, dt):
        return self.stack.enter_context(self.nc.psum_tensor("ps_" + name, list(shape), dt))

    def _add(self, eng, fn, outs, ins, is_dma=False):
        idx = len(self.ops)
        deps = {}
        rregs = [_region(a) for a in ins]
        wregs = [_region(a) for a in outs]
        for r in rregs:
            for w in self.wr.get(r[0], ()):
                if _ovl(r, w[1]):
                    deps.setdefault(w[0], set()).add("raw")
        for r in wregs:
            for w in self.wr.get(r[0], ()):
                if _ovl(r, w[1]):
                    deps.setdefault(w[0], set()).add("waw")
            for w in self.rd.get(r[0], ()):
                if _ovl(r, w[1]):
                    deps.setdefault(w[0], set()).add("war")
        for r in wregs:
            lw = self.wr.setdefault(r[0], [])
            lw[:] = [w for w in lw if not _covers(r, w[1])]
            lw.append((idx, r))
            lr = self.rd.get(r[0])
            if lr:
                lr[:] = [w for w in lr if not _covers(r, w[1])]
        for r in rregs:
            self.rd.setdefault(r[0], []).append((idx, r))
        self.ops.append(dict(eng=eng, fn=fn, deps=deps, dma=is_dma, sig=False))
        return idx

    def pe(self, fn, outs, ins):
        return self._add("pe", fn, outs, ins)

    def dve(self, fn, outs, ins):
        return self._add("dve", fn, outs, ins)

    def act(self, fn, outs, ins):
        return self._add("act", fn, outs, ins)

    def pool(self, fn, outs, ins):
        return self._add("pool", fn, outs, ins)

    def dma(self, q, out, in_, **kw):
        return self._add(q, lambda e: e.dma_start(out=out, in_=in_, **kw), [out], [in_], is_dma=True)

    def finish(self):
        nc = self.nc
        ops = self.ops
        engs = ["pe", "dve", "act", "pool", "sp"]
        for i, op in enumerate(ops):
            nd = set()
            for j, kinds in op["deps"].items():
                o = ops[j]
                if o["dma"]:
                    nd.add(j)
                    continue
                if o["eng"] == op["eng"] and not op["dma"]:
                    if op["eng"] == "pe":
                        continue
                nd.add(j)
            op["deps"] = nd
            for j in nd:
                ops[j]["sig"] = True
        esem = {e: self.stack.enter_context(nc.semaphore("s_" + e)) for e in engs}
        nds = {"sp": 40, "pool": 16, "act": 8}
        dsem = {q: [self.stack.enter_context(nc.semaphore("d_%s%d" % (q, k))) for k in range(n)] for q, n in nds.items()}
        dcnt = {q: [0] * n for q, n in nds.items()}
        dnext = {q: 0 for q in nds}
        ecnt = {e: 0 for e in engs}
        for i, op in enumerate(ops):
            if op["dma"]:
                q = op["eng"]
                k = dnext[q] % nds[q]
                dnext[q] += 1
                prev = dcnt[q][k]
                dcnt[q][k] += 16
                op["event"] = (dsem[q][k], dcnt[q][k])
                op["prevev"] = (dsem[q][k], prev) if prev > 0 else None
            elif op["sig"]:
                ecnt[op["eng"]] += 1
                op["event"] = (esem[op["eng"]], ecnt[op["eng"]])
        waited = {e: {} for e in engs}
        per_eng = {e: [] for e in engs}
        for i, op in enumerate(ops):
            e = op["eng"]
            need = {}
            for j in op["deps"]:
                s, v = ops[j]["event"]
                key = id(s)
                if need.get(key, (None, 0))[1] < v:
                    need[key] = (s, v)
            if op["dma"] and op["prevev"] is not None:
                s, v = op["prevev"]
                key = id(s)
                if need.get(key, (None, 0))[1] < v:
                    need[key] = (s, v)
            waits = []
            for key, (s, v) in need.items():
                if waited[e].get(key, 0) >= v:
                    continue
                waited[e][key] = v
                waits.append((s, v))
            op["waits"] = waits
            per_eng[e].append(op)
        self.n_waits = sum(len(o["waits"]) for o in ops)
        final = []
        for q in nds:
            for k in range(nds[q]):
                if dcnt[q][k] > 0:
                    final.append((dsem[q][k], dcnt[q][k]))

        def emit(ename, e):
            for op in per_eng[ename]:
                for (s, v) in op["waits"]:
                    e.wait_ge(s, v)
                ins = op["fn"](e)
                if op["dma"]:
                    ins.then_inc(op["event"][0], 16)
                elif op["sig"]:
                    ins.then_inc(op["event"][0], 1)
            if ename == "sp":
                for (s, v) in final:
                    e.wait_ge(s, v)

        with nc.Block() as block:
            @block.tensor
            def _(e):
                emit("pe", e)

            @block.vector
            def _(e):
                emit("dve", e)

            @block.scalar
            def _(e):
                emit("act", e)

            @block.gpsimd
            def _(e):
                emit("pool", e)

            @block.sync
            def _(e):
                emit("sp", e)
        self.stack.close()
        self.finished = True


D_MODEL = 1024
D_IN = 5204
D_MIX = 2048
EPS = 1e-6
NEG = -30000.0
C_XL, C_GL, C_Q, C_K, C_GA, C_QI, C_KI, C_Z, C_XBC, C_DT = 0, 512, 1024, 1536, 1792, 2304, 2560, 2628, 3652, 5188
NPP = 100
NPB = 1208
NCST = 1152
PB_GQ, PB_GK, PB_DTB, PB_ALOG, PB_D, PB_GSSD, PB_RB15 = 0, 64, 128, 144, 160, 176, 1200
PP_GN, PP_LCW, PP_LCB, PP_LBA, PP_LBX, PP_LAM, PP_SCW, PP_SCB = 0, 8, 24, 28, 32, 36, 40, 88


def build_program(NSP=2, TP=2048, SAMPLE=True, PS=1024, TS=64, DEPTH=2, TG=256, NBIS=16):
    nc = bass.Bass("TRN2", target_bir_lowering=False)
    pg = Prog(nc)
    dt_in = lambda name, shape: nc.dram_tensor(name, list(shape), F32, kind="ExternalInput").ap()
    dt_out = lambda name, shape: nc.dram_tensor(name, list(shape), F32, kind="ExternalOutput").ap()
    dt_tmp = lambda name, shape: nc.dram_tensor(name, list(shape), F32, kind="Internal").ap()
    I = {}
    I["xp"] = dt_in("xp", [NSP, TP, D_MODEL])
    I["w_in"] = dt_in("w_in", [DEPTH, D_MODEL, D_IN])
    I["w_out"] = dt_in("w_out", [DEPTH, D_MIX, D_MODEL])
    I["pp"] = dt_in("pp", [DEPTH, 128, NPP])
    I["pb"] = dt_in("pb", [DEPTH, 128, NPB])
    I["wabd"] = dt_in("wabd", [DEPTH, 128, 2 * 4 * 128])
    I["rb"] = dt_in("rb", [32, 8])
    I["cst"] = dt_in("cst", [128, NCST])
    O = {}
    O["yp"] = dt_out("yp", [NSP, TP, D_MODEL])
    O["akp"] = dt_out("akp", [DEPTH, NSP, TP, 128])
    O["avp"] = dt_out("avp", [DEPTH, NSP, TP, 128])
    O["ikp"] = dt_out("ikp", [DEPTH, NSP, TP, 64])
    O["lcp"] = dt_out("lcp", [DEPTH, NSP, 3, 512])
    O["lhp"] = dt_out("lhp", [DEPTH, NSP, 512])
    O["scp"] = dt_out("scp", [DEPTH, NSP, 3, 1536])
    O["shp"] = dt_out("shp", [DEPTH, NSP, 1024, 128])
    xmid_p = dt_tmp("xmid_p", [NSP, TP, D_MODEL])
    if DEBUG_MIX[0]:
        O["dbg"] = nc.dram_tensor("dbg", [16, 128, TP], BF16, kind="ExternalOutput").ap()
    vecd = dt_tmp("vecd", [8, 384])
    if SAMPLE:
        I["xs"] = dt_in("xs", [TS, D_MODEL])
        I["ck"] = dt_in("ck", [DEPTH, PS, 128])
        I["cv"] = dt_in("cv", [DEPTH, PS, 128])
        I["cki"] = dt_in("cki", [DEPTH, PS, 64])
        I["slc"] = dt_in("slc", [DEPTH, 3, 512])
        I["slh"] = dt_in("slh", [DEPTH, 512])
        I["ssc"] = dt_in("ssc", [DEPTH, 3, 1536])
        I["ssh"] = dt_in("ssh", [DEPTH, 1024, 128])
        O["ys"] = dt_out("ys", [TS, D_MODEL])
        O["aks"] = dt_out("aks", [DEPTH, TS, 128])
        O["avs"] = dt_out("avs", [DEPTH, TS, 128])
        O["iks"] = dt_out("iks", [DEPTH, TS, 64])
        O["lcs"] = dt_out("lcs", [DEPTH, 3, 512])
        O["lhs"] = dt_out("lhs", [DEPTH, 512])
        O["scs"] = dt_out("scs", [DEPTH, 3, 1536])
        O["shs"] = dt_out("shs", [DEPTH, 1024, 128])
        xmid_s = dt_tmp("xmid_s", [TS, D_MODEL])

    LMAX = max(TP, (PS + TS) if SAMPLE else 0)
    LMAX = ((LMAX + 127) // 128) * 128
    NKB = LMAX // 128
    sb, ps = pg.sb, pg.ps
    win = sb("win", [128, 8, D_IN], BF16)
    wout = sb("wout", [128, 16, D_MODEL], BF16)
    ppt = sb("ppt", [128, NPP], F32)
    pbt = sb("pbt", [128, NPB], F32)
    wabd = sb("wabdb", [128, 8, 128], BF16)
    cst = sb("cst", [128, 768], F32)
    ident = cst[:, 0:128]
    tri = cst[:, 128:256]
    astr = cst[:, 256:384]
    dmask = cst[:, 640:768]
    cbf = sb("cbf", [128, 6, 128], BF16)
    trib, astrb = cbf[:, 4, :], cbf[:, 5, :]
    RB = sb("RB", [128, 2, 512], BF16)
    JK = sb("JK", [128, LMAX], mybir.dt.uint8)
    dAs = sb("dAs", [128, 3, 16], BF16)
    identb, j128b, j64b, onesb = cbf[:, 0, :], cbf[:, 1, :], cbf[:, 2, :], cbf[:, 3, :]
    hk = sb("hk", [128, 2, 8, 128], BF16)
    p2row = sb("p2row", [128, NBIS + 1], F32)
    rbt = sb("rbt", [32, 8], F32)
    rb15row = sb("rb15row", [1, 8, 128], BF16)
    vecs = sb("vecs", [8, 384], F32)
    drv = sb("drv", [128, 64], F32)
    c1 = drv[:, 0:4]
    aneg = drv[:, 4:20]
    gq8 = sb("gq8", [128, 64], F32)
    hT = sb("hT", [128, 8, TG], BF16)
    mixT = sb("mixT", [128, 16, TG], BF16)
    khT = sb("khT", [128, LMAX], BF16)
    vtok = sb("vtok", [128, NKB, 128], BF16)
    kiT = sb("kiT", [128, LMAX], BF16)
    qhT = sb("qhT", [128, 4, TG], BF16)
    qiT = sb("qiT", [128, 2, TG], BF16)
    gaT = sb("gaT", [128, 4, TG], BF16)
    lhist = sb("lhist", [128, 4, 3], F32)
    shist = sb("shist", [128, 12, 3], F32)
    lh = sb("lh", [128, 4], F32)
    xbcT = sb("xbcT", [128, 12, TG], BF16)
    sz = sb("sz", [128, 2, 1024], BF16)
    hst = sb("hst", [128, 1024], F32)
    hsb = sb("hsb", [128, 1024], BF16)
    sm = sb("sm", [128, 256], F32)
    smb = sb("smb", [128, 2, 80], F32)
    kiw = sb("kiw", [128, 2, 68], F32)
    dtt = sb("dtt", [128, 2, 16], F32)
    dAt = sb("dAt", [128, 2, 16], F32)
    A32 = sb("A32", [128, 3328], F32)
    AB = sb("AB", [128, 4608], BF16)
    pA = ps("pA", [128, 512], F32)
    pB = ps("pB", [128, 512], F32)
    pC = ps("pC", [128, 512], F32)
    pD = ps("pD", [128, 512], F32)
    pE = ps("pE", [128, 512], F32)
    pF = ps("pF", [128, 512], F32)
    pG = ps("pG", [128, 512], F32)
    pH = ps("pH", [128, 512], F32)

    act, dve, pe, pool, dma = pg.act, pg.dve, pg.pe, pg.pool, pg.dma

    def A_(fn, out, ins):
        return act(fn, [out] if not isinstance(out, list) else out, ins)

    def acopy(out, in_):
        act(lambda e: e.activation(out=out, in_=in_, func=AF.Copy), [out], [in_])

    def afunc(out, in_, func, bias=None, scale=None, accum=None):
        kw = {}
        reads = [in_]
        outs = [out]
        if bias is not None:
            kw["bias"] = bias
            if not isinstance(bias, float):
                reads.append(bias)
        if scale is not None:
            kw["scale"] = scale
            if not isinstance(scale, float):
                reads.append(scale)
        if accum is not None:
            kw["accum_out"] = accum
            outs.append(accum)
        act(lambda e: e.activation(out=out, in_=in_, func=func, **kw), outs, reads)

    def vtt(out, a, b, op):
        dve(lambda e: e.tensor_tensor(out=out, in0=a, in1=b, op=op), [out], [a, b])

    def vts(out, a, s1, op0, s2=None, op1=None, accum=None):
        reads = [a] + [s for s in (s1, s2) if s is not None and not isinstance(s, float)]
        outs = [out] + ([accum] if accum is not None else [])
        kw = {}
        if op1 is not None:
            kw["op1"] = op1
        if accum is not None:
            kw["accum_out"] = accum
        dve(lambda e: e.tensor_scalar(out=out, in0=a, scalar1=s1, scalar2=s2, op0=op0, **kw), outs, reads)

    def vstt(out, a, s, b, op0, op1):
        reads = [a, b] + ([s] if not isinstance(s, float) else [])
        dve(lambda e: e.scalar_tensor_tensor(out=out, in0=a, scalar=s, in1=b, op0=op0, op1=op1), [out], reads)

    def vcopy(out, in_):
        dve(lambda e: e.tensor_copy(out=out, in_=in_), [out], [in_])

    def vscan(out, d0, d1, init):
        reads = [d0, d1] + ([init] if not isinstance(init, float) else [])
        dve(lambda e: e.tensor_tensor_scan(out=out, data0=d0, data1=d1, initial=init, op0=ALU.mult, op1=ALU.add), [out], reads)

    def vreduce(out, in_, op, absval=False):
        if absval:
            dve(lambda e: e.tensor_reduce(out=out, in_=in_, axis=AX.X, op=op, apply_absolute_value=True), [out], [in_])
        else:
            dve(lambda e: e.tensor_reduce(out=out, in_=in_, axis=AX.X, op=op), [out], [in_])

    def vmemset(ap, val):
        dve(lambda e: e.memset(ap, val), [ap], [])

    def ptt(out, a, b, op):
        pool(lambda e: e.tensor_tensor(out=out, in0=a, in1=b, op=op), [out], [a, b])

    def split3(dst, src32, tmpa, tmpb):
        vcopy(dst[:, 0, :], src32)
        vtt(tmpa, src32, dst[:, 0, :], ALU.subtract)
        vcopy(dst[:, 1, :], tmpa)
        vtt(tmpb, tmpa, dst[:, 1, :], ALU.subtract)
        vcopy(dst[:, 2, :], tmpb)

    def rstd_pow(out, ss, n, tmp):
        afunc(tmp, ss, AF.Ln, scale=1.0 / n, bias=EPS)
        afunc(out, tmp, AF.Exp, scale=-0.5)

    def sigm(buf, src, scale=-1.0, bias=None):
        afunc(buf, src, AF.Exp, scale=scale, bias=bias)
        afunc(buf, buf, AF.Ln, bias=1.0)
        afunc(buf, buf, AF.Exp, scale=-1.0)

    def vrecip(out, in_):
        dve(lambda e: e.reciprocal(out=out, in_=in_), [out], [in_])

    def mm(out, lhsT, rhs, start, stop=True):
        pe(lambda e: e.matmul(out, lhsT=lhsT, rhs=rhs, start=start, stop=stop), [out], [lhsT, rhs])

    def tr(out, in_, idt):
        pe(lambda e: e.transpose(out, in_, idt), [out], [in_, idt])

    def bc(ap, axis, n):
        a = ap.unsqueeze(axis)
        shp = list(a.shape)
        shp[axis] = n
        return a.broadcast_to(shp)

    try:
      _body(locals())
    except _Stop:
      pass
    pg.finish()
    return nc


def _body(env):
    globals().update({k: v for k, v in env.items() if not k.startswith("__")})
    dma("sp", cst[:], I["cst"][:, 0:768])
    dma("sp", A32[0:32, 0:384], I["cst"][0:32, 768:1152])
    dma("sp", rbt[:], I["rb"])
    vcopy(cbf[:, 0, :], cst[:, 0:128])
    vcopy(cbf[:, 1, :], cst[:, 384:512])
    vcopy(cbf[:, 2, :], cst[:, 512:640])
    vmemset(cbf[:, 3, :], 1.0)
    vcopy(cbf[:, 4, :], cst[:, 128:256])
    vcopy(cbf[:, 5, :], cst[:, 256:384])
    mm(pA[0:8, 0:384], rbt[:, :], A32[0:32, 0:384], True)
    vcopy(vecs[:], pA[0:8, 0:384])
    vcopy(rbt[0:8, 0:1], vecs[:, 0:1])
    vts(vecs[:], vecs[:], rbt[0:8, 0:1], ALU.subtract)
    dma("sp", vecd, vecs[:])
    hktmp = A32[:, 0:1024].rearrange("p (h s) -> p h s", h=8)
    for k in range(NBIS + 1):
        vmemset(p2row[:, k:k + 1], 2.0 ** -k)
    for si, (TTq, dl) in enumerate([(128, 0), (128, -128)]):
        base = dl - TTq + 256
        src = bass.AP(tensor=vecd.tensor, offset=base, ap=[[1, TTq], [384, 8], [1, 128]])
        dma("sp", hktmp[:TTq], src)
        vcopy(hk[:TTq, si, :, :], hktmp[:TTq])

    CK_OFF[0] = 0
    chk(1)
    seqs = []
    for q in range(NSP):
        seqs.append(dict(kind="p", idx=q, T=TP, P=0))
    if SAMPLE:
        seqs.append(dict(kind="s", idx=0, T=TS, P=PS))

    for l in range(DEPTH):
        for kc in range(8):
            for hf in range(2):
                c0, c1_ = hf * 2602, (hf + 1) * 2602
                dma("pool", win[:, kc, c0:c1_], I["w_in"][l, kc * 128:(kc + 1) * 128, c0:c1_])
        for ec in range(16):
            dma("pool", wout[:, ec, :], I["w_out"][l, ec * 128:(ec + 1) * 128, :])
        dma("sp", ppt[:], I["pp"][l])
        dma("sp", pbt[:], I["pb"][l])
        dma("pool", wabd[:].rearrange("p a b -> p (a b)"), I["wabd"][l])
        afunc(drv[:, 20:24], ppt[:, PP_LAM:PP_LAM + 4], AF.Exp, scale=-1.0)
        afunc(drv[:, 24:28], drv[:, 20:24], AF.Ln, bias=1.0)
        vts(c1, drv[:, 24:28], -8.0, ALU.mult)
        afunc(drv[:, 28:44], pbt[:, PB_ALOG:PB_ALOG + 16], AF.Exp)
        vts(aneg, drv[:, 28:44], -1.0, ALU.mult)
        vts(gq8[:], pbt[:, PB_GQ:PB_GQ + 64], 0.125, ALU.mult)
        vts(drv[:, 48:52], ppt[:, PP_LBA:PP_LBA + 4], -1.0, ALU.mult)
        vts(drv[:, 52:56], ppt[:, PP_LBX:PP_LBX + 4], -1.0, ALU.mult)
        vcopy(rb15row[0:1, :, :], bc(pbt[0:1, PB_RB15:PB_RB15 + 8], 2, 128))

        chk(2)
        for sq in seqs:
            T, P0 = sq["T"], sq["P"]
            isS = sq["kind"] == "s"
            qi_ = sq["idx"]
            L = P0 + T
            ktop = min(256, L // 4)
            if isS:
                xin = I["xs"] if l == 0 else xmid_s
                xout = O["ys"] if l == DEPTH - 1 else xmid_s
                o_ak, o_av, o_ik = O["aks"][l], O["avs"][l], O["iks"][l]
                o_lc, o_lh, o_sc, o_sh = O["lcs"][l], O["lhs"][l], O["scs"][l], O["shs"][l]
            else:
                xin = I["xp"][qi_] if l == 0 else xmid_p[qi_]
                xout = O["yp"][qi_] if l == DEPTH - 1 else xmid_p[qi_]
                o_ak, o_av, o_ik = O["akp"][l, qi_], O["avp"][l, qi_], O["ikp"][l, qi_]
                o_lc, o_lh, o_sc, o_sh = O["lcp"][l, qi_], O["lhp"][l, qi_], O["scp"][l, qi_], O["shp"][l, qi_]
            if isS:
                for g in range(4):
                    dma("sp", lhist[:, g, :], I["slc"][l].rearrange("j (g p) -> p g j", p=128)[:, g, :], allow_slow_non_contiguous=True)
                for g in range(12):
                    dma("sp", shist[:, g, :], I["ssc"][l].rearrange("j (g p) -> p g j", p=128)[:, g, :], allow_slow_non_contiguous=True)
                dma("sp", lh[:], I["slh"][l].rearrange("(g p) -> p g", p=128), allow_slow_non_contiguous=True)
                stt = A32[:, 0:1024].rearrange("p (c n) -> p c n", c=8)
                dma("sp", stt, I["ssh"][l].rearrange("(c p) n -> p c n", p=128))
                sp3 = AB[:, 0:3072].rearrange("p (k n) -> p k n", k=3)
                split3(sp3, A32[:, 0:1024], A32[:, 1024:2048], A32[:, 2048:3072])
                for c in range(8):
                    pbank = pA if c < 4 else pB
                    for k in range(3):
                        mm(pbank[:, (c % 4) * 128:(c % 4 + 1) * 128], sp3[:, k, c * 128:(c + 1) * 128], identb, c % 4 == 0 and k == 0, c % 4 == 3 and k == 2)
                vcopy(hst[:, 0:512], pA[:, :])
                vcopy(hst[:, 512:1024], pB[:, :])
                acopy(hsb[:, :], hst[:, :])
                nb = P0 // 128
                ckb = AB[:, 0:nb * 128].rearrange("p (c n) -> p c n", c=nb)
                kid = AB[:, nb * 128:2 * nb * 128].rearrange("p (c n) -> p c n", c=nb)
                dma("pool", ckb, I["ck"][l].rearrange("(c p) n -> p c n", p=128))
                dma("pool", vtok[:, 0:nb, :], I["cv"][l].rearrange("(c p) n -> p c n", p=128))
                dma("pool", kid[:, :, 0:64], I["cki"][l].rearrange("(c p) n -> p c n", p=128))
                dma("pool", kid[:, :, 64:128], I["cki"][l].rearrange("(c p) n -> p c n", p=128))
                for c in range(nb):
                    mm(pC[:, (c % 4) * 128:(c % 4 + 1) * 128], ckb[:, c, :], identb, c % 4 == 0, c % 4 == 3 or c == nb - 1)
                    mm(pD[:, (c % 4) * 128:(c % 4 + 1) * 128], kid[:, c, :], identb, c % 4 == 0, c % 4 == 3 or c == nb - 1)
                    if c % 4 == 3 or c == nb - 1:
                        c0 = (c // 4) * 4
                        n_ = c - c0 + 1
                        vcopy(khT[:, c0 * 128:(c0 + n_) * 128], pC[:, 0:n_ * 128])
                        acopy(kiT[:, c0 * 128:(c0 + n_) * 128], pD[:, 0:n_ * 128])
            else:
                vmemset(lhist[:], 0.0)
                vmemset(shist[:], 0.0)
                vmemset(lh[:], 0.0)
                vmemset(hst[:], 0.0)
                vmemset(hsb[:], 0.0)

            CK_OFF[0] = 10 if isS else 0
            chk(3)
            ngroups = (T + TG - 1) // TG
            for gi in range(ngroups):
                t0 = gi * TG
                NV = min(TG, T - t0)
                NT = ((NV + 127) // 128) * 128
                TT = 128
                ntile = NT // TT
                for ti in range(ntile):
                    r0 = t0 + ti * TT
                    TV = min(TT, NV - ti * TT)
                    xt32 = A32[:TT, 0:1024]
                    if TV < TT:
                        vmemset(A32[TV:TT, 0:1024], 0.0)
                    dma("sp", A32[:TV, 0:1024], xin[r0:r0 + TV, :])
                    junk = AB[:TT, 0:1024]
                    xn = AB[:TT, 1024:2048]
                    ssq = sm[:TT, ti:ti + 1]
                    afunc(junk, xt32, AF.Square, accum=ssq)
                    rstd_pow(sm[:TT, 4 + ti:5 + ti], ssq, D_MODEL, sm[:TT, 2 + ti:3 + ti])
                    vts(xn, xt32, sm[:TT, 4 + ti:5 + ti], ALU.mult)
                    for kc in range(8):
                        pb_ = pA if kc < 4 else pB
                        mm(pb_[:, (kc % 4) * TT:(kc % 4 + 1) * TT], xn[:, kc * 128:(kc + 1) * 128], identb[:TT, :TT], kc % 4 == 0, kc % 4 == 3)
                    for hf in range(2):
                        pb_ = pA if hf == 0 else pB
                        vtt(hT[:, hf * 4:(hf + 1) * 4, ti * TT:(ti + 1) * TT], pb_[:, 0:4 * TT].rearrange("p (c t) -> p c t", c=4),
                            bc(ppt[:, PP_GN + hf * 4:PP_GN + hf * 4 + 4], 2, TT), ALU.mult)

                chk(4)
                def proj_fm(pbank, c0, M, prow=0):
                    for kc in range(8):
                        mm(pbank[prow:prow + M, :NT], win[:, kc, c0:c0 + M], hT[:, kc, :NT], kc == 0, kc == 7)

                def conv(pbank, hist, g, wcol, bcol, out32, roff=0, ppt=ppt):
                    raw = A32[:, roff:roff + 3 + NT]
                    vcopy(raw[:, 0:3], hist[:, g, :])
                    acopy(raw[:, 3:3 + NT], pbank[:, :NT])
                    vcopy(hist[:, g, :], raw[:, NV:NV + 3])
                    vts(out32, raw[:, 0:NT], ppt[:, wcol:wcol + 1], ALU.mult, ppt[:, bcol:bcol + 1], ALU.add)
                    for j in range(1, 4):
                        vstt(out32, raw[:, j:j + NT], ppt[:, wcol + j:wcol + j + 1], out32, ALU.mult, ALU.add)
                    return raw

                def gen_lru(g):
                    lbase = (g % 2) * 1664
                    f = lambda k: A32[:, lbase + 260 + k * 256:lbase + 260 + k * 256 + NT]
                    gC, gD = (pC, pD) if g % 2 == 0 else (pG, pH)
                    xc, rr, ii, aa, s_ = [f(k) for k in range(5)]
                    gx, bb, hseq, sg = ii, s_, xc, rr
                    xcb = AB[:, 2048 + (g % 2) * 256:2048 + (g % 2) * 256 + NT]
                    pb_ = pA if g % 2 == 0 else pB
                    proj_fm(pb_, C_XL + g * 128, 128)
                    raw = conv(pb_, lhist, g, PP_LCW + g * 4, PP_LCB + g, xc, lbase)
                    if gi == ngroups - 1:
                        dma("sp", o_lc.rearrange("j (g p) -> p g j", p=128)[:, g, :], raw[:, NV:NV + 3], allow_slow_non_contiguous=True)
                    acopy(xcb, xc)
                    yield
                    mm(gC[:, :NT], wabd[:, g, :], xcb, True)
                    mm(gD[:, :NT], wabd[:, 4 + g, :], xcb, True)
                    sigm(rr, gC[:, :NT], -1.0, drv[:, 48 + g:49 + g])
                    yield
                    sigm(ii, gD[:, :NT], -1.0, drv[:, 52 + g:53 + g])
                    yield
                    afunc(aa, rr, AF.Exp, scale=c1[:, g:g + 1])
                    afunc(s_, aa, AF.Square)
                    afunc(s_, s_, AF.Ln, scale=-1.0, bias=1.0)
                    afunc(s_, s_, AF.Exp, scale=0.5)
                    yield
                    vtt(gx, ii, xc, ALU.mult)
                    vtt(bb, s_, gx, ALU.mult)
                    vscan(hseq, aa, bb, lh[:, g:g + 1])
                    vcopy(lh[:, g:g + 1], hseq[:, NV - 1:NV])
                    yield
                    pg_ = pE if g % 2 == 0 else pF
                    proj_fm(pg_, C_GL + g * 128, 128)
                    sigm(sg, pg_[:, :NT])
                    yield
                    vtt(sg, sg, pg_[:, :NT], ALU.mult)
                    vtt(mixT[:, g, :NT], hseq, sg, ALU.mult)
                    yield

                interleave(gen_lru(0), gen_lru(1))
                interleave(gen_lru(2), gen_lru(3))
                if gi == ngroups - 1:
                    dma("sp", o_lh.rearrange("(g p) -> p g", p=128), lh[:], allow_slow_non_contiguous=True)

                chk(5)
                for g in range(4):
                    pb_ = pA if g % 2 == 0 else pB
                    proj_fm(pb_, C_GA + g * 128, 128)
                    gth = A32[:, 512 + (g % 2) * 256:512 + (g % 2) * 256 + NT]
                    sigm(gth, pb_[:, :NT])
                    vtt(gaT[:, g, :NT], gth, pb_[:, :NT], ALU.mult)
                for g in range(2):
                    pb_ = pE if g % 2 == 0 else pF
                    proj_fm(pb_, C_QI + g * 128, 128)
                    vcopy(qiT[:, g, :NT], pb_[:, :NT])
                proj_fm(pA, C_KI, 64, 0)
                proj_fm(pA, C_KI, 64, 64)
                acopy(kiT[:, P0 + t0:P0 + t0 + NT], pA[:, :NT])

                def gen_conv(g):
                    pb_ = pE if g % 2 == 0 else pF
                    proj_fm(pb_, C_XBC + g * 128, 128)
                    roff = (g % 2) * 1280
                    acc = A32[:, roff + 512:roff + 512 + NT]
                    raw = conv(pb_, shist, g, PP_SCW + g * 4, PP_SCB + g, acc, roff)
                    if gi == ngroups - 1:
                        dma("sp", o_sc.rearrange("j (g p) -> p g j", p=128)[:, g, :], raw[:, NV:NV + 3], allow_slow_non_contiguous=True)
                    yield
                    cth = A32[:, roff + 768:roff + 768 + NT]
                    sigm(cth, acc)
                    yield
                    vtt(xbcT[:, g, :NT], cth, acc, ALU.mult)
                    yield

                for g2 in range(0, 12, 2):
                    interleave(gen_conv(g2), gen_conv(g2 + 1))

                chk(6)
                def gen_p2b(ti):
                    tsl = slice(ti * TT, (ti + 1) * TT)
                    r0 = t0 + ti * TT
                    TV = min(TT, NV - ti * TT)
                    kb = (P0 + r0) // 128
                    koff = (P0 + r0) % 128
                    qC, qD, qE, qF = (pC, pD, pE, pF) if ti % 2 == 0 else (pA, pB, pG, pH)
                    tb = (ti % 2) * 1536
                    for kc in range(8):
                        mm(qC[:TT, 0:512], hT[:, kc, tsl], win[:, kc, C_Q:C_Q + 512], kc == 0, kc == 7)
                    for (o0, c0, n_) in ((0, C_K, 256), (256, C_KI, 68), (324, C_DT, 16)):
                        for kc in range(8):
                            mm(qD[:TT, o0:o0 + n_], hT[:, kc, tsl], win[:, kc, c0:c0 + n_], kc == 0, kc == 7)
                    yield
                    for hf in range(2):
                        pz = qE if hf == 0 else qF
                        for kc in range(8):
                            mm(pz[:TT, :], hT[:, kc, tsl], win[:, kc, C_Z + hf * 512:C_Z + (hf + 1) * 512], kc == 0, kc == 7)
                        zth = A32[:TT, tb + 512 + hf * 512:tb + 1024 + hf * 512]
                        sigm(zth, pz[:TT, :])
                        vtt(sz[:TT, ti, hf * 512:(hf + 1) * 512], zth, pz[:TT, :], ALU.mult)
                        yield
                    sqt = A32[:TT, tb + 512:tb + 1024]
                    qn = A32[:TT, tb + 1024:tb + 1536]
                    kv32 = A32[:TT, tb + 1536:tb + 1792]
                    qhat = AB[:TT, 2304 + (ti % 2) * 640:2816 + (ti % 2) * 640]
                    khat = AB[:TT, 2816 + (ti % 2) * 640:2944 + (ti % 2) * 640]
                    afunc(sqt, qC[:TT, :], AF.Square)
                    vreduce(smb[:TT, ti % 2, 8:16], sqt.rearrange("p (h d) -> p h d", h=8), ALU.add)
                    rstd_pow(smb[:TT, ti % 2, 24:32], smb[:TT, ti % 2, 8:16], 64, smb[:TT, ti % 2, 16:24])
                    vtt(qn.rearrange("p (h d) -> p h d", h=8), qC[:TT, :].rearrange("p (h d) -> p h d", h=8), bc(smb[:TT, ti % 2, 24:32], 2, 64), ALU.mult)
                    vtt(qhat.rearrange("p (m g d) -> p g m d", m=4, g=2), qn.rearrange("p (g m d) -> p g m d", g=2, m=4),
                        bc(bc(gq8[:TT, :], 1, 4), 1, 2), ALU.mult)
                    yield
                    afunc(sqt[:, 0:128], qD[:TT, 0:128], AF.Square)
                    vreduce(smb[:TT, ti % 2, 32:34], sqt[:, 0:128].rearrange("p (h d) -> p h d", h=2), ALU.add)
                    rstd_pow(smb[:TT, ti % 2, 36:38], smb[:TT, ti % 2, 32:34], 64, smb[:TT, ti % 2, 34:36])
                    vtt(qn[:, 0:128].rearrange("p (h d) -> p h d", h=2), qD[:TT, 0:128].rearrange("p (h d) -> p h d", h=2), bc(smb[:TT, ti % 2, 36:38], 2, 64), ALU.mult)
                    vtt(kv32[:, 0:128].rearrange("p (h d) -> p h d", h=2), qn[:, 0:128].rearrange("p (h d) -> p h d", h=2),
                        bc(pbt[:TT, PB_GK:PB_GK + 64], 1, 2), ALU.mult)
                    acopy(khat, kv32[:, 0:128])
                    acopy(kv32[:, 128:256], qD[:TT, 128:256])
                    acopy(vtok[koff:koff + TT, kb, :], qD[:TT, 128:256])
                    dma("sp", o_ak[r0:r0 + TV, :], kv32[:TV, 0:128])
                    dma("sp", o_av[r0:r0 + TV, :], kv32[:TV, 128:256])
                    vcopy(kiw[:TT, ti, :], qD[:TT, 256:324])
                    dma("sp", o_ik[r0:r0 + TV, :], kiw[:TV, ti, 0:64])
                    yield
                    vtt(smb[:TT, ti % 2, 40:56], qD[:TT, 324:340], pbt[:TT, PB_DTB:PB_DTB + 16], ALU.add)
                    afunc(smb[:TT, ti % 2, 56:72], smb[:TT, ti % 2, 40:56], AF.Exp)
                    afunc(dtt[:TT, ti, :], smb[:TT, ti % 2, 56:72], AF.Ln, bias=1.0)
                    if TV < TT:
                        vmemset(dtt[TV:TT, ti, :], 0.0)
                    vtt(dAt[:TT, ti, :], dtt[:TT, ti, :], aneg[:TT, :], ALU.mult)
                    yield
                    mm(qE[:, 0:TT], khat, identb[:TT, :TT], True)
                    vcopy(khT[:, P0 + r0:P0 + r0 + TT], qE[:, 0:TT])
                    for m in range(4):
                        mm(qF[:, m * TT:(m + 1) * TT], qhat[:, m * 128:(m + 1) * 128], identb[:TT, :TT], m == 0, m == 3)
                    vcopy(qhT[:, :, tsl], qF[:, 0:4 * TT].rearrange("p (c t) -> p c t", c=4))
                    yield

                if ntile == 2:
                    interleave(gen_p2b(0), gen_p2b(1))
                else:
                    drain(gen_p2b(0))

                chk(7)
                for ti in range(ntile):
                    tsl = slice(ti * TT, (ti + 1) * TT)
                    if ti == 1:
                        chk(7.9)
                    dA = dAt[:TT, ti, :]
                    dtv = dtt[:TT, ti, :]
                    split3(dAs[:TT], dA, sm[:TT, 176:192], sm[:TT, 192:208])
                    for k in range(3):
                        mm(pF[:TT, 0:16], trib[:TT, :TT], dAs[:TT, k, :], k == 0, k == 2)
                    for k in range(3):
                        mm(pF[:TT, 16:32], astrb[:TT, :TT], dAs[:TT, k, :], k == 0, k == 2)
                    for k in range(3):
                        mm(pF[:, 32:48], onesb[:TT, :], dAs[:TT, k, :], k == 0, k == 2)
                    ecum = sm[:TT, 80:96]
                    toend = sm[:TT, 96:112]
                    dec = sm[:, 112:128]
                    afunc(sm[:TT, 80:112], pF[:TT, 0:32], AF.Exp)
                    afunc(dec, pF[:, 32:48], AF.Exp)
                    chk(7.1)
                    for c in range(8):
                        pb_ = pC if c < 4 else pD
                        mm(pb_[:TT, (c % 4) * 128:(c % 4 + 1) * 128], xbcT[:, c, tsl], identb, c % 4 == 0, c % 4 == 3)
                    xt_ = AB[:TT, 0:1024]
                    x2_ = AB[:TT, 1024:2048]
                    xD = A32[:TT, 0:1024]
                    for hf in range(2):
                        pTv = (pC if hf == 0 else pD)[:TT, :].rearrange("p (h d) -> p h d", h=8)
                        hs_ = slice(hf * 512, (hf + 1) * 512)
                        vtt(xt_[:, hs_].rearrange("p (h d) -> p h d", h=8), pTv, bc(dtv[:, hf * 8:(hf + 1) * 8], 2, 64), ALU.mult)
                        vtt(xD[:, hs_].rearrange("p (h d) -> p h d", h=8), pTv, bc(pbt[:TT, PB_D + hf * 8:PB_D + hf * 8 + 8], 2, 64), ALU.mult)
                    vtt(x2_.rearrange("p (h d) -> p h d", h=16), xt_.rearrange("p (h d) -> p h d", h=16), bc(toend, 2, 64), ALU.mult)
                    def gen_ssd(g, ti=ti, tsl=tsl, xt_=xt_, x2_=x2_, xD=xD, ecum=ecum, dec=dec):
                        eb = 2048 + g * 1280
                        E = AB[:TT, eb:eb + 8 * TT].rearrange("p (h l) -> p h l", h=8)
                        WT = E
                        G = AB[:TT, eb + 1024:eb + 1024 + TT]
                        bmtok = AB[:TT, eb + 1152:eb + 1280]
                        ycb = AB[:TT, eb:eb + 512]
                        t1 = A32[:TT, 1024 + g * 1024:1536 + g * 1024]
                        yz = A32[:TT, 1536 + g * 1024:2048 + g * 1024]
                        dbk = (pA, pB) if g == 0 else (pG, pH)
                        cbk = pC if g == 0 else pF
                        ydk = pD if g == 0 else pG
                        yok = pE if g == 0 else pH
                        ytk = dbk[0]
                        hpb = 512 // TT
                        for bk in range(8 // hpb):
                            pb_ = dbk[bk % 2]
                            for k in range(2):
                                Rk = RB[:TT, (g * 2 + bk + k) % 2, 0:hpb * TT]
                                h0_ = g * 8 + bk * hpb
                                vtt(Rk.rearrange("p (h l) -> p h l", h=hpb), bc(dAs[:TT, k, h0_:h0_ + hpb], 2, TT), bc(trib[:TT, :TT], 1, hpb), ALU.mult)
                                mm(pb_[:TT, 0:hpb * TT], astrb[:TT, :TT], Rk, k == 0, k == 1)
                            afunc(E[:, bk * hpb:(bk + 1) * hpb, :], pb_[:TT, 0:hpb * TT].rearrange("p (h l) -> p h l", h=hpb), AF.Exp)
                            yield
                        mm(cbk[:TT, :TT], xbcT[:, 8 + g, tsl], xbcT[:, 10 + g, tsl], True)
                        vtt(G, cbk[:TT, :TT], tri[:TT, :TT], ALU.mult)
                        vtt(WT, E, bc(G, 1, 8), ALU.mult)
                        yield
                        for hh in range(8):
                            h_ = g * 8 + hh
                            mm(ydk[:TT, hh * 64:(hh + 1) * 64], WT[:, hh, :], xt_[:, h_ * 64:(h_ + 1) * 64], hh == 0, hh == 7)
                        mm(yok[:TT, :], xbcT[:, 10 + g, tsl], hsb[:, g * 512:(g + 1) * 512], True)
                        yield
                        vtt(t1.rearrange("p (h d) -> p h d", h=8), yok[:TT, :].rearrange("p (h d) -> p h d", h=8), bc(ecum[:, g * 8:(g + 1) * 8], 2, 64), ALU.mult)
                        vtt(t1, t1, ydk[:TT, :], ALU.add)
                        vtt(t1, t1, xD[:, g * 512:(g + 1) * 512], ALU.add)
                        vtt(yz, t1, sz[:TT, ti, g * 512:(g + 1) * 512], ALU.mult)
                        yield
                        afunc(t1, yz, AF.Square, accum=sm[:TT, 128 + g:129 + g])
                        rstd_pow(sm[:TT, 132 + g:133 + g], sm[:TT, 128 + g:129 + g], 512, sm[:TT, 130 + g:131 + g])
                        vts(t1, yz, sm[:TT, 132 + g:133 + g], ALU.mult)
                        vtt(ycb, t1, pbt[:TT, PB_GSSD + g * 512:PB_GSSD + (g + 1) * 512], ALU.mult)
                        yield
                        for c in range(4):
                            mm(ytk[:, c * TT:(c + 1) * TT], ycb[:, c * 128:(c + 1) * 128], identb[:TT, :TT], c == 0, c == 3)
                        vcopy(mixT[:, 8 + g * 4:12 + g * 4, tsl], ytk[:, 0:4 * TT].rearrange("p (c t) -> p c t", c=4))
                        yield
                        mm(cbk[:TT, 0:128], xbcT[:, 8 + g, tsl], identb, True)
                        acopy(bmtok, cbk[:TT, 0:128])
                        mm(cbk[:, :], bmtok, x2_[:, g * 512:(g + 1) * 512], True)
                        hv = hst[:, g * 512:(g + 1) * 512]
                        vtt(hv.rearrange("p (h d) -> p h d", h=8), hv.rearrange("p (h d) -> p h d", h=8), bc(dec[:, g * 8:(g + 1) * 8], 2, 64), ALU.mult)
                        vtt(hv, hv, cbk[:, :], ALU.add)
                        acopy(hsb[:, g * 512:(g + 1) * 512], hv)
                        yield

                    interleave(gen_ssd(0), gen_ssd(1))
                if gi == ngroups - 1:
                    stt = A32[:, 0:1024].rearrange("p (c n) -> p c n", c=8)
                    sp3 = AB[:, 0:3072].rearrange("p (k n) -> p k n", k=3)
                    split3(sp3, hst[:, :], A32[:, 1024:2048], A32[:, 2048:3072])
                    for c in range(8):
                        pbank = pA if c < 4 else pB
                        for k in range(3):
                            mm(pbank[:, (c % 4) * 128:(c % 4 + 1) * 128], sp3[:, k, c * 128:(c + 1) * 128], identb, c % 4 == 0 and k == 0, c % 4 == 3 and k == 2)
                    vcopy(stt[:, 0:4, :], pA[:, :].rearrange("p (c n) -> p c n", c=4))
                    vcopy(stt[:, 4:8, :], pB[:, :].rearrange("p (c n) -> p c n", c=4))
                    dma("sp", o_sh.rearrange("(c p) n -> p c n", p=128), stt)

                chk(8)
                def p4_vars(ti):
                    q0 = P0 + t0 + ti * TT
                    Lk = q0 + TT
                    return (slice(ti * TT, (ti + 1) * TT), q0, Lk, (Lk + 127) // 128, A32[:TT, 0:Lk], AB[:TT, 0:Lk], JK[:TT, 0:Lk])

                def gen_topk(ti):
                    tsl, q0, Lk, nkb, score, negm, junkb = p4_vars(ti)
                    for c0 in range(0, Lk, 512):
                        c1_ = min(Lk, c0 + 512)
                        for hh in range(4):
                            pb_ = pA if hh % 2 == 0 else pB
                            pr = (hh % 2) * 64
                            mm(pb_[:TT, 0:c1_ - c0], qiT[pr:pr + 64, hh // 2, tsl], kiT[pr:pr + 64, c0:c1_], True)
                            rl = A32[:TT, 2048 + (hh % 2) * 512:2560 + (hh % 2) * 512]
                            afunc(rl[:, 0:c1_ - c0], pb_[:TT, 0:c1_ - c0], AF.Relu)
                            if hh == 0:
                                vts(score[:, c0:c1_], rl[:, 0:c1_ - c0], kiw[:TT, ti, 64:65], ALU.mult)
                            else:
                                vstt(score[:, c0:c1_], rl[:, 0:c1_ - c0], kiw[:TT, ti, 64 + hh:65 + hh], score[:, c0:c1_], ALU.mult, ALU.add)
                            yield
                    lo = sm[:TT, 140:141]
                    if Lk > ktop:
                        amax = sm[:TT, 141:142]
                        hw = sm[:TT, 144:144 + NBIS + 1]
                        vreduce(amax, score, ALU.max, absval=True)
                        vtt(score[:, Lk - 128:Lk], score[:, Lk - 128:Lk], dmask[:, :], ALU.add)
                        vts(amax, amax, 1.001, ALU.mult, 1e-3, ALU.add)
                        vts(hw, p2row[:TT, :], amax, ALU.mult)
                        mid = sm[:TT, 142:143]
                        cnt = sm[:TT, 143:144]
                        ind = sm[:TT, 170:171]
                        vmemset(mid, 0.0)
                        yield
                        for it in range(NBIS):
                            vts(junkb, score, mid, ALU.is_gt, 0.0, ALU.add, accum=cnt)
                            vstt(ind, cnt, float(ktop) - 0.5, hw[:, it:it + 1], ALU.is_gt, ALU.mult)
                            vstt(mid, ind, hw[:, it + 1:it + 2], mid, ALU.subtract, ALU.add)
                            yield
                        vtt(lo, mid, hw[:, NBIS:NBIS + 1], ALU.subtract)
                    else:
                        vtt(score[:, Lk - 128:Lk], score[:, Lk - 128:Lk], dmask[:, :], ALU.add)
                        vmemset(lo, NEG / 2)

                def do_mask(ti):
                    tsl, q0, Lk, nkb, score, negm, junkb = p4_vars(ti)
                    lo = sm[:TT, 140:141]
                    if Lk > ktop:
                        c0 = sm[:TT, 171:172]
                        mrem = sm[:TT, 172:173]
                        vts(junkb, score, 0.0, ALU.is_gt, 0.0, ALU.add, accum=c0)
                        vts(mrem, c0, -1.0, ALU.mult, float(ktop), ALU.add)
                        vts(negm, score, 0.0, ALU.is_equal)
                        vscan(negm, onesb[:TT, 0:1].broadcast_to([TT, Lk]), negm, 0.0)
                        vts(negm, negm, mrem, ALU.is_gt)
                        vstt(negm, score, 0.0, negm, ALU.is_equal, ALU.mult)
                        vstt(negm, score, lo, negm, ALU.is_le, ALU.max)
                        vts(negm, negm, NEG, ALU.mult)
                    else:
                        vts(negm, score, lo, ALU.is_le, NEG, ALU.mult)

                def gen_attn(ti):
                    tsl, q0, Lk, nkb, score, negm, junkb = p4_vars(ti)
                    for g in range(2):
                        for jb in range(nkb):
                            S = min(128, Lk - jb * 128)
                            pl = pC if jb % 2 == 0 else pD
                            plv = pl[:S, 0:4 * TT].rearrange("p (m t) -> p m t", m=4)
                            mm(plv, negm[:, jb * 128:jb * 128 + S], bc(identb[:TT, :TT], 1, 4), True, False)
                            dl = jb * 128 - q0
                            mm(plv, khT[g * 64:(g + 1) * 64, jb * 128:jb * 128 + S], qhT[g * 64:(g + 1) * 64, :, tsl], False, dl <= -256)
                            if dl > -256:
                                si = 0 if dl == 0 else 1
                                for m in range(4):
                                    mm(plv[:, m, :], hk[:TT, si, g * 4 + m, :S], j128b[:TT, :TT], False, m == 3)
                            ET = AB[:S, 2048 + (jb % 2) * 512:2048 + (jb % 2) * 512 + 4 * TT].rearrange("p (m t) -> p m t", m=4)
                            afunc(ET, plv, AF.Exp)
                            pov = pE[:, 0:2 * TT].rearrange("p (a t) -> p a t", a=2)
                            pdv = pF[:, 0:2 * TT].rearrange("p (a t) -> p a t", a=2)
                            for par in range(2):
                                mm(pov[par * 64:(par + 1) * 64], vtok[:S, jb, g * 64:(g + 1) * 64], ET[:, par::2, :], jb == 0, jb == nkb - 1)
                                mm(pdv[par * 64:(par + 1) * 64], onesb[:S, 0:64], ET[:, par::2, :], jb == 0, jb == nkb - 1)
                            yield
                        rden = A32[:, 3072:3072 + 2 * TT]
                        vrecip(rden, pF[:, 0:2 * TT])
                        vtt(rden, pE[:, 0:2 * TT], rden, ALU.mult)
                        vtt(mixT[:, 4 + 2 * g:6 + 2 * g, tsl], rden.rearrange("p (a t) -> p a t", a=2), gaT[:, 2 * g:2 * g + 2, tsl], ALU.mult)
                        yield

                def gen_p5(ti):
                    tsl = slice(ti * TT, (ti + 1) * TT)
                    r0 = t0 + ti * TT
                    TV = min(TT, NV - ti * TT)
                    base = (ti % 2) * 1024
                    xr = A32[:TT, base:base + 1024]
                    if TV < TT:
                        vmemset(A32[TV:TT, base:base + 1024], 0.0)
                    dma("sp", A32[:TV, base:base + 1024], xin[r0:r0 + TV, :])
                    yield
                    for hf in range(2):
                        pb_ = pA if hf == 0 else pB
                        for ec in range(16):
                            mm(pb_[:TT, :], mixT[:, ec, tsl], wout[:, ec, hf * 512:(hf + 1) * 512], ec == 0, ec == 15)
                            if ec % 4 == 3:
                                yield
                        vtt(xr[:, hf * 512:(hf + 1) * 512], pb_[:TT, :], xr[:, hf * 512:(hf + 1) * 512], ALU.add)
                        yield
                    dma("sp", xout[r0:r0 + TV, :], xr[:TV])
                    yield

                drain(gen_topk(0))
                do_mask(0)
                if ntile == 2:
                    interleave(gen_attn(0), gen_topk(1))
                    do_mask(1)
                    interleave(gen_attn(1), gen_p5(0))
                else:
                    drain(gen_attn(0))

                chk(9)
                if DEBUG_MIX[0] and not isS and qi_ == 0 and l == 0:
                    dma("sp", O["dbg"][:, :, t0:t0 + NT].rearrange("c p t -> p c t"), mixT[:, :, :NT])
                drain(gen_p5(1 if ntile == 2 else 0))


def _t5_bucket_np(rel):
    import math
    import jax
    import jax.numpy as jnp
    with jax.default_device(jax.devices("cpu")[0]):
        return _t5_bucket_cpu(rel, math, jnp)


def _t5_bucket_cpu(rel, math, jnp):
    rel = jnp.asarray(rel, dtype=jnp.int32)
    half, max_exact = 16, 8
    n = jnp.abs(rel)
    large = max_exact + (jnp.log(jnp.maximum(n, 1).astype(jnp.float32) / max_exact) / math.log(128 / max_exact) * (half - max_exact)).astype(jnp.int32)
    large = jnp.minimum(large, half - 1)
    return np.asarray(jnp.where(rel > 0, half, 0) + jnp.where(n < max_exact, n, large))


def _consts():
    c = np.zeros((128, NCST), np.float32)
    i = np.arange(128)
    c[:, 0:128] = np.eye(128)
    c[:, 128:256] = (i[:, None] <= i[None, :])
    c[:, 256:384] = (i[:, None] > i[None, :])
    c[:, 384:512] = (i[:, None] == 127 - i[None, :])
    c[:64, 512:576] = (i[:64, None] == 63 - i[None, :64])
    c[:, 640:768] = np.where((i[None, :] // 64) > (i[:, None] // 64), NEG, 0.0)
    bk = _t5_bucket_np(np.arange(384) - 255)
    c[0:32, 768:1152] = (np.arange(32)[:, None] == bk[None, :])
    return c


_PROG_CACHE = {}


def _run(inputs, NCORES, NSP, TP, SAMPLE, PS, TS, DEPTH):
    key = (NSP, TP, SAMPLE, PS, TS, DEPTH)
    f32 = np.float32
    g = lambda k: np.ascontiguousarray(np.asarray(inputs[k], dtype=f32))
    w_in, w_out = g("w_in"), g("w_out")
    cst = _consts()
    pp = np.zeros((DEPTH, 128, NPP), f32)
    pb = np.zeros((DEPTH, 128, NPB), f32)
    wabd = np.zeros((DEPTH, 128, 2, 4, 128), f32)
    fm = lambda v, n: v.reshape(n, 128).T
    for l in range(DEPTH):
        pp[l, :, PP_GN:PP_GN + 8] = fm(g("norm_w")[l], 8)
        lcw = g("lru_conv_w")[l]
        pp[l, :, PP_LCW:PP_LCW + 16] = lcw.reshape(4, 4, 128).transpose(2, 1, 0).reshape(128, 16)
        pp[l, :, PP_LCB:PP_LCB + 4] = fm(g("lru_conv_b")[l], 4)
        pp[l, :, PP_LBA:PP_LBA + 4] = fm(g("lru_b_a")[l], 4)
        pp[l, :, PP_LBX:PP_LBX + 4] = fm(g("lru_b_x")[l], 4)
        pp[l, :, PP_LAM:PP_LAM + 4] = fm(g("lru_lambda")[l], 4)
        scw = g("ssd_conv_w")[l]
        pp[l, :, PP_SCW:PP_SCW + 48] = scw.reshape(4, 12, 128).transpose(2, 1, 0).reshape(128, 48)
        pp[l, :, PP_SCB:PP_SCB + 12] = fm(g("ssd_conv_b")[l], 12)
        pb[l, :, PB_GQ:PB_GQ + 64] = g("att_q_norm")[l][None, :]
        pb[l, :, PB_GK:PB_GK + 64] = g("att_k_norm")[l][None, :]
        pb[l, :, PB_DTB:PB_DTB + 16] = g("ssd_dt_bias")[l][None, :]
        pb[l, :, PB_ALOG:PB_ALOG + 16] = g("ssd_a_log")[l][None, :]
        pb[l, :, PB_D:PB_D + 16] = g("ssd_d")[l][None, :]
        pb[l, :, PB_GSSD:PB_GSSD + 1024] = g("ssd_norm")[l][None, :]
        pb[l, :, PB_RB15:PB_RB15 + 8] = g("rel_bias")[15][None, :]
        for a, nm in enumerate(("lru_w_a", "lru_w_x")):
            w = g(nm)[l]
            for gg in range(4):
                wabd[l, 0:64, a, gg, 0:64] = w[2 * gg]
                wabd[l, 64:128, a, gg, 64:128] = w[2 * gg + 1]
    wabd = wabd.reshape(DEPTH, 128, 1024)
    xp = g("x_prompt")
    in_maps = []
    for c in range(NCORES):
        m = {"xp": xp[c * NSP:(c + 1) * NSP], "w_in": w_in, "w_out": w_out, "pp": pp, "pb": pb, "wabd": wabd,
             "rb": g("rel_bias"), "cst": cst}
        if SAMPLE:
            m["xs"] = g("x_sample")[c]
            m["ck"] = g("cache_att_k")[:, c].reshape(DEPTH, PS, 128)
            m["cv"] = g("cache_att_v")[:, c].reshape(DEPTH, PS, 128)
            m["cki"] = g("cache_idx_k")[:, c]
            m["slc"] = g("state_lru_conv")[:, c]
            m["slh"] = g("state_lru_h")[:, c]
            m["ssc"] = g("state_ssd_conv")[:, c]
            m["ssh"] = g("state_ssd_h")[:, c].reshape(DEPTH, 1024, 128)
        in_maps.append({k: np.ascontiguousarray(v) for k, v in m.items()})
    if key not in _PROG_CACHE:
        _PROG_CACHE[key] = build_program(NSP=NSP, TP=TP, SAMPLE=SAMPLE, PS=PS, TS=TS, DEPTH=DEPTH)
    nc = _PROG_CACHE[key]
    res = run_bass_kernel_spmd(nc, in_maps, core_ids=list(range(NCORES)))
    R = res.results
    cat = lambda k, ax: np.concatenate([np.asarray(r[k]) for r in R], axis=ax)
    stk = lambda k, ax: np.stack([np.asarray(r[k]) for r in R], axis=ax)
    B = NCORES * NSP
    outs = [cat("yp", 0)]
    if SAMPLE:
        outs.append(stk("ys", 0))
    outs += [cat("akp", 1).reshape(DEPTH, B, TP, 2, 64), cat("avp", 1).reshape(DEPTH, B, TP, 2, 64), cat("ikp", 1),
             cat("lcp", 1), cat("lhp", 1), cat("scp", 1), cat("shp", 1).reshape(DEPTH, B, 16, 64, 128)]
    if SAMPLE:
        outs += [stk("aks", 1).reshape(DEPTH, NCORES, TS, 2, 64), stk("avs", 1).reshape(DEPTH, NCORES, TS, 2, 64), stk("iks", 1),
                 stk("lcs", 1), stk("lhs", 1), stk("scs", 1), stk("shs", 1).reshape(DEPTH, NCORES, 16, 64, 128)]
    return tuple(np.ascontiguousarray(o, dtype=np.float32) for o in outs)


def kernel(**inputs):
    return _run(inputs, 8, 2, 2048, True, 1024, 64, 2)
```

```python
import numpy as np
from contextlib import ExitStack
import concourse.bass as bass
import concourse.mybir as mybir
from concourse.bass_utils import run_bass_kernel_spmd

F32 = mybir.dt.float32
BF16 = mybir.dt.bfloat16
AF = mybir.ActivationFunctionType
ALU = mybir.AluOpType
AX = mybir.AxisListType


def _region(ap):
    t = ap.tensor
    pat = [(int(s), int(c)) for (s, c) in ap.ap]
    off = int(ap.offset)
    kind = type(t).__name__
    if kind.startswith("DRam"):
        lo = off
        ext = sum((c - 1) * abs(s) for s, c in pat)
        n = 1
        for s, c in pat:
            if s != 0:
                n *= c
        return (t.name, 0, 1, lo, lo + ext + 1, n == ext + 1)
    shp = [int(v) for v in t.shape]
    pstride = 1
    for v in shp[1:]:
        pstride *= v
    p0 = off // pstride
    lo = off % pstride
    npart = pat[0][1]
    ext = sum((c - 1) * abs(s) for s, c in pat[1:])
    n = 1
    for s, c in pat[1:]:
        if s != 0:
            n *= c
    if kind.startswith("PSum"):
        full = (lo == 0 and lo + ext + 1 == pstride and n == ext + 1)
        return (t.name, (p0 // 32) * 32, ((p0 + npart + 31) // 32) * 32, 0, pstride, full and p0 % 32 == 0 and (p0 + npart) % 32 == 0)
    return (t.name, p0, p0 + npart, lo, lo + ext + 1, n == ext + 1)


def _ovl(a, b):
    return a[1] < b[2] and b[1] < a[2] and a[3] < b[4] and b[3] < a[4]


def _covers(a, b):
    return a[5] and a[1] <= b[1] and a[2] >= b[2] and a[3] <= b[3] and a[4] >= b[4]


class _Stop(Exception):
    pass


STOP_AT = [99]
DEBUG_MIX = [False]


CK_OFF = [0]


def drain(gen):
    for _ in gen:
        pass


def interleave(ga_, gb_):
    a_live = b_live = True
    while a_live or b_live:
        if a_live:
            a_live = next(ga_, "end") != "end"
        if b_live:
            b_live = next(gb_, "end") != "end"


def chk(k):
    if k + CK_OFF[0] > STOP_AT[0]:
        raise _Stop()


class Prog:
    def __init__(self, nc):
        self.nc = nc
        self.stack = ExitStack()
        self.ops = []
        self.wr = {}
        self.rd = {}
        self.finished = False

    def sb(self, name, shape, dt):
        return self.stack.enter_context(self.nc.sbuf_tensor("sb_" + name, list(shape), dt))

    def ps(self, name, shape, dt):
        return self.stack.enter_context(self.nc.psum_tensor("ps_" + name, list(shape), dt))

    def _add(self, eng, fn, outs, ins, is_dma=False):
        idx = len(self.ops)
        deps = {}
        rregs = [_region(a) for a in ins]
        wregs = [_region(a) for a in outs]
        for r in rregs:
            for w in self.wr.get(r[0], ()):
                if _ovl(r, w[1]):
                    deps.setdefault(w[0], set()).add("raw")
        for r in wregs:
            for w in self.wr.get(r[0], ()):
                if _ovl(r, w[1]):
                    deps.setdefault(w[0], set()).add("waw")
            for w in self.rd.get(r[0], ()):
                if _ovl(r, w[1]):
                    deps.setdefault(w[0], set()).add("war")
        for r in wregs:
            lw = self.wr.setdefault(r[0], [])
            lw[:] = [w for w in lw if not _covers(r, w[1])]
            lw.append((idx, r))
            lr = self.rd.get(r[0])
            if lr:
                lr[:] = [w for w in lr if not _covers(r, w[1])]
        for r in rregs:
            self.rd.setdefault(r[0], []).append((idx, r))
        self.ops.append(dict(eng=eng, fn=fn, deps=deps, dma=is_dma, sig=False))
        return idx

    def pe(self, fn, outs, ins):
        return self._add("pe", fn, outs, ins)

    def dve(self, fn, outs, ins):
        return self._add("dve", fn, outs, ins)

    def act(self, fn, outs, ins):
        return self._add("act", fn, outs, ins)

    def pool(self, fn, outs, ins):
        return self._add("pool", fn, outs, ins)

    def dma(self, q, out, in_, **kw):
        return self._add(q, lambda e: e.dma_start(out=out, in_=in_, **kw), [out], [in_], is_dma=True)

    def finish(self):
        nc = self.nc
        ops = self.ops
        engs = ["pe", "dve", "act", "pool", "sp"]
        for i, op in enumerate(ops):
            nd = set()
            for j, kinds in op["deps"].items():
                o = ops[j]
                if o["dma"]:
                    nd.add(j)
                    continue
                if o["eng"] == op["eng"] and not op["dma"]:
                    if op["eng"] == "pe":
                        continue
                nd.add(j)
            op["deps"] = nd
            for j in nd:
                ops[j]["sig"] = True
        esem = {e: self.stack.enter_context(nc.semaphore("s_" + e)) for e in engs}
        nds = {"sp": 40, "pool": 16, "act": 8}
        dsem = {q: [self.stack.enter_context(nc.semaphore("d_%s%d" % (q, k))) for k in range(n)] for q, n in nds.items()}
        dcnt = {q: [0] * n for q, n in nds.items()}
        dnext = {q: 0 for q in nds}
        ecnt = {e: 0 for e in engs}
        for i, op in enumerate(ops):
            if op["dma"]:
                q = op["eng"]
                k = dnext[q] % nds[q]
                dnext[q] += 1
                prev = dcnt[q][k]
                dcnt[q][k] += 16
                op["event"] = (dsem[q][k], dcnt[q][k])
                op["prevev"] = (dsem[q][k], prev) if prev > 0 else None
            elif op["sig"]:
                ecnt[op["eng"]] += 1
                op["event"] = (esem[op["eng"]], ecnt[op["eng"]])
        waited = {e: {} for e in engs}
        per_eng = {e: [] for e in engs}
        for i, op in enumerate(ops):
            e = op["eng"]
            need = {}
            for j in op["deps"]:
                s, v = ops[j]["event"]
                key = id(s)
                if need.get(key, (None, 0))[1] < v:
                    need[key] = (s, v)
            if op["dma"] and op["prevev"] is not None:
                s, v = op["prevev"]
                key = id(s)
                if need.get(key, (None, 0))[1] < v:
                    need[key] = (s, v)
            waits = []
            for key, (s, v) in need.items():
                if waited[e].get(key, 0) >= v:
                    continue
                waited[e][key] = v
                waits.append((s, v))
            op["waits"] = waits
            per_eng[e].append(op)
        self.n_waits = sum(len(o["waits"]) for o in ops)
        final = []
        for q in nds:
            for k in range(nds[q]):
                if dcnt[q][k] > 0:
                    final.append((dsem[q][k], dcnt[q][k]))

        def emit(ename, e):
            for op in per_eng[ename]:
                for (s, v) in op["waits"]:
                    e.wait_ge(s, v)
                ins = op["fn"](e)
                if op["dma"]:
                    ins.then_inc(op["event"][0], 16)
                elif op["sig"]:
                    ins.then_inc(op["event"][0], 1)
            if ename == "sp":
                for (s, v) in final:
                    e.wait_ge(s, v)

        with nc.Block() as block:
            @block.tensor
            def _(e):
                emit("pe", e)

            @block.vector
            def _(e):
                emit("dve", e)

            @block.scalar
            def _(e):
                emit("act", e)

            @block.gpsimd
            def _(e):
                emit("pool", e)

            @block.sync
            def _(e):
                emit("sp", e)
        self.stack.close()
        self.finished = True


D_MODEL = 1024
D_IN = 5204
D_MIX = 2048
EPS = 1e-6
NEG = -30000.0
C_XL, C_GL, C_Q, C_K, C_GA, C_QI, C_KI, C_Z, C_XBC, C_DT = 0, 512, 1024, 1536, 1792, 2304, 2560, 2628, 3652, 5188
NPP = 100
NPB = 1208
NCST = 1152
PB_GQ, PB_GK, PB_DTB, PB_ALOG, PB_D, PB_GSSD, PB_RB15 = 0, 64, 128, 144, 160, 176, 1200
PP_GN, PP_LCW, PP_LCB, PP_LBA, PP_LBX, PP_LAM, PP_SCW, PP_SCB = 0, 8, 24, 28, 32, 36, 40, 88


def build_program(NSP=2, TP=2048, SAMPLE=True, PS=1024, TS=64, DEPTH=2, TG=256, NBIS=16):
    nc = bass.Bass("TRN2", target_bir_lowering=False)
    pg = Prog(nc)
    dt_in = lambda name, shape: nc.dram_tensor(name, list(shape), F32, kind="ExternalInput").ap()
    dt_out = lambda name, shape: nc.dram_tensor(name, list(shape), F32, kind="ExternalOutput").ap()
    dt_tmp = lambda name, shape: nc.dram_tensor(name, list(shape), F32, kind="Internal").ap()
    I = {}
    I["xp"] = dt_in("xp", [NSP, TP, D_MODEL])
    I["w_in"] = dt_in("w_in", [DEPTH, D_MODEL, D_IN])
    I["w_out"] = dt_in("w_out", [DEPTH, D_MIX, D_MODEL])
    I["pp"] = dt_in("pp", [DEPTH, 128, NPP])
    I["pb"] = dt_in("pb", [DEPTH, 128, NPB])
    I["wabd"] = dt_in("wabd", [DEPTH, 128, 2 * 4 * 128])
    I["rb"] = dt_in("rb", [32, 8])
    I["cst"] = dt_in("cst", [128, NCST])
    O = {}
    O["yp"] = dt_out("yp", [NSP, TP, D_MODEL])
    O["akp"] = dt_out("akp", [DEPTH, NSP, TP, 128])
    O["avp"] = dt_out("avp", [DEPTH, NSP, TP, 128])
    O["ikp"] = dt_out("ikp", [DEPTH, NSP, TP, 64])
    O["lcp"] = dt_out("lcp", [DEPTH, NSP, 3, 512])
    O["lhp"] = dt_out("lhp", [DEPTH, NSP, 512])
    O["scp"] = dt_out("scp", [DEPTH, NSP, 3, 1536])
    O["shp"] = dt_out("shp", [DEPTH, NSP, 1024, 128])
    xmid_p = dt_tmp("xmid_p", [NSP, TP, D_MODEL])
    if DEBUG_MIX[0]:
        O["dbg"] = nc.dram_tensor("dbg", [16, 128, TP], BF16, kind="ExternalOutput").ap()
    vecd = dt_tmp("vecd", [8, 384])
    if SAMPLE:
        I["xs"] = dt_in("xs", [TS, D_MODEL])
        I["ck"] = dt_in("ck", [DEPTH, PS, 128])
        I["cv"] = dt_in("cv", [DEPTH, PS, 128])
        I["cki"] = dt_in("cki", [DEPTH, PS, 64])
        I["slc"] = dt_in("slc", [DEPTH, 3, 512])
        I["slh"] = dt_in("slh", [DEPTH, 512])
        I["ssc"] = dt_in("ssc", [DEPTH, 3, 1536])
        I["ssh"] = dt_in("ssh", [DEPTH, 1024, 128])
        O["ys"] = dt_out("ys", [TS, D_MODEL])
        O["aks"] = dt_out("aks", [DEPTH, TS, 128])
        O["avs"] = dt_out("avs", [DEPTH, TS, 128])
        O["iks"] = dt_out("iks", [DEPTH, TS, 64])
        O["lcs"] = dt_out("lcs", [DEPTH, 3, 512])
        O["lhs"] = dt_out("lhs", [DEPTH, 512])
        O["scs"] = dt_out("scs", [DEPTH, 3, 1536])
        O["shs"] = dt_out("shs", [DEPTH, 1024, 128])
        xmid_s = dt_tmp("xmid_s", [TS, D_MODEL])

    LMAX = max(TP, (PS + TS) if SAMPLE else 0)
    LMAX = ((LMAX + 127) // 128) * 128
    NKB = LMAX // 128
    sb, ps = pg.sb, pg.ps
    win = sb("win", [128, 8, D_IN], BF16)
    wout = sb("wout", [128, 16, D_MODEL], BF16)
    ppt = sb("ppt", [128, NPP], F32)
    pbt = sb("pbt", [128, NPB], F32)
    wabd = sb("wabdb", [128, 8, 128], BF16)
    cst = sb("cst", [128, 768], F32)
    ident = cst[:, 0:128]
    tri = cst[:, 128:256]
    astr = cst[:, 256:384]
    dmask = cst[:, 640:768]
    cbf = sb("cbf", [128, 6, 128], BF16)
    trib, astrb = cbf[:, 4, :], cbf[:, 5, :]
    RB = sb("RB", [128, 2, 512], BF16)
    JK = sb("JK", [128, LMAX], mybir.dt.uint8)
    dAs = sb("dAs", [128, 3, 16], BF16)
    identb, j128b, j64b, onesb = cbf[:, 0, :], cbf[:, 1, :], cbf[:, 2, :], cbf[:, 3, :]
    hk = sb("hk", [128, 2, 8, 128], BF16)
    p2row = sb("p2row", [128, NBIS + 1], F32)
    rbt = sb("rbt", [32, 8], F32)
    rb15row = sb("rb15row", [1, 8, 128], BF16)
    vecs = sb("vecs", [8, 384], F32)
    drv = sb("drv", [128, 64], F32)
    c1 = drv[:, 0:4]
    aneg = drv[:, 4:20]
    gq8 = sb("gq8", [128, 64], F32)
    hT = sb("hT", [128, 8, TG], BF16)
    mixT = sb("mixT", [128, 16, TG], BF16)
    khT = sb("khT", [128, LMAX], BF16)
    vtok = sb("vtok", [128, NKB, 128], BF16)
    kiT = sb("kiT", [128, LMAX], BF16)
    qhT = sb("qhT", [128, 4, TG], BF16)
    qiT = sb("qiT", [128, 2, TG], BF16)
    gaT = sb("gaT", [128, 4, TG], BF16)
    lhist = sb("lhist", [128, 4, 3], F32)
    shist = sb("shist", [128, 12, 3], F32)
    lh = sb("lh", [128, 4], F32)
    xbcT = sb("xbcT", [128, 12, TG], BF16)
    sz = sb("sz", [128, 2, 1024], BF16)
    hst = sb("hst", [128, 1024], F32)
    hsb = sb("hsb", [128, 1024], BF16)
    sm = sb("sm", [128, 256], F32)
    smb = sb("smb", [128, 2, 80], F32)
    kiw = sb("kiw", [128, 2, 68], F32)
    dtt = sb("dtt", [128, 2, 16], F32)
    dAt = sb("dAt", [128, 2, 16], F32)
    A32 = sb("A32", [128, 3328], F32)
    AB = sb("AB", [128, 4608], BF16)
    pA = ps("pA", [128, 512], F32)
    pB = ps("pB", [128, 512], F32)
    pC = ps("pC", [128, 512], F32)
    pD = ps("pD", [128, 512], F32)
    pE = ps("pE", [128, 512], F32)
    pF = ps("pF", [128, 512], F32)
    pG = ps("pG", [128, 512], F32)
    pH = ps("pH", [128, 512], F32)

    act, dve, pe, pool, dma = pg.act, pg.dve, pg.pe, pg.pool, pg.dma

    def A_(fn, out, ins):
        return act(fn, [out] if not isinstance(out, list) else out, ins)

    def acopy(out, in_):
        act(lambda e: e.activation(out=out, in_=in_, func=AF.Copy), [out], [in_])

    def afunc(out, in_, func, bias=None, scale=None, accum=None):
        kw = {}
        reads = [in_]
        outs = [out]
        if bias is not None:
            kw["bias"] = bias
            if not isinstance(bias, float):
                reads.append(bias)
        if scale is not None:
            kw["scale"] = scale
            if not isinstance(scale, float):
                reads.append(scale)
        if accum is not None:
            kw["accum_out"] = accum
            outs.append(accum)
        act(lambda e: e.activation(out=out, in_=in_, func=func, **kw), outs, reads)

    def vtt(out, a, b, op):
        dve(lambda e: e.tensor_tensor(out=out, in0=a, in1=b, op=op), [out], [a, b])

    def vts(out, a, s1, op0, s2=None, op1=None, accum=None):
        reads = [a] + [s for s in (s1, s2) if s is not None and not isinstance(s, float)]
        outs = [out] + ([accum] if accum is not None else [])
        kw = {}
        if op1 is not None:
            kw["op1"] = op1
        if accum is not None:
            kw["accum_out"] = accum
        dve(lambda e: e.tensor_scalar(out=out, in0=a, scalar1=s1, scalar2=s2, op0=op0, **kw), outs, reads)

    def vstt(out, a, s, b, op0, op1):
        reads = [a, b] + ([s] if not isinstance(s, float) else [])
        dve(lambda e: e.scalar_tensor_tensor(out=out, in0=a, scalar=s, in1=b, op0=op0, op1=op1), [out], reads)

    def vcopy(out, in_):
        dve(lambda e: e.tensor_copy(out=out, in_=in_), [out], [in_])

    def vscan(out, d0, d1, init):
        reads = [d0, d1] + ([init] if not isinstance(init, float) else [])
        dve(lambda e: e.tensor_tensor_scan(out=out, data0=d0, data1=d1, initial=init, op0=ALU.mult, op1=ALU.add), [out], reads)

    def vreduce(out, in_, op, absval=False):
        if absval:
            dve(lambda e: e.tensor_reduce(out=out, in_=in_, axis=AX.X, op=op, apply_absolute_value=True), [out], [in_])
        else:
            dve(lambda e: e.tensor_reduce(out=out, in_=in_, axis=AX.X, op=op), [out], [in_])

    def vmemset(ap, val):
        dve(lambda e: e.memset(ap, val), [ap], [])

    def ptt(out, a, b, op):
        pool(lambda e: e.tensor_tensor(out=out, in0=a, in1=b, op=op), [out], [a, b])

    def split3(dst, src32, tmpa, tmpb):
        vcopy(dst[:, 0, :], src32)
        vtt(tmpa, src32, dst[:, 0, :], ALU.subtract)
        vcopy(dst[:, 1, :], tmpa)
        vtt(tmpb, tmpa, dst[:, 1, :], ALU.subtract)
        vcopy(dst[:, 2, :], tmpb)

    def rstd_pow(out, ss, n, tmp):
        afunc(tmp, ss, AF.Ln, scale=1.0 / n, bias=EPS)
        afunc(out, tmp, AF.Exp, scale=-0.5)

    def sigm(buf, src, scale=-1.0, bias=None):
        afunc(buf, src, AF.Exp, scale=scale, bias=bias)
        afunc(buf, buf, AF.Ln, bias=1.0)
        afunc(buf, buf, AF.Exp, scale=-1.0)

    def vrecip(out, in_):
        dve(lambda e: e.reciprocal(out=out, in_=in_), [out], [in_])

    def mm(out, lhsT, rhs, start, stop=True):
        pe(lambda e: e.matmul(out, lhsT=lhsT, rhs=rhs, start=start, stop=stop), [out], [lhsT, rhs])

    def tr(out, in_, idt):
        pe(lambda e: e.transpose(out, in_, idt), [out], [in_, idt])

    def bc(ap, axis, n):
        a = ap.unsqueeze(axis)
        shp = list(a.shape)
        shp[axis] = n
        return a.broadcast_to(shp)

    try:
      _body(locals())
    except _Stop:
      pass
    pg.finish()
    return nc


def _body(env):
    globals().update({k: v for k, v in env.items() if not k.startswith("__")})
    dma("sp", cst[:], I["cst"][:, 0:768])
    dma("sp", A32[0:32, 0:384], I["cst"][0:32, 768:1152])
    dma("sp", rbt[:], I["rb"])
    vcopy(cbf[:, 0, :], cst[:, 0:128])
    vcopy(cbf[:, 1, :], cst[:, 384:512])
    vcopy(cbf[:, 2, :], cst[:, 512:640])
    vmemset(cbf[:, 3, :], 1.0)
    vcopy(cbf[:, 4, :], cst[:, 128:256])
    vcopy(cbf[:, 5, :], cst[:, 256:384])
    mm(pA[0:8, 0:384], rbt[:, :], A32[0:32, 0:384], True)
    vcopy(vecs[:], pA[0:8, 0:384])
    vcopy(rbt[0:8, 0:1], vecs[:, 0:1])
    vts(vecs[:], vecs[:], rbt[0:8, 0:1], ALU.subtract)
    dma("sp", vecd, vecs[:])
    hktmp = A32[:, 0:1024].rearrange("p (h s) -> p h s", h=8)
    for k in range(NBIS + 1):
        vmemset(p2row[:, k:k + 1], 2.0 ** -k)
    for si, (TTq, dl) in enumerate([(128, 0), (128, -128)]):
        base = dl - TTq + 256
        src = bass.AP(tensor=vecd.tensor, offset=base, ap=[[1, TTq], [384, 8], [1, 128]])
        dma("sp", hktmp[:TTq], src)
        vcopy(hk[:TTq, si, :, :], hktmp[:TTq])

    CK_OFF[0] = 0
    chk(1)
    seqs = []
    for q in range(NSP):
        seqs.append(dict(kind="p", idx=q, T=TP, P=0))
    if SAMPLE:
        seqs.append(dict(kind="s", idx=0, T=TS, P=PS))

    for l in range(DEPTH):
        for kc in range(8):
            for hf in range(2):
                c0, c1_ = hf * 2602, (hf + 1) * 2602
                dma("pool", win[:, kc, c0:c1_], I["w_in"][l, kc * 128:(kc + 1) * 128, c0:c1_])
        for ec in range(16):
            dma("pool", wout[:, ec, :], I["w_out"][l, ec * 128:(ec + 1) * 128, :])
        dma("sp", ppt[:], I["pp"][l])
        dma("sp", pbt[:], I["pb"][l])
        dma("pool", wabd[:].rearrange("p a b -> p (a b)"), I["wabd"][l])
        afunc(drv[:, 20:24], ppt[:, PP_LAM:PP_LAM + 4], AF.Exp, scale=-1.0)
        afunc(drv[:, 24:28], drv[:, 20:24], AF.Ln, bias=1.0)
        vts(c1, drv[:, 24:28], -8.0, ALU.mult)
        afunc(drv[:, 28:44], pbt[:, PB_ALOG:PB_ALOG + 16], AF.Exp)
        vts(aneg, drv[:, 28:44], -1.0, ALU.mult)
        vts(gq8[:], pbt[:, PB_GQ:PB_GQ + 64], 0.125, ALU.mult)
        vts(drv[:, 48:52], ppt[:, PP_LBA:PP_LBA + 4], -1.0, ALU.mult)
        vts(drv[:, 52:56], ppt[:, PP_LBX:PP_LBX + 4], -1.0, ALU.mult)
        vcopy(rb15row[0:1, :, :], bc(pbt[0:1, PB_RB15:PB_RB15 + 8], 2, 128))

        chk(2)
        for sq in seqs:
            T, P0 = sq["T"], sq["P"]
            isS = sq["kind"] == "s"
            qi_ = sq["idx"]
            L = P0 + T
            ktop = min(256, L // 4)
            if isS:
                xin = I["xs"] if l == 0 else xmid_s
                xout = O["ys"] if l == DEPTH - 1 else xmid_s
                o_ak, o_av, o_ik = O["aks"][l], O["avs"][l], O["iks"][l]
                o_lc, o_lh, o_sc, o_sh = O["lcs"][l], O["lhs"][l], O["scs"][l], O["shs"][l]
            else:
                xin = I["xp"][qi_] if l == 0 else xmid_p[qi_]
                xout = O["yp"][qi_] if l == DEPTH - 1 else xmid_p[qi_]
                o_ak, o_av, o_ik = O["akp"][l, qi_], O["avp"][l, qi_], O["ikp"][l, qi_]
                o_lc, o_lh, o_sc, o_sh = O["lcp"][l, qi_], O["lhp"][l, qi_], O["scp"][l, qi_], O["shp"][l, qi_]
            if isS:
                for g in range(4):
                    dma("sp", lhist[:, g, :], I["slc"][l].rearrange("j (g p) -> p g j", p=128)[:, g, :], allow_slow_non_contiguous=True)
                for g in range(12):
                    dma("sp", shist[:, g, :], I["ssc"][l].rearrange("j (g p) -> p g j", p=128)[:, g, :], allow_slow_non_contiguous=True)
                dma("sp", lh[:], I["slh"][l].rearrange("(g p) -> p g", p=128), allow_slow_non_contiguous=True)
                stt = A32[:, 0:1024].rearrange("p (c n) -> p c n", c=8)
                dma("sp", stt, I["ssh"][l].rearrange("(c p) n -> p c n", p=128))
                sp3 = AB[:, 0:3072].rearrange("p (k n) -> p k n", k=3)
                split3(sp3, A32[:, 0:1024], A32[:, 1024:2048], A32[:, 2048:3072])
                for c in range(8):
                    pbank = pA if c < 4 else pB
                    for k in range(3):
                        mm(pbank[:, (c % 4) * 128:(c % 4 + 1) * 128], sp3[:, k, c * 128:(c + 1) * 128], identb, c % 4 == 0 and k == 0, c % 4 == 3 and k == 2)
                vcopy(hst[:, 0:512], pA[:, :])
                vcopy(hst[:, 512:1024], pB[:, :])
                acopy(hsb[:, :], hst[:, :])
                nb = P0 // 128
                ckb = AB[:, 0:nb * 128].rearrange("p (c n) -> p c n", c=nb)
                kid = AB[:, nb * 128:2 * nb * 128].rearrange("p (c n) -> p c n", c=nb)
                dma("pool", ckb, I["ck"][l].rearrange("(c p) n -> p c n", p=128))
                dma("pool", vtok[:, 0:nb, :], I["cv"][l].rearrange("(c p) n -> p c n", p=128))
                dma("pool", kid[:, :, 0:64], I["cki"][l].rearrange("(c p) n -> p c n", p=128))
                dma("pool", kid[:, :, 64:128], I["cki"][l].rearrange("(c p) n -> p c n", p=128))
                for c in range(nb):
                    mm(pC[:, (c % 4) * 128:(c % 4 + 1) * 128], ckb[:, c, :], identb, c % 4 == 0, c % 4 == 3 or c == nb - 1)
                    mm(pD[:, (c % 4) * 128:(c % 4 + 1) * 128], kid[:, c, :], identb, c % 4 == 0, c % 4 == 3 or c == nb - 1)
                    if c % 4 == 3 or c == nb - 1:
                        c0 = (c // 4) * 4
                        n_ = c - c0 + 1
                        vcopy(khT[:, c0 * 128:(c0 + n_) * 128], pC[:, 0:n_ * 128])
                        acopy(kiT[:, c0 * 128:(c0 + n_) * 128], pD[:, 0:n_ * 128])
            else:
                vmemset(lhist[:], 0.0)
                vmemset(shist[:], 0.0)
                vmemset(lh[:], 0.0)
                vmemset(hst[:], 0.0)
                vmemset(hsb[:], 0.0)

            CK_OFF[0] = 10 if isS else 0
            chk(3)
            ngroups = (T + TG - 1) // TG
            for gi in range(ngroups):
                t0 = gi * TG
                NV = min(TG, T - t0)
                NT = ((NV + 127) // 128) * 128
                TT = 128
                ntile = NT // TT
                def gen_p1(ti):
                    r0 = t0 + ti * TT
                    TV = min(TT, NV - ti * TT)
                    xb_ = (ti % 2) * 1024
                    xt32 = A32[:TT, xb_:xb_ + 1024]
                    if TV < TT:
                        vmemset(A32[TV:TT, xb_:xb_ + 1024], 0.0)
                    dma("sp", A32[:TV, xb_:xb_ + 1024], xin[r0:r0 + TV, :])
                    yield
                    ab_ = (ti % 2) * 2048
                    junk = AB[:TT, ab_:ab_ + 1024]
                    xn = AB[:TT, ab_ + 1024:ab_ + 2048]
                    bk0, bk1 = (pA, pB) if ti % 2 == 0 else (pC, pD)
                    ssq = sm[:TT, ti:ti + 1]
                    afunc(junk, xt32, AF.Square, accum=ssq)
                    yield
                    rstd_pow(sm[:TT, 4 + ti:5 + ti], ssq, D_MODEL, sm[:TT, 2 + ti:3 + ti])
                    yield
                    vts(xn, xt32, sm[:TT, 4 + ti:5 + ti], ALU.mult)
                    yield
                    for kc in range(8):
                        pb_ = bk0 if kc < 4 else bk1
                        mm(pb_[:, (kc % 4) * TT:(kc % 4 + 1) * TT], xn[:, kc * 128:(kc + 1) * 128], identb[:TT, :TT], kc % 4 == 0, kc % 4 == 3)
                    yield
                    for hf in range(2):
                        pb_ = bk0 if hf == 0 else bk1
                        vtt(hT[:, hf * 4:(hf + 1) * 4, ti * TT:(ti + 1) * TT], pb_[:, 0:4 * TT].rearrange("p (c t) -> p c t", c=4),
                            bc(ppt[:, PP_GN + hf * 4:PP_GN + hf * 4 + 4], 2, TT), ALU.mult)
                    yield

                if ntile == 2:
                    interleave(gen_p1(0), gen_p1(1))
                else:
                    drain(gen_p1(0))

                chk(4)
                def proj_fm(pbank, c0, M, prow=0):
                    for kc in range(8):
                        mm(pbank[prow:prow + M, :NT], win[:, kc, c0:c0 + M], hT[:, kc, :NT], kc == 0, kc == 7)

                def conv(pbank, hist, g, wcol, bcol, out32, roff=0, ppt=ppt):
                    raw = A32[:, roff:roff + 3 + NT]
                    vcopy(raw[:, 0:3], hist[:, g, :])
                    acopy(raw[:, 3:3 + NT], pbank[:, :NT])
                    vcopy(hist[:, g, :], raw[:, NV:NV + 3])
                    vts(out32, raw[:, 0:NT], ppt[:, wcol:wcol + 1], ALU.mult, ppt[:, bcol:bcol + 1], ALU.add)
                    for j in range(1, 4):
                        vstt(out32, raw[:, j:j + NT], ppt[:, wcol + j:wcol + j + 1], out32, ALU.mult, ALU.add)
                    return raw

                def gen_lru(g):
                    lbase = (g % 2) * 1664
                    f = lambda k: A32[:, lbase + 260 + k * 256:lbase + 260 + k * 256 + NT]
                    gC, gD = (pC, pD) if g % 2 == 0 else (pG, pH)
                    xc, rr, ii, aa, s_ = [f(k) for k in range(5)]
                    gx, bb, hseq, sg = ii, s_, xc, rr
                    xcb = AB[:, 2048 + (g % 2) * 256:2048 + (g % 2) * 256 + NT]
                    pb_ = pA if g % 2 == 0 else pB
                    proj_fm(pb_, C_XL + g * 128, 128)
                    raw = conv(pb_, lhist, g, PP_LCW + g * 4, PP_LCB + g, xc, lbase)
                    if gi == ngroups - 1:
                        dma("sp", o_lc.rearrange("j (g p) -> p g j", p=128)[:, g, :], raw[:, NV:NV + 3], allow_slow_non_contiguous=True)
                    acopy(xcb, xc)
                    yield
                    mm(gC[:, :NT], wabd[:, g, :], xcb, True)
                    mm(gD[:, :NT], wabd[:, 4 + g, :], xcb, True)
                    sigm(rr, gC[:, :NT], -1.0, drv[:, 48 + g:49 + g])
                    yield
                    sigm(ii, gD[:, :NT], -1.0, drv[:, 52 + g:53 + g])
                    yield
                    afunc(aa, rr, AF.Exp, scale=c1[:, g:g + 1])
                    afunc(s_, aa, AF.Square)
                    afunc(s_, s_, AF.Ln, scale=-1.0, bias=1.0)
                    afunc(s_, s_, AF.Exp, scale=0.5)
                    yield
                    vtt(gx, ii, xc, ALU.mult)
                    vtt(bb, s_, gx, ALU.mult)
                    vscan(hseq, aa, bb, lh[:, g:g + 1])
                    vcopy(lh[:, g:g + 1], hseq[:, NV - 1:NV])
                    yield
                    pg_ = pE if g % 2 == 0 else pF
                    proj_fm(pg_, C_GL + g * 128, 128)
                    sigm(sg, pg_[:, :NT])
                    yield
                    vtt(sg, sg, pg_[:, :NT], ALU.mult)
                    vtt(mixT[:, g, :NT], hseq, sg, ALU.mult)
                    yield

                interleave(gen_lru(0), gen_lru(1))
                interleave(gen_lru(2), gen_lru(3))
                if gi == ngroups - 1:
                    dma("sp", o_lh.rearrange("(g p) -> p g", p=128), lh[:], allow_slow_non_contiguous=True)

                chk(5)
                for g in range(4):
                    pb_ = pA if g % 2 == 0 else pB
                    proj_fm(pb_, C_GA + g * 128, 128)
                    gth = A32[:, 512 + (g % 2) * 256:512 + (g % 2) * 256 + NT]
                    sigm(gth, pb_[:, :NT])
                    vtt(gaT[:, g, :NT], gth, pb_[:, :NT], ALU.mult)
                for g in range(2):
                    pb_ = pE if g % 2 == 0 else pF
                    proj_fm(pb_, C_QI + g * 128, 128)
                    vcopy(qiT[:, g, :NT], pb_[:, :NT])
                proj_fm(pA, C_KI, 64, 0)
                proj_fm(pA, C_KI, 64, 64)
                acopy(kiT[:, P0 + t0:P0 + t0 + NT], pA[:, :NT])

                def gen_conv(g):
                    pb_ = pE if g % 2 == 0 else pF
                    proj_fm(pb_, C_XBC + g * 128, 128)
                    roff = (g % 2) * 1280
                    acc = A32[:, roff + 512:roff + 512 + NT]
                    raw = conv(pb_, shist, g, PP_SCW + g * 4, PP_SCB + g, acc, roff)
                    if gi == ngroups - 1:
                        dma("sp", o_sc.rearrange("j (g p) -> p g j", p=128)[:, g, :], raw[:, NV:NV + 3], allow_slow_non_contiguous=True)
                    yield
                    cth = A32[:, roff + 768:roff + 768 + NT]
                    sigm(cth, acc)
                    yield
                    vtt(xbcT[:, g, :NT], cth, acc, ALU.mult)
                    yield

                for g2 in range(0, 12, 2):
                    interleave(gen_conv(g2), gen_conv(g2 + 1))

                chk(6)
                def gen_p2b(ti):
                    tsl = slice(ti * TT, (ti + 1) * TT)
                    r0 = t0 + ti * TT
                    TV = min(TT, NV - ti * TT)
                    kb = (P0 + r0) // 128
                    koff = (P0 + r0) % 128
                    qC, qD, qE, qF = (pC, pD, pE, pF) if ti % 2 == 0 else (pA, pB, pG, pH)
                    tb = (ti % 2) * 1536
                    for kc in range(8):
                        mm(qC[:TT, 0:512], hT[:, kc, tsl], win[:, kc, C_Q:C_Q + 512], kc == 0, kc == 7)
                    for (o0, c0, n_) in ((0, C_K, 256), (256, C_KI, 68), (324, C_DT, 16)):
                        for kc in range(8):
                            mm(qD[:TT, o0:o0 + n_], hT[:, kc, tsl], win[:, kc, c0:c0 + n_], kc == 0, kc == 7)
                    yield
                    for hf in range(2):
                        pz = qE if hf == 0 else qF
                        for kc in range(8):
                            mm(pz[:TT, :], hT[:, kc, tsl], win[:, kc, C_Z + hf * 512:C_Z + (hf + 1) * 512], kc == 0, kc == 7)
                        zth = A32[:TT, tb + 512 + hf * 512:tb + 1024 + hf * 512]
                        sigm(zth, pz[:TT, :])
                        vtt(sz[:TT, ti, hf * 512:(hf + 1) * 512], zth, pz[:TT, :], ALU.mult)
                        yield
                    sqt = A32[:TT, tb + 512:tb + 1024]
                    qn = A32[:TT, tb + 1024:tb + 1536]
                    kv32 = A32[:TT, tb + 1536:tb + 1792]
                    qhat = AB[:TT, 2304 + (ti % 2) * 640:2816 + (ti % 2) * 640]
                    khat = AB[:TT, 2816 + (ti % 2) * 640:2944 + (ti % 2) * 640]
                    afunc(sqt, qC[:TT, :], AF.Square)
                    vreduce(smb[:TT, ti % 2, 8:16], sqt.rearrange("p (h d) -> p h d", h=8), ALU.add)
                    rstd_pow(smb[:TT, ti % 2, 24:32], smb[:TT, ti % 2, 8:16], 64, smb[:TT, ti % 2, 16:24])
                    vtt(qn.rearrange("p (h d) -> p h d", h=8), qC[:TT, :].rearrange("p (h d) -> p h d", h=8), bc(smb[:TT, ti % 2, 24:32], 2, 64), ALU.mult)
                    vtt(qhat.rearrange("p (m g d) -> p g m d", m=4, g=2), qn.rearrange("p (g m d) -> p g m d", g=2, m=4),
                        bc(bc(gq8[:TT, :], 1, 4), 1, 2), ALU.mult)
                    yield
                    afunc(sqt[:, 0:128], qD[:TT, 0:128], AF.Square)
                    vreduce(smb[:TT, ti % 2, 32:34], sqt[:, 0:128].rearrange("p (h d) -> p h d", h=2), ALU.add)
                    rstd_pow(smb[:TT, ti % 2, 36:38], smb[:TT, ti % 2, 32:34], 64, smb[:TT, ti % 2, 34:36])
                    vtt(qn[:, 0:128].rearrange("p (h d) -> p h d", h=2), qD[:TT, 0:128].rearrange("p (h d) -> p h d", h=2), bc(smb[:TT, ti % 2, 36:38], 2, 64), ALU.mult)
                    vtt(kv32[:, 0:128].rearrange("p (h d) -> p h d", h=2), qn[:, 0:128].rearrange("p (h d) -> p h d", h=2),
                        bc(pbt[:TT, PB_GK:PB_GK + 64], 1, 2), ALU.mult)
                    acopy(khat, kv32[:, 0:128])
                    acopy(kv32[:, 128:256], qD[:TT, 128:256])
                    acopy(vtok[koff:koff + TT, kb, :], qD[:TT, 128:256])
                    dma("sp", o_ak[r0:r0 + TV, :], kv32[:TV, 0:128])
                    dma("sp", o_av[r0:r0 + TV, :], kv32[:TV, 128:256])
                    vcopy(kiw[:TT, ti, :], qD[:TT, 256:324])
                    dma("sp", o_ik[r0:r0 + TV, :], kiw[:TV, ti, 0:64])
                    yield
                    vtt(smb[:TT, ti % 2, 40:56], qD[:TT, 324:340], pbt[:TT, PB_DTB:PB_DTB + 16], ALU.add)
                    afunc(smb[:TT, ti % 2, 56:72], smb[:TT, ti % 2, 40:56], AF.Exp)
                    afunc(dtt[:TT, ti, :], smb[:TT, ti % 2, 56:72], AF.Ln, bias=1.0)
                    if TV < TT:
                        vmemset(dtt[TV:TT, ti, :], 0.0)
                    vtt(dAt[:TT, ti, :], dtt[:TT, ti, :], aneg[:TT, :], ALU.mult)
                    yield
                    mm(qE[:, 0:TT], khat, identb[:TT, :TT], True)
                    vcopy(khT[:, P0 + r0:P0 + r0 + TT], qE[:, 0:TT])
                    for m in range(4):
                        mm(qF[:, m * TT:(m + 1) * TT], qhat[:, m * 128:(m + 1) * 128], identb[:TT, :TT], m == 0, m == 3)
                    vcopy(qhT[:, :, tsl], qF[:, 0:4 * TT].rearrange("p (c t) -> p c t", c=4))
                    yield

                if ntile == 2:
                    interleave(gen_p2b(0), gen_p2b(1))
                else:
                    drain(gen_p2b(0))

                chk(7)
                for ti in range(ntile):
                    tsl = slice(ti * TT, (ti + 1) * TT)
                    if ti == 1:
                        chk(7.9)
                    dA = dAt[:TT, ti, :]
                    dtv = dtt[:TT, ti, :]
                    split3(dAs[:TT], dA, sm[:TT, 176:192], sm[:TT, 192:208])
                    for k in range(3):
                        mm(pF[:TT, 0:16], trib[:TT, :TT], dAs[:TT, k, :], k == 0, k == 2)
                    for k in range(3):
                        mm(pF[:TT, 16:32], astrb[:TT, :TT], dAs[:TT, k, :], k == 0, k == 2)
                    for k in range(3):
                        mm(pF[:, 32:48], onesb[:TT, :], dAs[:TT, k, :], k == 0, k == 2)
                    ecum = sm[:TT, 80:96]
                    toend = sm[:TT, 96:112]
                    dec = sm[:, 112:128]
                    afunc(sm[:TT, 80:112], pF[:TT, 0:32], AF.Exp)
                    afunc(dec, pF[:, 32:48], AF.Exp)
                    chk(7.1)
                    for c in range(8):
                        pb_ = pC if c < 4 else pD
                        mm(pb_[:TT, (c % 4) * 128:(c % 4 + 1) * 128], xbcT[:, c, tsl], identb, c % 4 == 0, c % 4 == 3)
                    xt_ = AB[:TT, 0:1024]
                    x2_ = AB[:TT, 1024:2048]
                    xD = A32[:TT, 0:1024]
                    for hf in range(2):
                        pTv = (pC if hf == 0 else pD)[:TT, :].rearrange("p (h d) -> p h d", h=8)
                        hs_ = slice(hf * 512, (hf + 1) * 512)
                        vtt(xt_[:, hs_].rearrange("p (h d) -> p h d", h=8), pTv, bc(dtv[:, hf * 8:(hf + 1) * 8], 2, 64), ALU.mult)
                        vtt(xD[:, hs_].rearrange("p (h d) -> p h d", h=8), pTv, bc(pbt[:TT, PB_D + hf * 8:PB_D + hf * 8 + 8], 2, 64), ALU.mult)
                    vtt(x2_.rearrange("p (h d) -> p h d", h=16), xt_.rearrange("p (h d) -> p h d", h=16), bc(toend, 2, 64), ALU.mult)
                    def gen_ssd(g, ti=ti, tsl=tsl, xt_=xt_, x2_=x2_, xD=xD, ecum=ecum, dec=dec):
                        eb = 2048 + g * 1280
                        E = AB[:TT, eb:eb + 8 * TT].rearrange("p (h l) -> p h l", h=8)
                        WT = E
                        G = AB[:TT, eb + 1024:eb + 1024 + TT]
                        bmtok = AB[:TT, eb + 1152:eb + 1280]
                        ycb = AB[:TT, eb:eb + 512]
                        t1 = A32[:TT, 1024 + g * 1024:1536 + g * 1024]
                        yz = A32[:TT, 1536 + g * 1024:2048 + g * 1024]
                        dbk = (pA, pB) if g == 0 else (pG, pH)
                        cbk = pC if g == 0 else pF
                        ydk = pD if g == 0 else pG
                        yok = pE if g == 0 else pH
                        ytk = dbk[0]
                        hpb = 512 // TT
                        for bk in range(8 // hpb):
                            pb_ = dbk[bk % 2]
                            for k in range(2):
                                Rk = RB[:TT, (g * 2 + bk + k) % 2, 0:hpb * TT]
                                h0_ = g * 8 + bk * hpb
                                vtt(Rk.rearrange("p (h l) -> p h l", h=hpb), bc(dAs[:TT, k, h0_:h0_ + hpb], 2, TT), bc(trib[:TT, :TT], 1, hpb), ALU.mult)
                                mm(pb_[:TT, 0:hpb * TT], astrb[:TT, :TT], Rk, k == 0, k == 1)
                            afunc(E[:, bk * hpb:(bk + 1) * hpb, :], pb_[:TT, 0:hpb * TT].rearrange("p (h l) -> p h l", h=hpb), AF.Exp)
                            yield
                        mm(cbk[:TT, :TT], xbcT[:, 8 + g, tsl], xbcT[:, 10 + g, tsl], True)
                        vtt(G, cbk[:TT, :TT], tri[:TT, :TT], ALU.mult)
                        vtt(WT, E, bc(G, 1, 8), ALU.mult)
                        yield
                        for hh in range(8):
                            h_ = g * 8 + hh
                            mm(ydk[:TT, hh * 64:(hh + 1) * 64], WT[:, hh, :], xt_[:, h_ * 64:(h_ + 1) * 64], hh == 0, hh == 7)
                        mm(yok[:TT, :], xbcT[:, 10 + g, tsl], hsb[:, g * 512:(g + 1) * 512], True)
                        yield
                        vtt(t1.rearrange("p (h d) -> p h d", h=8), yok[:TT, :].rearrange("p (h d) -> p h d", h=8), bc(ecum[:, g * 8:(g + 1) * 8], 2, 64), ALU.mult)
                        vtt(t1, t1, ydk[:TT, :], ALU.add)
                        vtt(t1, t1, xD[:, g * 512:(g + 1) * 512], ALU.add)
                        vtt(yz, t1, sz[:TT, ti, g * 512:(g + 1) * 512], ALU.mult)
                        yield
                        afunc(t1, yz, AF.Square, accum=sm[:TT, 128 + g:129 + g])
                        rstd_pow(sm[:TT, 132 + g:133 + g], sm[:TT, 128 + g:129 + g], 512, sm[:TT, 130 + g:131 + g])
                        vts(t1, yz, sm[:TT, 132 + g:133 + g], ALU.mult)
                        vtt(ycb, t1, pbt[:TT, PB_GSSD + g * 512:PB_GSSD + (g + 1) * 512], ALU.mult)
                        yield
                        for c in range(4):
                            mm(ytk[:, c * TT:(c + 1) * TT], ycb[:, c * 128:(c + 1) * 128], identb[:TT, :TT], c == 0, c == 3)
                        vcopy(mixT[:, 8 + g * 4:12 + g * 4, tsl], ytk[:, 0:4 * TT].rearrange("p (c t) -> p c t", c=4))
                        yield
                        mm(cbk[:TT, 0:128], xbcT[:, 8 + g, tsl], identb, True)
                        acopy(bmtok, cbk[:TT, 0:128])
                        mm(cbk[:, :], bmtok, x2_[:, g * 512:(g + 1) * 512], True)
                        hv = hst[:, g * 512:(g + 1) * 512]
                        vtt(hv.rearrange("p (h d) -> p h d", h=8), hv.rearrange("p (h d) -> p h d", h=8), bc(dec[:, g * 8:(g + 1) * 8], 2, 64), ALU.mult)
                        vtt(hv, hv, cbk[:, :], ALU.add)
                        acopy(hsb[:, g * 512:(g + 1) * 512], hv)
                        yield

                    interleave(gen_ssd(0), gen_ssd(1))
                if gi == ngroups - 1:
                    stt = A32[:, 0:1024].rearrange("p (c n) -> p c n", c=8)
                    sp3 = AB[:, 0:3072].rearrange("p (k n) -> p k n", k=3)
                    split3(sp3, hst[:, :], A32[:, 1024:2048], A32[:, 2048:3072])
                    for c in range(8):
                        pbank = pA if c < 4 else pB
                        for k in range(3):
                            mm(pbank[:, (c % 4) * 128:(c % 4 + 1) * 128], sp3[:, k, c * 128:(c + 1) * 128], identb, c % 4 == 0 and k == 0, c % 4 == 3 and k == 2)
                    vcopy(stt[:, 0:4, :], pA[:, :].rearrange("p (c n) -> p c n", c=4))
                    vcopy(stt[:, 4:8, :], pB[:, :].rearrange("p (c n) -> p c n", c=4))
                    dma("sp", o_sh.rearrange("(c p) n -> p c n", p=128), stt)

                chk(8)
                def p4_vars(ti):
                    q0 = P0 + t0 + ti * TT
                    Lk = q0 + TT
                    return (slice(ti * TT, (ti + 1) * TT), q0, Lk, (Lk + 127) // 128, A32[:TT, 0:Lk], AB[:TT, 0:Lk], JK[:TT, 0:Lk])

                def gen_topk(ti):
                    tsl, q0, Lk, nkb, score, negm, junkb = p4_vars(ti)
                    for c0 in range(0, Lk, 512):
                        c1_ = min(Lk, c0 + 512)
                        for hh in range(4):
                            pb_ = pA if hh % 2 == 0 else pB
                            pr = (hh % 2) * 64
                            mm(pb_[:TT, 0:c1_ - c0], qiT[pr:pr + 64, hh // 2, tsl], kiT[pr:pr + 64, c0:c1_], True)
                            rl = A32[:TT, 2048 + (hh % 2) * 512:2560 + (hh % 2) * 512]
                            afunc(rl[:, 0:c1_ - c0], pb_[:TT, 0:c1_ - c0], AF.Relu)
                            if hh == 0:
                                vts(score[:, c0:c1_], rl[:, 0:c1_ - c0], kiw[:TT, ti, 64:65], ALU.mult)
                            else:
                                vstt(score[:, c0:c1_], rl[:, 0:c1_ - c0], kiw[:TT, ti, 64 + hh:65 + hh], score[:, c0:c1_], ALU.mult, ALU.add)
                            yield
                    lo = sm[:TT, 140:141]
                    if Lk > ktop:
                        amax = sm[:TT, 141:142]
                        hw = sm[:TT, 144:144 + NBIS + 1]
                        vreduce(amax, score, ALU.max, absval=True)
                        vtt(score[:, Lk - 128:Lk], score[:, Lk - 128:Lk], dmask[:, :], ALU.add)
                        vts(amax, amax, 1.001, ALU.mult, 1e-3, ALU.add)
                        vts(hw, p2row[:TT, :], amax, ALU.mult)
                        mid = sm[:TT, 142:143]
                        cnt = sm[:TT, 143:144]
                        ind = sm[:TT, 170:171]
                        vmemset(mid, 0.0)
                        yield
                        for it in range(NBIS):
                            vts(junkb, score, mid, ALU.is_gt, 0.0, ALU.add, accum=cnt)
                            vstt(ind, cnt, float(ktop) - 0.5, hw[:, it:it + 1], ALU.is_gt, ALU.mult)
                            vstt(mid, ind, hw[:, it + 1:it + 2], mid, ALU.subtract, ALU.add)
                            yield
                        vtt(lo, mid, hw[:, NBIS:NBIS + 1], ALU.subtract)
                    else:
                        vtt(score[:, Lk - 128:Lk], score[:, Lk - 128:Lk], dmask[:, :], ALU.add)
                        vmemset(lo, NEG / 2)

                def do_mask(ti):
                    tsl, q0, Lk, nkb, score, negm, junkb = p4_vars(ti)
                    lo = sm[:TT, 140:141]
                    if Lk > ktop:
                        c0 = sm[:TT, 171:172]
                        mrem = sm[:TT, 172:173]
                        vts(junkb, score, 0.0, ALU.is_gt, 0.0, ALU.add, accum=c0)
                        vts(mrem, c0, -1.0, ALU.mult, float(ktop), ALU.add)
                        vts(negm, score, 0.0, ALU.is_equal)
                        vscan(negm, onesb[:TT, 0:1].broadcast_to([TT, Lk]), negm, 0.0)
                        vts(negm, negm, mrem, ALU.is_gt)
                        vstt(negm, score, 0.0, negm, ALU.is_equal, ALU.mult)
                        vstt(negm, score, lo, negm, ALU.is_le, ALU.max)
                        vts(negm, negm, NEG, ALU.mult)
                    else:
                        vts(negm, score, lo, ALU.is_le, NEG, ALU.mult)

                def gen_attn(ti):
                    tsl, q0, Lk, nkb, score, negm, junkb = p4_vars(ti)
                    for g in range(2):
                        for jb in range(nkb):
                            S = min(128, Lk - jb * 128)
                            pl = pC if jb % 2 == 0 else pD
                            plv = pl[:S, 0:4 * TT].rearrange("p (m t) -> p m t", m=4)
                            mm(plv, negm[:, jb * 128:jb * 128 + S], bc(identb[:TT, :TT], 1, 4), True, False)
                            dl = jb * 128 - q0
                            mm(plv, khT[g * 64:(g + 1) * 64, jb * 128:jb * 128 + S], qhT[g * 64:(g + 1) * 64, :, tsl], False, dl <= -256)
                            if dl > -256:
                                si = 0 if dl == 0 else 1
                                for m in range(4):
                                    mm(plv[:, m, :], hk[:TT, si, g * 4 + m, :S], j128b[:TT, :TT], False, m == 3)
                            ET = AB[:S, 2048 + (jb % 2) * 512:2048 + (jb % 2) * 512 + 4 * TT].rearrange("p (m t) -> p m t", m=4)
                            afunc(ET, plv, AF.Exp)
                            pov = pE[:, 0:2 * TT].rearrange("p (a t) -> p a t", a=2)
                            pdv = pF[:, 0:2 * TT].rearrange("p (a t) -> p a t", a=2)
                            for par in range(2):
                                mm(pov[par * 64:(par + 1) * 64], vtok[:S, jb, g * 64:(g + 1) * 64], ET[:, par::2, :], jb == 0, jb == nkb - 1)
                                mm(pdv[par * 64:(par + 1) * 64], onesb[:S, 0:64], ET[:, par::2, :], jb == 0, jb == nkb - 1)
                            yield
                        rden = A32[:, 3072:3072 + 2 * TT]
                        vrecip(rden, pF[:, 0:2 * TT])
                        vtt(rden, pE[:, 0:2 * TT], rden, ALU.mult)
                        vtt(mixT[:, 4 + 2 * g:6 + 2 * g, tsl], rden.rearrange("p (a t) -> p a t", a=2), gaT[:, 2 * g:2 * g + 2, tsl], ALU.mult)
                        yield

                def gen_p5(ti):
                    tsl = slice(ti * TT, (ti + 1) * TT)
                    r0 = t0 + ti * TT
                    TV = min(TT, NV - ti * TT)
                    base = (ti % 2) * 1024
                    xr = A32[:TT, base:base + 1024]
                    if TV < TT:
                        vmemset(A32[TV:TT, base:base + 1024], 0.0)
                    dma("sp", A32[:TV, base:base + 1024], xin[r0:r0 + TV, :])
                    yield
                    for hf in range(2):
                        pb_ = pA if hf == 0 else pB
                        for ec in range(16):
                            mm(pb_[:TT, :], mixT[:, ec, tsl], wout[:, ec, hf * 512:(hf + 1) * 512], ec == 0, ec == 15)
                            if ec % 4 == 3:
                                yield
                        vtt(xr[:, hf * 512:(hf + 1) * 512], pb_[:TT, :], xr[:, hf * 512:(hf + 1) * 512], ALU.add)
                        yield
                    dma("sp", xout[r0:r0 + TV, :], xr[:TV])
                    yield

                drain(gen_topk(0))
                do_mask(0)
                if ntile == 2:
                    interleave(gen_attn(0), gen_topk(1))
                    do_mask(1)
                    interleave(gen_attn(1), gen_p5(0))
                else:
                    drain(gen_attn(0))

                chk(9)
                if DEBUG_MIX[0] and not isS and qi_ == 0 and l == 0:
                    dma("sp", O["dbg"][:, :, t0:t0 + NT].rearrange("c p t -> p c t"), mixT[:, :, :NT])
                drain(gen_p5(1 if ntile == 2 else 0))


def _t5_bucket_np(rel):
    import math
    import jax
    import jax.numpy as jnp
    with jax.default_device(jax.devices("cpu")[0]):
        return _t5_bucket_cpu(rel, math, jnp)


def _t5_bucket_cpu(rel, math, jnp):
    rel = jnp.asarray(rel, dtype=jnp.int32)
    half, max_exact = 16, 8
    n = jnp.abs(rel)
    large = max_exact + (jnp.log(jnp.maximum(n, 1).astype(jnp.float32) / max_exact) / math.log(128 / max_exact) * (half - max_exact)).astype(jnp.int32)
    large = jnp.minimum(large, half - 1)
    return np.asarray(jnp.where(rel > 0, half, 0) + jnp.where(n < max_exact, n, large))


def _consts():
    c = np.zeros((128, NCST), np.float32)
    i = np.arange(128)
    c[:, 0:128] = np.eye(128)
    c[:, 128:256] = (i[:, None] <= i[None, :])
    c[:, 256:384] = (i[:, None] > i[None, :])
    c[:, 384:512] = (i[:, None] == 127 - i[None, :])
    c[:64, 512:576] = (i[:64, None] == 63 - i[None, :64])
    c[:, 640:768] = np.where((i[None, :] // 64) > (i[:, None] // 64), NEG, 0.0)
    bk = _t5_bucket_np(np.arange(384) - 255)
    c[0:32, 768:1152] = (np.arange(32)[:, None] == bk[None, :])
    return c


_PROG_CACHE = {}


def _run(inputs, NCORES, NSP, TP, SAMPLE, PS, TS, DEPTH):
    key = (NSP, TP, SAMPLE, PS, TS, DEPTH)
    f32 = np.float32
    g = lambda k: np.ascontiguousarray(np.asarray(inputs[k], dtype=f32))
    w_in, w_out = g("w_in"), g("w_out")
    cst = _consts()
    pp = np.zeros((DEPTH, 128, NPP), f32)
    pb = np.zeros((DEPTH, 128, NPB), f32)
    wabd = np.zeros((DEPTH, 128, 2, 4, 128), f32)
    fm = lambda v, n: v.reshape(n, 128).T
    for l in range(DEPTH):
        pp[l, :, PP_GN:PP_GN + 8] = fm(g("norm_w")[l], 8)
        lcw = g("lru_conv_w")[l]
        pp[l, :, PP_LCW:PP_LCW + 16] = lcw.reshape(4, 4, 128).transpose(2, 1, 0).reshape(128, 16)
        pp[l, :, PP_LCB:PP_LCB + 4] = fm(g("lru_conv_b")[l], 4)
        pp[l, :, PP_LBA:PP_LBA + 4] = fm(g("lru_b_a")[l], 4)
        pp[l, :, PP_LBX:PP_LBX + 4] = fm(g("lru_b_x")[l], 4)
        pp[l, :, PP_LAM:PP_LAM + 4] = fm(g("lru_lambda")[l], 4)
        scw = g("ssd_conv_w")[l]
        pp[l, :, PP_SCW:PP_SCW + 48] = scw.reshape(4, 12, 128).transpose(2, 1, 0).reshape(128, 48)
        pp[l, :, PP_SCB:PP_SCB + 12] = fm(g("ssd_conv_b")[l], 12)
        pb[l, :, PB_GQ:PB_GQ + 64] = g("att_q_norm")[l][None, :]
        pb[l, :, PB_GK:PB_GK + 64] = g("att_k_norm")[l][None, :]
        pb[l, :, PB_DTB:PB_DTB + 16] = g("ssd_dt_bias")[l][None, :]
        pb[l, :, PB_ALOG:PB_ALOG + 16] = g("ssd_a_log")[l][None, :]
        pb[l, :, PB_D:PB_D + 16] = g("ssd_d")[l][None, :]
        pb[l, :, PB_GSSD:PB_GSSD + 1024] = g("ssd_norm")[l][None, :]
        pb[l, :, PB_RB15:PB_RB15 + 8] = g("rel_bias")[15][None, :]
        for a, nm in enumerate(("lru_w_a", "lru_w_x")):
            w = g(nm)[l]
            for gg in range(4):
                wabd[l, 0:64, a, gg, 0:64] = w[2 * gg]
                wabd[l, 64:128, a, gg, 64:128] = w[2 * gg + 1]
    wabd = wabd.reshape(DEPTH, 128, 1024)
    xp = g("x_prompt")
    in_maps = []
    for c in range(NCORES):
        m = {"xp": xp[c * NSP:(c + 1) * NSP], "w_in": w_in, "w_out": w_out, "pp": pp, "pb": pb, "wabd": wabd,
             "rb": g("rel_bias"), "cst": cst}
        if SAMPLE:
            m["xs"] = g("x_sample")[c]
            m["ck"] = g("cache_att_k")[:, c].reshape(DEPTH, PS, 128)
            m["cv"] = g("cache_att_v")[:, c].reshape(DEPTH, PS, 128)
            m["cki"] = g("cache_idx_k")[:, c]
            m["slc"] = g("state_lru_conv")[:, c]
            m["slh"] = g("state_lru_h")[:, c]
            m["ssc"] = g("state_ssd_conv")[:, c]
            m["ssh"] = g("state_ssd_h")[:, c].reshape(DEPTH, 1024, 128)
        in_maps.append({k: np.ascontiguousarray(v) for k, v in m.items()})
    if key not in _PROG_CACHE:
        _PROG_CACHE[key] = build_program(NSP=NSP, TP=TP, SAMPLE=SAMPLE, PS=PS, TS=TS, DEPTH=DEPTH)
    nc = _PROG_CACHE[key]
    res = run_bass_kernel_spmd(nc, in_maps, core_ids=list(range(NCORES)))
    R = res.results
    cat = lambda k, ax: np.concatenate([np.asarray(r[k]) for r in R], axis=ax)
    stk = lambda k, ax: np.stack([np.asarray(r[k]) for r in R], axis=ax)
    B = NCORES * NSP
    outs = [cat("yp", 0)]
    if SAMPLE:
        outs.append(stk("ys", 0))
    outs += [cat("akp", 1).reshape(DEPTH, B, TP, 2, 64), cat("avp", 1).reshape(DEPTH, B, TP, 2, 64), cat("ikp", 1),
             cat("lcp", 1), cat("lhp", 1), cat("scp", 1), cat("shp", 1).reshape(DEPTH, B, 16, 64, 128)]
    if SAMPLE:
        outs += [stk("aks", 1).reshape(DEPTH, NCORES, TS, 2, 64), stk("avs", 1).reshape(DEPTH, NCORES, TS, 2, 64), stk("iks", 1),
                 stk("lcs", 1), stk("lhs", 1), stk("scs", 1), stk("shs", 1).reshape(DEPTH, NCORES, 16, 64, 128)]
    return tuple(np.ascontiguousarray(o, dtype=np.float32) for o in outs)


def kernel(**inputs):
    return _run(inputs, 8, 2, 2048, True, 1024, 64, 2)
```

```python
import numpy as np
from contextlib import ExitStack
import concourse.bass as bass
import concourse.mybir as mybir
from concourse.bass_utils import run_bass_kernel_spmd

F32 = mybir.dt.float32
BF16 = mybir.dt.bfloat16
AF = mybir.ActivationFunctionType
ALU = mybir.AluOpType
AX = mybir.AxisListType


def _region(ap):
    t = ap.tensor
    pat = [(int(s), int(c)) for (s, c) in ap.ap]
    off = int(ap.offset)
    kind = type(t).__name__
    if kind.startswith("DRam"):
        lo = off
        ext = sum((c - 1) * abs(s) for s, c in pat)
        n = 1
        for s, c in pat:
            if s != 0:
                n *= c
        return (t.name, 0, 1, lo, lo + ext + 1, n == ext + 1)
    shp = [int(v) for v in t.shape]
    pstride = 1
    for v in shp[1:]:
        pstride *= v
    p0 = off // pstride
    lo = off % pstride
    npart = pat[0][1]
    ext = sum((c - 1) * abs(s) for s, c in pat[1:])
    n = 1
    for s, c in pat[1:]:
        if s != 0:
            n *= c
    if kind.startswith("PSum"):
        full = (lo == 0 and lo + ext + 1 == pstride and n == ext + 1)
        return (t.name, (p0 // 32) * 32, ((p0 + npart + 31) // 32) * 32, 0, pstride, full and p0 % 32 == 0 and (p0 + npart) % 32 == 0)
    return (t.name, p0, p0 + npart, lo, lo + ext + 1, n == ext + 1)


def _ovl(a, b):
    return a[1] < b[2] and b[1] < a[2] and a[3] < b[4] and b[3] < a[4]


def _covers(a, b):
    return a[5] and a[1] <= b[1] and a[2] >= b[2] and a[3] <= b[3] and a[4] >= b[4]


class _Stop(Exception):
    pass


STOP_AT = [99]
DEBUG_MIX = [False]


CK_OFF = [0]


def drain(gen):
    for _ in gen:
        pass


def interleave_n(*gens):
    live = list(gens)
    while live:
        live = [g for g in live if next(g, "end") != "end"]


def interleave(ga_, gb_):
    a_live = b_live = True
    while a_live or b_live:
        if a_live:
            a_live = next(ga_, "end") != "end"
        if b_live:
            b_live = next(gb_, "end") != "end"


def chk(k):
    if k + CK_OFF[0] > STOP_AT[0]:
        raise _Stop()


class Prog:
    def __init__(self, nc):
        self.nc = nc
        self.stack = ExitStack()
        self.ops = []
        self.wr = {}
        self.rd = {}
        self.finished = False

    def sb(self, name, shape, dt):
        return self.stack.enter_context(self.nc.sbuf_tensor("sb_" + name, list(shape), dt))

    def ps(self, name, shape, dt):
        return self.stack.enter_context(self.nc.psum_tensor("ps_" + name, list(shape), dt))

    def _add(self, eng, fn, outs, ins, is_dma=False):
        idx = len(self.ops)
        deps = {}
        rregs = [_region(a) for a in ins]
        wregs = [_region(a) for a in outs]
        for r in rregs:
            for w in self.wr.get(r[0], ()):
                if _ovl(r, w[1]):
                    deps.setdefault(w[0], set()).add("raw")
        for r in wregs:
            for w in self.wr.get(r[0], ()):
                if _ovl(r, w[1]):
                    deps.setdefault(w[0], set()).add("waw")
            for w in self.rd.get(r[0], ()):
                if _ovl(r, w[1]):
                    deps.setdefault(w[0], set()).add("war")
        for r in wregs:
            lw = self.wr.setdefault(r[0], [])
            lw[:] = [w for w in lw if not _covers(r, w[1])]
            lw.append((idx, r))
            lr = self.rd.get(r[0])
            if lr:
                lr[:] = [w for w in lr if not _covers(r, w[1])]
        for r in rregs:
            self.rd.setdefault(r[0], []).append((idx, r))
        self.ops.append(dict(eng=eng, fn=fn, deps=deps, dma=is_dma, sig=False))
        return idx

    def pe(self, fn, outs, ins):
        return self._add("pe", fn, outs, ins)

    def dve(self, fn, outs, ins):
        return self._add("dve", fn, outs, ins)

    def act(self, fn, outs, ins):
        return self._add("act", fn, outs, ins)

    def pool(self, fn, outs, ins):
        return self._add("pool", fn, outs, ins)

    def dma(self, q, out, in_, **kw):
        return self._add(q, lambda e: e.dma_start(out=out, in_=in_, **kw), [out], [in_], is_dma=True)

    def finish(self):
        nc = self.nc
        ops = self.ops
        engs = ["pe", "dve", "act", "pool", "sp"]
        for i, op in enumerate(ops):
            nd = set()
            for j, kinds in op["deps"].items():
                o = ops[j]
                if o["dma"]:
                    nd.add(j)
                    continue
                if o["eng"] == op["eng"] and not op["dma"]:
                    if op["eng"] == "pe":
                        continue
                nd.add(j)
            op["deps"] = nd
            for j in nd:
                ops[j]["sig"] = True
        esem = {e: self.stack.enter_context(nc.semaphore("s_" + e)) for e in engs}
        nds = {"sp": 40, "pool": 16, "act": 8}
        dsem = {q: [self.stack.enter_context(nc.semaphore("d_%s%d" % (q, k))) for k in range(n)] for q, n in nds.items()}
        dcnt = {q: [0] * n for q, n in nds.items()}
        dnext = {q: 0 for q in nds}
        ecnt = {e: 0 for e in engs}
        for i, op in enumerate(ops):
            if op["dma"]:
                q = op["eng"]
                k = dnext[q] % nds[q]
                dnext[q] += 1
                prev = dcnt[q][k]
                dcnt[q][k] += 16
                op["event"] = (dsem[q][k], dcnt[q][k])
                op["prevev"] = (dsem[q][k], prev) if prev > 0 else None
            elif op["sig"]:
                ecnt[op["eng"]] += 1
                op["event"] = (esem[op["eng"]], ecnt[op["eng"]])
        waited = {e: {} for e in engs}
        per_eng = {e: [] for e in engs}
        for i, op in enumerate(ops):
            e = op["eng"]
            need = {}
            for j in op["deps"]:
                s, v = ops[j]["event"]
                key = id(s)
                if need.get(key, (None, 0))[1] < v:
                    need[key] = (s, v)
            if op["dma"] and op["prevev"] is not None:
                s, v = op["prevev"]
                key = id(s)
                if need.get(key, (None, 0))[1] < v:
                    need[key] = (s, v)
            waits = []
            for key, (s, v) in need.items():
                if waited[e].get(key, 0) >= v:
                    continue
                waited[e][key] = v
                waits.append((s, v))
            op["waits"] = waits
            per_eng[e].append(op)
        self.n_waits = sum(len(o["waits"]) for o in ops)
        final = []
        for q in nds:
            for k in range(nds[q]):
                if dcnt[q][k] > 0:
                    final.append((dsem[q][k], dcnt[q][k]))

        def emit(ename, e):
            for op in per_eng[ename]:
                for (s, v) in op["waits"]:
                    e.wait_ge(s, v)
                ins = op["fn"](e)
                if op["dma"]:
                    ins.then_inc(op["event"][0], 16)
                elif op["sig"]:
                    ins.then_inc(op["event"][0], 1)
            if ename == "sp":
                for (s, v) in final:
                    e.wait_ge(s, v)

        with nc.Block() as block:
            @block.tensor
            def _(e):
                emit("pe", e)

            @block.vector
            def _(e):
                emit("dve", e)

            @block.scalar
            def _(e):
                emit("act", e)

            @block.gpsimd
            def _(e):
                emit("pool", e)

            @block.sync
            def _(e):
                emit("sp", e)
        self.stack.close()
        self.finished = True


D_MODEL = 1024
D_IN = 5204
D_MIX = 2048
EPS = 1e-6
NEG = -30000.0
C_XL, C_GL, C_Q, C_K, C_GA, C_QI, C_KI, C_Z, C_XBC, C_DT = 0, 512, 1024, 1536, 1792, 2304, 2560, 2628, 3652, 5188
NPP = 100
NPB = 1208
NCST = 1152
PB_GQ, PB_GK, PB_DTB, PB_ALOG, PB_D, PB_GSSD, PB_RB15 = 0, 64, 128, 144, 160, 176, 1200
PP_GN, PP_LCW, PP_LCB, PP_LBA, PP_LBX, PP_LAM, PP_SCW, PP_SCB = 0, 8, 24, 28, 32, 36, 40, 88


def build_program(NSP=2, TP=2048, SAMPLE=True, PS=1024, TS=64, DEPTH=2, TG=256, NBIS=16):
    nc = bass.Bass("TRN2", target_bir_lowering=False)
    pg = Prog(nc)
    dt_in = lambda name, shape: nc.dram_tensor(name, list(shape), F32, kind="ExternalInput").ap()
    dt_out = lambda name, shape: nc.dram_tensor(name, list(shape), F32, kind="ExternalOutput").ap()
    dt_tmp = lambda name, shape: nc.dram_tensor(name, list(shape), F32, kind="Internal").ap()
    I = {}
    I["xp"] = dt_in("xp", [NSP, TP, D_MODEL])
    I["w_in"] = dt_in("w_in", [DEPTH, D_MODEL, D_IN])
    I["w_out"] = dt_in("w_out", [DEPTH, D_MIX, D_MODEL])
    I["pp"] = dt_in("pp", [DEPTH, 128, NPP])
    I["pb"] = dt_in("pb", [DEPTH, 128, NPB])
    I["wabd"] = dt_in("wabd", [DEPTH, 128, 2 * 4 * 128])
    I["rb"] = dt_in("rb", [32, 8])
    I["cst"] = dt_in("cst", [128, NCST])
    O = {}
    O["yp"] = dt_out("yp", [NSP, TP, D_MODEL])
    O["akp"] = dt_out("akp", [DEPTH, NSP, TP, 128])
    O["avp"] = dt_out("avp", [DEPTH, NSP, TP, 128])
    O["ikp"] = dt_out("ikp", [DEPTH, NSP, TP, 64])
    O["lcp"] = dt_out("lcp", [DEPTH, NSP, 3, 512])
    O["lhp"] = dt_out("lhp", [DEPTH, NSP, 512])
    O["scp"] = dt_out("scp", [DEPTH, NSP, 3, 1536])
    O["shp"] = dt_out("shp", [DEPTH, NSP, 1024, 128])
    xmid_p = dt_tmp("xmid_p", [NSP, TP, D_MODEL])
    if DEBUG_MIX[0]:
        O["dbg"] = nc.dram_tensor("dbg", [16, 128, TP], BF16, kind="ExternalOutput").ap()
    vecd = dt_tmp("vecd", [8, 384])
    if SAMPLE:
        I["xs"] = dt_in("xs", [TS, D_MODEL])
        I["ck"] = dt_in("ck", [DEPTH, PS, 128])
        I["cv"] = dt_in("cv", [DEPTH, PS, 128])
        I["cki"] = dt_in("cki", [DEPTH, PS, 64])
        I["slc"] = dt_in("slc", [DEPTH, 3, 512])
        I["slh"] = dt_in("slh", [DEPTH, 512])
        I["ssc"] = dt_in("ssc", [DEPTH, 3, 1536])
        I["ssh"] = dt_in("ssh", [DEPTH, 1024, 128])
        O["ys"] = dt_out("ys", [TS, D_MODEL])
        O["aks"] = dt_out("aks", [DEPTH, TS, 128])
        O["avs"] = dt_out("avs", [DEPTH, TS, 128])
        O["iks"] = dt_out("iks", [DEPTH, TS, 64])
        O["lcs"] = dt_out("lcs", [DEPTH, 3, 512])
        O["lhs"] = dt_out("lhs", [DEPTH, 512])
        O["scs"] = dt_out("scs", [DEPTH, 3, 1536])
        O["shs"] = dt_out("shs", [DEPTH, 1024, 128])
        xmid_s = dt_tmp("xmid_s", [TS, D_MODEL])

    LMAX = max(TP, (PS + TS) if SAMPLE else 0)
    LMAX = ((LMAX + 127) // 128) * 128
    NKB = LMAX // 128
    sb, ps = pg.sb, pg.ps
    win = sb("win", [128, 8, D_IN], BF16)
    wout = sb("wout", [128, 16, D_MODEL], BF16)
    ppt = sb("ppt", [128, NPP], F32)
    pbt = sb("pbt", [128, NPB], F32)
    wabd = sb("wabdb", [128, 8, 128], BF16)
    cst = sb("cst", [128, 768], F32)
    ident = cst[:, 0:128]
    tri = cst[:, 128:256]
    astr = cst[:, 256:384]
    dmask = cst[:, 640:768]
    cbf = sb("cbf", [128, 6, 128], BF16)
    trib, astrb = cbf[:, 4, :], cbf[:, 5, :]
    RB = sb("RB", [128, 2, 512], BF16)
    JK = sb("JK", [128, LMAX], mybir.dt.uint8)
    dAs = sb("dAs", [128, 3, 16], BF16)
    identb, j128b, j64b, onesb = cbf[:, 0, :], cbf[:, 1, :], cbf[:, 2, :], cbf[:, 3, :]
    hk = sb("hk", [128, 2, 8, 128], BF16)
    p2row = sb("p2row", [128, NBIS + 1], F32)
    rbt = sb("rbt", [32, 8], F32)
    rb15row = sb("rb15row", [1, 8, 128], BF16)
    vecs = sb("vecs", [8, 384], F32)
    drv = sb("drv", [128, 64], F32)
    c1 = drv[:, 0:4]
    aneg = drv[:, 4:20]
    gq8 = sb("gq8", [128, 64], F32)
    hT = sb("hT", [128, 8, TG], BF16)
    mixT = sb("mixT", [128, 16, TG], BF16)
    khT = sb("khT", [128, LMAX], BF16)
    vtok = sb("vtok", [128, NKB, 128], BF16)
    kiT = sb("kiT", [128, LMAX], BF16)
    qhT = sb("qhT", [128, 4, TG], BF16)
    qiT = sb("qiT", [128, 2, TG], BF16)
    gaT = sb("gaT", [128, 4, TG], BF16)
    lhist = sb("lhist", [128, 4, 3], F32)
    shist = sb("shist", [128, 12, 3], F32)
    lh = sb("lh", [128, 4], F32)
    xbcT = sb("xbcT", [128, 12, TG], BF16)
    sz = sb("sz", [128, 2, 1024], BF16)
    hst = sb("hst", [128, 1024], F32)
    hsb = sb("hsb", [128, 1024], BF16)
    sm = sb("sm", [128, 256], F32)
    smb = sb("smb", [128, 2, 80], F32)
    kiw = sb("kiw", [128, 2, 68], F32)
    dtt = sb("dtt", [128, 2, 16], F32)
    dAt = sb("dAt", [128, 2, 16], F32)
    A32 = sb("A32", [128, 3328], F32)
    AB = sb("AB", [128, 4608], BF16)
    pA = ps("pA", [128, 512], F32)
    pB = ps("pB", [128, 512], F32)
    pC = ps("pC", [128, 512], F32)
    pD = ps("pD", [128, 512], F32)
    pE = ps("pE", [128, 512], F32)
    pF = ps("pF", [128, 512], F32)
    pG = ps("pG", [128, 512], F32)
    pH = ps("pH", [128, 512], F32)

    act, dve, pe, pool, dma = pg.act, pg.dve, pg.pe, pg.pool, pg.dma

    def A_(fn, out, ins):
        return act(fn, [out] if not isinstance(out, list) else out, ins)

    def acopy(out, in_):
        act(lambda e: e.activation(out=out, in_=in_, func=AF.Copy), [out], [in_])

    def afunc(out, in_, func, bias=None, scale=None, accum=None):
        kw = {}
        reads = [in_]
        outs = [out]
        if bias is not None:
            kw["bias"] = bias
            if not isinstance(bias, float):
                reads.append(bias)
        if scale is not None:
            kw["scale"] = scale
            if not isinstance(scale, float):
                reads.append(scale)
        if accum is not None:
            kw["accum_out"] = accum
            outs.append(accum)
        act(lambda e: e.activation(out=out, in_=in_, func=func, **kw), outs, reads)

    def vtt(out, a, b, op):
        dve(lambda e: e.tensor_tensor(out=out, in0=a, in1=b, op=op), [out], [a, b])

    def vts(out, a, s1, op0, s2=None, op1=None, accum=None):
        reads = [a] + [s for s in (s1, s2) if s is not None and not isinstance(s, float)]
        outs = [out] + ([accum] if accum is not None else [])
        kw = {}
        if op1 is not None:
            kw["op1"] = op1
        if accum is not None:
            kw["accum_out"] = accum
        dve(lambda e: e.tensor_scalar(out=out, in0=a, scalar1=s1, scalar2=s2, op0=op0, **kw), outs, reads)

    def vstt(out, a, s, b, op0, op1):
        reads = [a, b] + ([s] if not isinstance(s, float) else [])
        dve(lambda e: e.scalar_tensor_tensor(out=out, in0=a, scalar=s, in1=b, op0=op0, op1=op1), [out], reads)

    def vcopy(out, in_):
        dve(lambda e: e.tensor_copy(out=out, in_=in_), [out], [in_])

    def vscan(out, d0, d1, init):
        reads = [d0, d1] + ([init] if not isinstance(init, float) else [])
        dve(lambda e: e.tensor_tensor_scan(out=out, data0=d0, data1=d1, initial=init, op0=ALU.mult, op1=ALU.add), [out], reads)

    def vreduce(out, in_, op, absval=False):
        if absval:
            dve(lambda e: e.tensor_reduce(out=out, in_=in_, axis=AX.X, op=op, apply_absolute_value=True), [out], [in_])
        else:
            dve(lambda e: e.tensor_reduce(out=out, in_=in_, axis=AX.X, op=op), [out], [in_])

    def vmemset(ap, val):
        dve(lambda e: e.memset(ap, val), [ap], [])

    def ptt(out, a, b, op):
        pool(lambda e: e.tensor_tensor(out=out, in0=a, in1=b, op=op), [out], [a, b])

    def split3(dst, src32, tmpa, tmpb):
        vcopy(dst[:, 0, :], src32)
        vtt(tmpa, src32, dst[:, 0, :], ALU.subtract)
        vcopy(dst[:, 1, :], tmpa)
        vtt(tmpb, tmpa, dst[:, 1, :], ALU.subtract)
        vcopy(dst[:, 2, :], tmpb)

    def rstd_pow(out, ss, n, tmp):
        afunc(tmp, ss, AF.Ln, scale=1.0 / n, bias=EPS)
        afunc(out, tmp, AF.Exp, scale=-0.5)

    def sigm(buf, src, scale=-1.0, bias=None):
        afunc(buf, src, AF.Exp, scale=scale, bias=bias)
        afunc(buf, buf, AF.Ln, bias=1.0)
        afunc(buf, buf, AF.Exp, scale=-1.0)

    def vrecip(out, in_):
        dve(lambda e: e.reciprocal(out=out, in_=in_), [out], [in_])

    def mm(out, lhsT, rhs, start, stop=True):
        pe(lambda e: e.matmul(out, lhsT=lhsT, rhs=rhs, start=start, stop=stop), [out], [lhsT, rhs])

    def tr(out, in_, idt):
        pe(lambda e: e.transpose(out, in_, idt), [out], [in_, idt])

    def bc(ap, axis, n):
        a = ap.unsqueeze(axis)
        shp = list(a.shape)
        shp[axis] = n
        return a.broadcast_to(shp)

    try:
      _body(locals())
    except _Stop:
      pass
    pg.finish()
    return nc


def _body(env):
    globals().update({k: v for k, v in env.items() if not k.startswith("__")})
    dma("sp", cst[:], I["cst"][:, 0:768])
    dma("sp", A32[0:32, 0:384], I["cst"][0:32, 768:1152])
    dma("sp", rbt[:], I["rb"])
    vcopy(cbf[:, 0, :], cst[:, 0:128])
    vcopy(cbf[:, 1, :], cst[:, 384:512])
    vcopy(cbf[:, 2, :], cst[:, 512:640])
    vmemset(cbf[:, 3, :], 1.0)
    vcopy(cbf[:, 4, :], cst[:, 128:256])
    vcopy(cbf[:, 5, :], cst[:, 256:384])
    mm(pA[0:8, 0:384], rbt[:, :], A32[0:32, 0:384], True)
    vcopy(vecs[:], pA[0:8, 0:384])
    vcopy(rbt[0:8, 0:1], vecs[:, 0:1])
    vts(vecs[:], vecs[:], rbt[0:8, 0:1], ALU.subtract)
    dma("sp", vecd, vecs[:])
    hktmp = A32[:, 0:1024].rearrange("p (h s) -> p h s", h=8)
    for k in range(NBIS + 1):
        vmemset(p2row[:, k:k + 1], 2.0 ** -k)
    for si, (TTq, dl) in enumerate([(128, 0), (128, -128)]):
        base = dl - TTq + 256
        src = bass.AP(tensor=vecd.tensor, offset=base, ap=[[1, TTq], [384, 8], [1, 128]])
        dma("sp", hktmp[:TTq], src)
        vcopy(hk[:TTq, si, :, :], hktmp[:TTq])

    CK_OFF[0] = 0
    chk(1)
    seqs = []
    for q in range(NSP):
        seqs.append(dict(kind="p", idx=q, T=TP, P=0))
    if SAMPLE:
        seqs.append(dict(kind="s", idx=0, T=TS, P=PS))

    for l in range(DEPTH):
        for kc in range(8):
            for hf in range(2):
                c0, c1_ = hf * 2602, (hf + 1) * 2602
                dma("pool", win[:, kc, c0:c1_], I["w_in"][l, kc * 128:(kc + 1) * 128, c0:c1_])
        for ec in range(16):
            dma("pool", wout[:, ec, :], I["w_out"][l, ec * 128:(ec + 1) * 128, :])
        dma("sp", ppt[:], I["pp"][l])
        dma("sp", pbt[:], I["pb"][l])
        dma("pool", wabd[:].rearrange("p a b -> p (a b)"), I["wabd"][l])
        afunc(drv[:, 20:24], ppt[:, PP_LAM:PP_LAM + 4], AF.Exp, scale=-1.0)
        afunc(drv[:, 24:28], drv[:, 20:24], AF.Ln, bias=1.0)
        vts(c1, drv[:, 24:28], -8.0, ALU.mult)
        afunc(drv[:, 28:44], pbt[:, PB_ALOG:PB_ALOG + 16], AF.Exp)
        vts(aneg, drv[:, 28:44], -1.0, ALU.mult)
        vts(gq8[:], pbt[:, PB_GQ:PB_GQ + 64], 0.125, ALU.mult)
        vts(drv[:, 48:52], ppt[:, PP_LBA:PP_LBA + 4], -1.0, ALU.mult)
        vts(drv[:, 52:56], ppt[:, PP_LBX:PP_LBX + 4], -1.0, ALU.mult)
        vcopy(rb15row[0:1, :, :], bc(pbt[0:1, PB_RB15:PB_RB15 + 8], 2, 128))

        chk(2)
        for sq in seqs:
            T, P0 = sq["T"], sq["P"]
            isS = sq["kind"] == "s"
            qi_ = sq["idx"]
            L = P0 + T
            ktop = min(256, L // 4)
            if isS:
                xin = I["xs"] if l == 0 else xmid_s
                xout = O["ys"] if l == DEPTH - 1 else xmid_s
                o_ak, o_av, o_ik = O["aks"][l], O["avs"][l], O["iks"][l]
                o_lc, o_lh, o_sc, o_sh = O["lcs"][l], O["lhs"][l], O["scs"][l], O["shs"][l]
            else:
                xin = I["xp"][qi_] if l == 0 else xmid_p[qi_]
                xout = O["yp"][qi_] if l == DEPTH - 1 else xmid_p[qi_]
                o_ak, o_av, o_ik = O["akp"][l, qi_], O["avp"][l, qi_], O["ikp"][l, qi_]
                o_lc, o_lh, o_sc, o_sh = O["lcp"][l, qi_], O["lhp"][l, qi_], O["scp"][l, qi_], O["shp"][l, qi_]
            if isS:
                for g in range(4):
                    dma("sp", lhist[:, g, :], I["slc"][l].rearrange("j (g p) -> p g j", p=128)[:, g, :], allow_slow_non_contiguous=True)
                for g in range(12):
                    dma("sp", shist[:, g, :], I["ssc"][l].rearrange("j (g p) -> p g j", p=128)[:, g, :], allow_slow_non_contiguous=True)
                dma("sp", lh[:], I["slh"][l].rearrange("(g p) -> p g", p=128), allow_slow_non_contiguous=True)
                stt = A32[:, 0:1024].rearrange("p (c n) -> p c n", c=8)
                dma("sp", stt, I["ssh"][l].rearrange("(c p) n -> p c n", p=128))
                sp3 = AB[:, 0:3072].rearrange("p (k n) -> p k n", k=3)
                split3(sp3, A32[:, 0:1024], A32[:, 1024:2048], A32[:, 2048:3072])
                for c in range(8):
                    pbank = pA if c < 4 else pB
                    for k in range(3):
                        mm(pbank[:, (c % 4) * 128:(c % 4 + 1) * 128], sp3[:, k, c * 128:(c + 1) * 128], identb, c % 4 == 0 and k == 0, c % 4 == 3 and k == 2)
                vcopy(hst[:, 0:512], pA[:, :])
                vcopy(hst[:, 512:1024], pB[:, :])
                acopy(hsb[:, :], hst[:, :])
                nb = P0 // 128
                ckb = AB[:, 0:nb * 128].rearrange("p (c n) -> p c n", c=nb)
                kid = AB[:, nb * 128:2 * nb * 128].rearrange("p (c n) -> p c n", c=nb)
                dma("pool", ckb, I["ck"][l].rearrange("(c p) n -> p c n", p=128))
                dma("pool", vtok[:, 0:nb, :], I["cv"][l].rearrange("(c p) n -> p c n", p=128))
                dma("pool", kid[:, :, 0:64], I["cki"][l].rearrange("(c p) n -> p c n", p=128))
                dma("pool", kid[:, :, 64:128], I["cki"][l].rearrange("(c p) n -> p c n", p=128))
                for c in range(nb):
                    mm(pC[:, (c % 4) * 128:(c % 4 + 1) * 128], ckb[:, c, :], identb, c % 4 == 0, c % 4 == 3 or c == nb - 1)
                    mm(pD[:, (c % 4) * 128:(c % 4 + 1) * 128], kid[:, c, :], identb, c % 4 == 0, c % 4 == 3 or c == nb - 1)
                    if c % 4 == 3 or c == nb - 1:
                        c0 = (c // 4) * 4
                        n_ = c - c0 + 1
                        vcopy(khT[:, c0 * 128:(c0 + n_) * 128], pC[:, 0:n_ * 128])
                        acopy(kiT[:, c0 * 128:(c0 + n_) * 128], pD[:, 0:n_ * 128])
            else:
                vmemset(lhist[:], 0.0)
                vmemset(shist[:], 0.0)
                vmemset(lh[:], 0.0)
                vmemset(hst[:], 0.0)
                vmemset(hsb[:], 0.0)

            CK_OFF[0] = 10 if isS else 0
            chk(3)
            ngroups = (T + TG - 1) // TG
            for gi in range(ngroups):
                t0 = gi * TG
                NV = min(TG, T - t0)
                NT = ((NV + 127) // 128) * 128
                TT = 128
                ntile = NT // TT
                def gen_p1(ti):
                    r0 = t0 + ti * TT
                    TV = min(TT, NV - ti * TT)
                    xb_ = (ti % 2) * 1024
                    xt32 = A32[:TT, xb_:xb_ + 1024]
                    if TV < TT:
                        vmemset(A32[TV:TT, xb_:xb_ + 1024], 0.0)
                    dma("sp", A32[:TV, xb_:xb_ + 1024], xin[r0:r0 + TV, :])
                    yield
                    ab_ = (ti % 2) * 2048
                    junk = AB[:TT, ab_:ab_ + 1024]
                    xn = AB[:TT, ab_ + 1024:ab_ + 2048]
                    bk0, bk1 = (pA, pB) if ti % 2 == 0 else (pC, pD)
                    ssq = sm[:TT, ti:ti + 1]
                    afunc(junk, xt32, AF.Square, accum=ssq)
                    yield
                    rstd_pow(sm[:TT, 4 + ti:5 + ti], ssq, D_MODEL, sm[:TT, 2 + ti:3 + ti])
                    yield
                    vts(xn, xt32, sm[:TT, 4 + ti:5 + ti], ALU.mult)
                    yield
                    for kc in range(8):
                        pb_ = bk0 if kc < 4 else bk1
                        mm(pb_[:, (kc % 4) * TT:(kc % 4 + 1) * TT], xn[:, kc * 128:(kc + 1) * 128], identb[:TT, :TT], kc % 4 == 0, kc % 4 == 3)
                    yield
                    for hf in range(2):
                        pb_ = bk0 if hf == 0 else bk1
                        vtt(hT[:, hf * 4:(hf + 1) * 4, ti * TT:(ti + 1) * TT], pb_[:, 0:4 * TT].rearrange("p (c t) -> p c t", c=4),
                            bc(ppt[:, PP_GN + hf * 4:PP_GN + hf * 4 + 4], 2, TT), ALU.mult)
                    yield

                if ntile == 2:
                    interleave(gen_p1(0), gen_p1(1))
                else:
                    drain(gen_p1(0))

                chk(4)
                def proj_fm(pbank, c0, M, prow=0):
                    for kc in range(8):
                        mm(pbank[prow:prow + M, :NT], win[:, kc, c0:c0 + M], hT[:, kc, :NT], kc == 0, kc == 7)

                def conv(pbank, hist, g, wcol, bcol, out32, roff=0, ppt=ppt):
                    raw = A32[:, roff:roff + 3 + NT]
                    vcopy(raw[:, 0:3], hist[:, g, :])
                    acopy(raw[:, 3:3 + NT], pbank[:, :NT])
                    vcopy(hist[:, g, :], raw[:, NV:NV + 3])
                    vts(out32, raw[:, 0:NT], ppt[:, wcol:wcol + 1], ALU.mult, ppt[:, bcol:bcol + 1], ALU.add)
                    for j in range(1, 4):
                        vstt(out32, raw[:, j:j + NT], ppt[:, wcol + j:wcol + j + 1], out32, ALU.mult, ALU.add)
                    return raw

                def gen_lru(g):
                    lbase = (g % 2) * 1664
                    f = lambda k: A32[:, lbase + 260 + k * 256:lbase + 260 + k * 256 + NT]
                    gC, gD = (pC, pD) if g % 2 == 0 else (pG, pH)
                    xc, rr, ii, aa, s_ = [f(k) for k in range(5)]
                    gx, bb, hseq, sg = ii, s_, xc, rr
                    xcb = AB[:, 2048 + (g % 2) * 256:2048 + (g % 2) * 256 + NT]
                    pb_ = pA if g % 2 == 0 else pB
                    proj_fm(pb_, C_XL + g * 128, 128)
                    raw = conv(pb_, lhist, g, PP_LCW + g * 4, PP_LCB + g, xc, lbase)
                    if gi == ngroups - 1:
                        dma("sp", o_lc.rearrange("j (g p) -> p g j", p=128)[:, g, :], raw[:, NV:NV + 3], allow_slow_non_contiguous=True)
                    acopy(xcb, xc)
                    yield
                    mm(gC[:, :NT], wabd[:, g, :], xcb, True)
                    mm(gD[:, :NT], wabd[:, 4 + g, :], xcb, True)
                    sigm(rr, gC[:, :NT], -1.0, drv[:, 48 + g:49 + g])
                    yield
                    sigm(ii, gD[:, :NT], -1.0, drv[:, 52 + g:53 + g])
                    yield
                    afunc(aa, rr, AF.Exp, scale=c1[:, g:g + 1])
                    afunc(s_, aa, AF.Square)
                    afunc(s_, s_, AF.Ln, scale=-1.0, bias=1.0)
                    afunc(s_, s_, AF.Exp, scale=0.5)
                    yield
                    vtt(gx, ii, xc, ALU.mult)
                    vtt(bb, s_, gx, ALU.mult)
                    vscan(hseq, aa, bb, lh[:, g:g + 1])
                    vcopy(lh[:, g:g + 1], hseq[:, NV - 1:NV])
                    yield
                    pg_ = pE if g % 2 == 0 else pF
                    proj_fm(pg_, C_GL + g * 128, 128)
                    sigm(sg, pg_[:, :NT])
                    yield
                    vtt(sg, sg, pg_[:, :NT], ALU.mult)
                    vtt(mixT[:, g, :NT], hseq, sg, ALU.mult)
                    yield

                interleave(gen_lru(0), gen_lru(1))
                interleave(gen_lru(2), gen_lru(3))
                if gi == ngroups - 1:
                    dma("sp", o_lh.rearrange("(g p) -> p g", p=128), lh[:], allow_slow_non_contiguous=True)

                chk(5)
                for g in range(4):
                    pb_ = pA if g % 2 == 0 else pB
                    proj_fm(pb_, C_GA + g * 128, 128)
                    gth = A32[:, 512 + (g % 2) * 256:512 + (g % 2) * 256 + NT]
                    sigm(gth, pb_[:, :NT])
                    vtt(gaT[:, g, :NT], gth, pb_[:, :NT], ALU.mult)
                for g in range(2):
                    pb_ = pE if g % 2 == 0 else pF
                    proj_fm(pb_, C_QI + g * 128, 128)
                    vcopy(qiT[:, g, :NT], pb_[:, :NT])
                proj_fm(pA, C_KI, 64, 0)
                proj_fm(pA, C_KI, 64, 64)
                acopy(kiT[:, P0 + t0:P0 + t0 + NT], pA[:, :NT])

                def gen_conv(g):
                    pb_ = (pE, pF, pG, pH)[g % 4]
                    proj_fm(pb_, C_XBC + g * 128, 128)
                    roff = (g % 4) * 772
                    acc = A32[:, roff + 260:roff + 260 + NT]
                    raw = conv(pb_, shist, g, PP_SCW + g * 4, PP_SCB + g, acc, roff)
                    if gi == ngroups - 1:
                        dma("sp", o_sc.rearrange("j (g p) -> p g j", p=128)[:, g, :], raw[:, NV:NV + 3], allow_slow_non_contiguous=True)
                    yield
                    cth = A32[:, roff + 516:roff + 516 + NT]
                    sigm(cth, acc)
                    yield
                    vtt(xbcT[:, g, :NT], cth, acc, ALU.mult)
                    yield

                for g4 in range(0, 12, 4):
                    interleave_n(*[gen_conv(g4 + k) for k in range(4)])

                chk(6)
                def gen_p2b(ti):
                    tsl = slice(ti * TT, (ti + 1) * TT)
                    r0 = t0 + ti * TT
                    TV = min(TT, NV - ti * TT)
                    kb = (P0 + r0) // 128
                    koff = (P0 + r0) % 128
                    qC, qD, qE, qF = (pC, pD, pE, pF) if ti % 2 == 0 else (pA, pB, pG, pH)
                    tb = (ti % 2) * 1536
                    for kc in range(8):
                        mm(qC[:TT, 0:512], hT[:, kc, tsl], win[:, kc, C_Q:C_Q + 512], kc == 0, kc == 7)
                    for (o0, c0, n_) in ((0, C_K, 256), (256, C_KI, 68), (324, C_DT, 16)):
                        for kc in range(8):
                            mm(qD[:TT, o0:o0 + n_], hT[:, kc, tsl], win[:, kc, c0:c0 + n_], kc == 0, kc == 7)
                    yield
                    for hf in range(2):
                        pz = qE if hf == 0 else qF
                        for kc in range(8):
                            mm(pz[:TT, :], hT[:, kc, tsl], win[:, kc, C_Z + hf * 512:C_Z + (hf + 1) * 512], kc == 0, kc == 7)
                        zth = A32[:TT, tb + 512 + hf * 512:tb + 1024 + hf * 512]
                        sigm(zth, pz[:TT, :])
                        vtt(sz[:TT, ti, hf * 512:(hf + 1) * 512], zth, pz[:TT, :], ALU.mult)
                        yield
                    sqt = A32[:TT, tb + 512:tb + 1024]
                    qn = A32[:TT, tb + 1024:tb + 1536]
                    kv32 = A32[:TT, tb + 1536:tb + 1792]
                    qhat = AB[:TT, 2304 + (ti % 2) * 640:2816 + (ti % 2) * 640]
                    khat = AB[:TT, 2816 + (ti % 2) * 640:2944 + (ti % 2) * 640]
                    afunc(sqt, qC[:TT, :], AF.Square)
                    vreduce(smb[:TT, ti % 2, 8:16], sqt.rearrange("p (h d) -> p h d", h=8), ALU.add)
                    rstd_pow(smb[:TT, ti % 2, 24:32], smb[:TT, ti % 2, 8:16], 64, smb[:TT, ti % 2, 16:24])
                    vtt(qn.rearrange("p (h d) -> p h d", h=8), qC[:TT, :].rearrange("p (h d) -> p h d", h=8), bc(smb[:TT, ti % 2, 24:32], 2, 64), ALU.mult)
                    vtt(qhat.rearrange("p (m g d) -> p g m d", m=4, g=2), qn.rearrange("p (g m d) -> p g m d", g=2, m=4),
                        bc(bc(gq8[:TT, :], 1, 4), 1, 2), ALU.mult)
                    yield
                    afunc(sqt[:, 0:128], qD[:TT, 0:128], AF.Square)
                    vreduce(smb[:TT, ti % 2, 32:34], sqt[:, 0:128].rearrange("p (h d) -> p h d", h=2), ALU.add)
                    rstd_pow(smb[:TT, ti % 2, 36:38], smb[:TT, ti % 2, 32:34], 64, smb[:TT, ti % 2, 34:36])
                    vtt(qn[:, 0:128].rearrange("p (h d) -> p h d", h=2), qD[:TT, 0:128].rearrange("p (h d) -> p h d", h=2), bc(smb[:TT, ti % 2, 36:38], 2, 64), ALU.mult)
                    vtt(kv32[:, 0:128].rearrange("p (h d) -> p h d", h=2), qn[:, 0:128].rearrange("p (h d) -> p h d", h=2),
                        bc(pbt[:TT, PB_GK:PB_GK + 64], 1, 2), ALU.mult)
                    acopy(khat, kv32[:, 0:128])
                    acopy(kv32[:, 128:256], qD[:TT, 128:256])
                    acopy(vtok[koff:koff + TT, kb, :], qD[:TT, 128:256])
                    dma("sp", o_ak[r0:r0 + TV, :], kv32[:TV, 0:128])
                    dma("sp", o_av[r0:r0 + TV, :], kv32[:TV, 128:256])
                    vcopy(kiw[:TT, ti, :], qD[:TT, 256:324])
                    dma("sp", o_ik[r0:r0 + TV, :], kiw[:TV, ti, 0:64])
                    yield
                    vtt(smb[:TT, ti % 2, 40:56], qD[:TT, 324:340], pbt[:TT, PB_DTB:PB_DTB + 16], ALU.add)
                    afunc(smb[:TT, ti % 2, 56:72], smb[:TT, ti % 2, 40:56], AF.Exp)
                    afunc(dtt[:TT, ti, :], smb[:TT, ti % 2, 56:72], AF.Ln, bias=1.0)
                    if TV < TT:
                        vmemset(dtt[TV:TT, ti, :], 0.0)
                    vtt(dAt[:TT, ti, :], dtt[:TT, ti, :], aneg[:TT, :], ALU.mult)
                    yield
                    mm(qE[:, 0:TT], khat, identb[:TT, :TT], True)
                    vcopy(khT[:, P0 + r0:P0 + r0 + TT], qE[:, 0:TT])
                    for m in range(4):
                        mm(qF[:, m * TT:(m + 1) * TT], qhat[:, m * 128:(m + 1) * 128], identb[:TT, :TT], m == 0, m == 3)
                    vcopy(qhT[:, :, tsl], qF[:, 0:4 * TT].rearrange("p (c t) -> p c t", c=4))
                    yield

                if ntile == 2:
                    interleave(gen_p2b(0), gen_p2b(1))
                else:
                    drain(gen_p2b(0))

                chk(7)
                for ti in range(ntile):
                    tsl = slice(ti * TT, (ti + 1) * TT)
                    if ti == 1:
                        chk(7.9)
                    dA = dAt[:TT, ti, :]
                    dtv = dtt[:TT, ti, :]
                    split3(dAs[:TT], dA, sm[:TT, 176:192], sm[:TT, 192:208])
                    for k in range(3):
                        mm(pF[:TT, 0:16], trib[:TT, :TT], dAs[:TT, k, :], k == 0, k == 2)
                    for k in range(3):
                        mm(pF[:TT, 16:32], astrb[:TT, :TT], dAs[:TT, k, :], k == 0, k == 2)
                    for k in range(3):
                        mm(pF[:, 32:48], onesb[:TT, :], dAs[:TT, k, :], k == 0, k == 2)
                    ecum = sm[:TT, 80:96]
                    toend = sm[:TT, 96:112]
                    dec = sm[:, 112:128]
                    afunc(sm[:TT, 80:112], pF[:TT, 0:32], AF.Exp)
                    afunc(dec, pF[:, 32:48], AF.Exp)
                    chk(7.1)
                    for c in range(8):
                        pb_ = pC if c < 4 else pD
                        mm(pb_[:TT, (c % 4) * 128:(c % 4 + 1) * 128], xbcT[:, c, tsl], identb, c % 4 == 0, c % 4 == 3)
                    xt_ = AB[:TT, 0:1024]
                    x2_ = AB[:TT, 1024:2048]
                    xD = A32[:TT, 0:1024]
                    for hf in range(2):
                        pTv = (pC if hf == 0 else pD)[:TT, :].rearrange("p (h d) -> p h d", h=8)
                        hs_ = slice(hf * 512, (hf + 1) * 512)
                        vtt(xt_[:, hs_].rearrange("p (h d) -> p h d", h=8), pTv, bc(dtv[:, hf * 8:(hf + 1) * 8], 2, 64), ALU.mult)
                        vtt(xD[:, hs_].rearrange("p (h d) -> p h d", h=8), pTv, bc(pbt[:TT, PB_D + hf * 8:PB_D + hf * 8 + 8], 2, 64), ALU.mult)
                    vtt(x2_.rearrange("p (h d) -> p h d", h=16), xt_.rearrange("p (h d) -> p h d", h=16), bc(toend, 2, 64), ALU.mult)
                    def gen_ssd(g, ti=ti, tsl=tsl, xt_=xt_, x2_=x2_, xD=xD, ecum=ecum, dec=dec):
                        eb = 2048 + g * 1280
                        E = AB[:TT, eb:eb + 8 * TT].rearrange("p (h l) -> p h l", h=8)
                        WT = E
                        G = AB[:TT, eb + 1024:eb + 1024 + TT]
                        bmtok = AB[:TT, eb + 1152:eb + 1280]
                        ycb = AB[:TT, eb:eb + 512]
                        t1 = A32[:TT, 1024 + g * 1024:1536 + g * 1024]
                        yz = A32[:TT, 1536 + g * 1024:2048 + g * 1024]
                        dbk = (pA, pB) if g == 0 else (pG, pH)
                        cbk = pC if g == 0 else pF
                        ydk = pD if g == 0 else pG
                        yok = pE if g == 0 else pH
                        ytk = dbk[0]
                        hpb = 512 // TT
                        for bk in range(8 // hpb):
                            pb_ = dbk[bk % 2]
                            for k in range(2):
                                Rk = RB[:TT, (g * 2 + bk + k) % 2, 0:hpb * TT]
                                h0_ = g * 8 + bk * hpb
                                vtt(Rk.rearrange("p (h l) -> p h l", h=hpb), bc(dAs[:TT, k, h0_:h0_ + hpb], 2, TT), bc(trib[:TT, :TT], 1, hpb), ALU.mult)
                                mm(pb_[:TT, 0:hpb * TT], astrb[:TT, :TT], Rk, k == 0, k == 1)
                            afunc(E[:, bk * hpb:(bk + 1) * hpb, :], pb_[:TT, 0:hpb * TT].rearrange("p (h l) -> p h l", h=hpb), AF.Exp)
                            yield
                        mm(cbk[:TT, :TT], xbcT[:, 8 + g, tsl], xbcT[:, 10 + g, tsl], True)
                        vtt(G, cbk[:TT, :TT], tri[:TT, :TT], ALU.mult)
                        vtt(WT, E, bc(G, 1, 8), ALU.mult)
                        yield
                        for hh in range(8):
                            h_ = g * 8 + hh
                            mm(ydk[:TT, hh * 64:(hh + 1) * 64], WT[:, hh, :], xt_[:, h_ * 64:(h_ + 1) * 64], hh == 0, hh == 7)
                        mm(yok[:TT, :], xbcT[:, 10 + g, tsl], hsb[:, g * 512:(g + 1) * 512], True)
                        yield
                        vtt(t1.rearrange("p (h d) -> p h d", h=8), yok[:TT, :].rearrange("p (h d) -> p h d", h=8), bc(ecum[:, g * 8:(g + 1) * 8], 2, 64), ALU.mult)
                        vtt(t1, t1, ydk[:TT, :], ALU.add)
                        vtt(t1, t1, xD[:, g * 512:(g + 1) * 512], ALU.add)
                        vtt(yz, t1, sz[:TT, ti, g * 512:(g + 1) * 512], ALU.mult)
                        yield
                        afunc(t1, yz, AF.Square, accum=sm[:TT, 128 + g:129 + g])
                        rstd_pow(sm[:TT, 132 + g:133 + g], sm[:TT, 128 + g:129 + g], 512, sm[:TT, 130 + g:131 + g])
                        vts(t1, yz, sm[:TT, 132 + g:133 + g], ALU.mult)
                        vtt(ycb, t1, pbt[:TT, PB_GSSD + g * 512:PB_GSSD + (g + 1) * 512], ALU.mult)
                        yield
                        for c in range(4):
                            mm(ytk[:, c * TT:(c + 1) * TT], ycb[:, c * 128:(c + 1) * 128], identb[:TT, :TT], c == 0, c == 3)
                        vcopy(mixT[:, 8 + g * 4:12 + g * 4, tsl], ytk[:, 0:4 * TT].rearrange("p (c t) -> p c t", c=4))
                        yield
                        mm(cbk[:TT, 0:128], xbcT[:, 8 + g, tsl], identb, True)
                        acopy(bmtok, cbk[:TT, 0:128])
                        mm(cbk[:, :], bmtok, x2_[:, g * 512:(g + 1) * 512], True)
                        hv = hst[:, g * 512:(g + 1) * 512]
                        vtt(hv.rearrange("p (h d) -> p h d", h=8), hv.rearrange("p (h d) -> p h d", h=8), bc(dec[:, g * 8:(g + 1) * 8], 2, 64), ALU.mult)
                        vtt(hv, hv, cbk[:, :], ALU.add)
                        acopy(hsb[:, g * 512:(g + 1) * 512], hv)
                        yield

                    interleave(gen_ssd(0), gen_ssd(1))
                if gi == ngroups - 1:
                    stt = A32[:, 0:1024].rearrange("p (c n) -> p c n", c=8)
                    sp3 = AB[:, 0:3072].rearrange("p (k n) -> p k n", k=3)
                    split3(sp3, hst[:, :], A32[:, 1024:2048], A32[:, 2048:3072])
                    for c in range(8):
                        pbank = pA if c < 4 else pB
                        for k in range(3):
                            mm(pbank[:, (c % 4) * 128:(c % 4 + 1) * 128], sp3[:, k, c * 128:(c + 1) * 128], identb, c % 4 == 0 and k == 0, c % 4 == 3 and k == 2)
                    vcopy(stt[:, 0:4, :], pA[:, :].rearrange("p (c n) -> p c n", c=4))
                    vcopy(stt[:, 4:8, :], pB[:, :].rearrange("p (c n) -> p c n", c=4))
                    dma("sp", o_sh.rearrange("(c p) n -> p c n", p=128), stt)

                chk(8)
                def p4_vars(ti):
                    q0 = P0 + t0 + ti * TT
                    Lk = q0 + TT
                    return (slice(ti * TT, (ti + 1) * TT), q0, Lk, (Lk + 127) // 128, A32[:TT, 0:Lk], AB[:TT, 0:Lk], JK[:TT, 0:Lk])

                def gen_topk(ti):
                    tsl, q0, Lk, nkb, score, negm, junkb = p4_vars(ti)
                    for c0 in range(0, Lk, 512):
                        c1_ = min(Lk, c0 + 512)
                        for hh in range(4):
                            pb_ = pA if hh % 2 == 0 else pB
                            pr = (hh % 2) * 64
                            mm(pb_[:TT, 0:c1_ - c0], qiT[pr:pr + 64, hh // 2, tsl], kiT[pr:pr + 64, c0:c1_], True)
                            rl = A32[:TT, 2048 + (hh % 2) * 512:2560 + (hh % 2) * 512]
                            afunc(rl[:, 0:c1_ - c0], pb_[:TT, 0:c1_ - c0], AF.Relu)
                            if hh == 0:
                                vts(score[:, c0:c1_], rl[:, 0:c1_ - c0], kiw[:TT, ti, 64:65], ALU.mult)
                            else:
                                vstt(score[:, c0:c1_], rl[:, 0:c1_ - c0], kiw[:TT, ti, 64 + hh:65 + hh], score[:, c0:c1_], ALU.mult, ALU.add)
                            yield
                    lo = sm[:TT, 140:141]
                    if Lk > ktop:
                        amax = sm[:TT, 141:142]
                        hw = sm[:TT, 144:144 + NBIS + 1]
                        vreduce(amax, score, ALU.max, absval=True)
                        vtt(score[:, Lk - 128:Lk], score[:, Lk - 128:Lk], dmask[:, :], ALU.add)
                        vts(amax, amax, 1.001, ALU.mult, 1e-3, ALU.add)
                        vts(hw, p2row[:TT, :], amax, ALU.mult)
                        mid = sm[:TT, 142:143]
                        cnt = sm[:TT, 143:144]
                        ind = sm[:TT, 170:171]
                        vmemset(mid, 0.0)
                        yield
                        for it in range(NBIS):
                            vts(junkb, score, mid, ALU.is_gt, 0.0, ALU.add, accum=cnt)
                            vstt(ind, cnt, float(ktop) - 0.5, hw[:, it:it + 1], ALU.is_gt, ALU.mult)
                            vstt(mid, ind, hw[:, it + 1:it + 2], mid, ALU.subtract, ALU.add)
                            yield
                        vtt(lo, mid, hw[:, NBIS:NBIS + 1], ALU.subtract)
                    else:
                        vtt(score[:, Lk - 128:Lk], score[:, Lk - 128:Lk], dmask[:, :], ALU.add)
                        vmemset(lo, NEG / 2)

                def do_mask(ti):
                    tsl, q0, Lk, nkb, score, negm, junkb = p4_vars(ti)
                    lo = sm[:TT, 140:141]
                    if Lk > ktop:
                        c0 = sm[:TT, 171:172]
                        mrem = sm[:TT, 172:173]
                        vts(junkb, score, 0.0, ALU.is_gt, 0.0, ALU.add, accum=c0)
                        vts(mrem, c0, -1.0, ALU.mult, float(ktop), ALU.add)
                        vts(negm, score, 0.0, ALU.is_equal)
                        vscan(negm, onesb[:TT, 0:1].broadcast_to([TT, Lk]), negm, 0.0)
                        vts(negm, negm, mrem, ALU.is_gt)
                        vstt(negm, score, 0.0, negm, ALU.is_equal, ALU.mult)
                        vstt(negm, score, lo, negm, ALU.is_le, ALU.max)
                        vts(negm, negm, NEG, ALU.mult)
                    else:
                        vts(negm, score, lo, ALU.is_le, NEG, ALU.mult)

                def gen_attn(ti):
                    tsl, q0, Lk, nkb, score, negm, junkb = p4_vars(ti)
                    for g in range(2):
                        for jb in range(nkb):
                            S = min(128, Lk - jb * 128)
                            pl = pC if jb % 2 == 0 else pD
                            plv = pl[:S, 0:4 * TT].rearrange("p (m t) -> p m t", m=4)
                            mm(plv, negm[:, jb * 128:jb * 128 + S], bc(identb[:TT, :TT], 1, 4), True, False)
                            dl = jb * 128 - q0
                            mm(plv, khT[g * 64:(g + 1) * 64, jb * 128:jb * 128 + S], qhT[g * 64:(g + 1) * 64, :, tsl], False, dl <= -256)
                            if dl > -256:
                                si = 0 if dl == 0 else 1
                                for m in range(4):
                                    mm(plv[:, m, :], hk[:TT, si, g * 4 + m, :S], j128b[:TT, :TT], False, m == 3)
                            ET = AB[:S, 2048 + (jb % 2) * 512:2048 + (jb % 2) * 512 + 4 * TT].rearrange("p (m t) -> p m t", m=4)
                            afunc(ET, plv, AF.Exp)
                            pov = pE[:, 0:2 * TT].rearrange("p (a t) -> p a t", a=2)
                            pdv = pF[:, 0:2 * TT].rearrange("p (a t) -> p a t", a=2)
                            for par in range(2):
                                mm(pov[par * 64:(par + 1) * 64], vtok[:S, jb, g * 64:(g + 1) * 64], ET[:, par::2, :], jb == 0, jb == nkb - 1)
                                mm(pdv[par * 64:(par + 1) * 64], onesb[:S, 0:64], ET[:, par::2, :], jb == 0, jb == nkb - 1)
                            yield
                        rden = A32[:, 3072:3072 + 2 * TT]
                        vrecip(rden, pF[:, 0:2 * TT])
                        vtt(rden, pE[:, 0:2 * TT], rden, ALU.mult)
                        vtt(mixT[:, 4 + 2 * g:6 + 2 * g, tsl], rden.rearrange("p (a t) -> p a t", a=2), gaT[:, 2 * g:2 * g + 2, tsl], ALU.mult)
                        yield

                def gen_p5(ti):
                    tsl = slice(ti * TT, (ti + 1) * TT)
                    r0 = t0 + ti * TT
                    TV = min(TT, NV - ti * TT)
                    base = (ti % 2) * 1024
                    xr = A32[:TT, base:base + 1024]
                    if TV < TT:
                        vmemset(A32[TV:TT, base:base + 1024], 0.0)
                    dma("sp", A32[:TV, base:base + 1024], xin[r0:r0 + TV, :])
                    yield
                    for hf in range(2):
                        pb_ = pA if hf == 0 else pB
                        for ec in range(16):
                            mm(pb_[:TT, :], mixT[:, ec, tsl], wout[:, ec, hf * 512:(hf + 1) * 512], ec == 0, ec == 15)
                            if ec % 4 == 3:
                                yield
                        vtt(xr[:, hf * 512:(hf + 1) * 512], pb_[:TT, :], xr[:, hf * 512:(hf + 1) * 512], ALU.add)
                        yield
                    dma("sp", xout[r0:r0 + TV, :], xr[:TV])
                    yield

                drain(gen_topk(0))
                do_mask(0)
                if ntile == 2:
                    interleave(gen_attn(0), gen_topk(1))
                    do_mask(1)
                    interleave(gen_attn(1), gen_p5(0))
                else:
                    drain(gen_attn(0))

                chk(9)
                if DEBUG_MIX[0] and not isS and qi_ == 0 and l == 0:
                    dma("sp", O["dbg"][:, :, t0:t0 + NT].rearrange("c p t -> p c t"), mixT[:, :, :NT])
                drain(gen_p5(1 if ntile == 2 else 0))


def _t5_bucket_np(rel):
    import math
    import jax
    import jax.numpy as jnp
    with jax.default_device(jax.devices("cpu")[0]):
        return _t5_bucket_cpu(rel, math, jnp)


def _t5_bucket_cpu(rel, math, jnp):
    rel = jnp.asarray(rel, dtype=jnp.int32)
    half, max_exact = 16, 8
    n = jnp.abs(rel)
    large = max_exact + (jnp.log(jnp.maximum(n, 1).astype(jnp.float32) / max_exact) / math.log(128 / max_exact) * (half - max_exact)).astype(jnp.int32)
    large = jnp.minimum(large, half - 1)
    return np.asarray(jnp.where(rel > 0, half, 0) + jnp.where(n < max_exact, n, large))


def _consts():
    c = np.zeros((128, NCST), np.float32)
    i = np.arange(128)
    c[:, 0:128] = np.eye(128)
    c[:, 128:256] = (i[:, None] <= i[None, :])
    c[:, 256:384] = (i[:, None] > i[None, :])
    c[:, 384:512] = (i[:, None] == 127 - i[None, :])
    c[:64, 512:576] = (i[:64, None] == 63 - i[None, :64])
    c[:, 640:768] = np.where((i[None, :] // 64) > (i[:, None] // 64), NEG, 0.0)
    bk = _t5_bucket_np(np.arange(384) - 255)
    c[0:32, 768:1152] = (np.arange(32)[:, None] == bk[None, :])
    return c


_PROG_CACHE = {}


def _run(inputs, NCORES, NSP, TP, SAMPLE, PS, TS, DEPTH):
    key = (NSP, TP, SAMPLE, PS, TS, DEPTH)
    f32 = np.float32
    g = lambda k: np.ascontiguousarray(np.asarray(inputs[k], dtype=f32))
    w_in, w_out = g("w_in"), g("w_out")
    cst = _consts()
    pp = np.zeros((DEPTH, 128, NPP), f32)
    pb = np.zeros((DEPTH, 128, NPB), f32)
    wabd = np.zeros((DEPTH, 128, 2, 4, 128), f32)
    fm = lambda v, n: v.reshape(n, 128).T
    for l in range(DEPTH):
        pp[l, :, PP_GN:PP_GN + 8] = fm(g("norm_w")[l], 8)
        lcw = g("lru_conv_w")[l]
        pp[l, :, PP_LCW:PP_LCW + 16] = lcw.reshape(4, 4, 128).transpose(2, 1, 0).reshape(128, 16)
        pp[l, :, PP_LCB:PP_LCB + 4] = fm(g("lru_conv_b")[l], 4)
        pp[l, :, PP_LBA:PP_LBA + 4] = fm(g("lru_b_a")[l], 4)
        pp[l, :, PP_LBX:PP_LBX + 4] = fm(g("lru_b_x")[l], 4)
        pp[l, :, PP_LAM:PP_LAM + 4] = fm(g("lru_lambda")[l], 4)
        scw = g("ssd_conv_w")[l]
        pp[l, :, PP_SCW:PP_SCW + 48] = scw.reshape(4, 12, 128).transpose(2, 1, 0).reshape(128, 48)
        pp[l, :, PP_SCB:PP_SCB + 12] = fm(g("ssd_conv_b")[l], 12)
        pb[l, :, PB_GQ:PB_GQ + 64] = g("att_q_norm")[l][None, :]
        pb[l, :, PB_GK:PB_GK + 64] = g("att_k_norm")[l][None, :]
        pb[l, :, PB_DTB:PB_DTB + 16] = g("ssd_dt_bias")[l][None, :]
        pb[l, :, PB_ALOG:PB_ALOG + 16] = g("ssd_a_log")[l][None, :]
        pb[l, :, PB_D:PB_D + 16] = g("ssd_d")[l][None, :]
        pb[l, :, PB_GSSD:PB_GSSD + 1024] = g("ssd_norm")[l][None, :]
        pb[l, :, PB_RB15:PB_RB15 + 8] = g("rel_bias")[15][None, :]
        for a, nm in enumerate(("lru_w_a", "lru_w_x")):
            w = g(nm)[l]
            for gg in range(4):
                wabd[l, 0:64, a, gg, 0:64] = w[2 * gg]
                wabd[l, 64:128, a, gg, 64:128] = w[2 * gg + 1]
    wabd = wabd.reshape(DEPTH, 128, 1024)
    xp = g("x_prompt")
    in_maps = []
    for c in range(NCORES):
        m = {"xp": xp[c * NSP:(c + 1) * NSP], "w_in": w_in, "w_out": w_out, "pp": pp, "pb": pb, "wabd": wabd,
             "rb": g("rel_bias"), "cst": cst}
        if SAMPLE:
            m["xs"] = g("x_sample")[c]
            m["ck"] = g("cache_att_k")[:, c].reshape(DEPTH, PS, 128)
            m["cv"] = g("cache_att_v")[:, c].reshape(DEPTH, PS, 128)
            m["cki"] = g("cache_idx_k")[:, c]
            m["slc"] = g("state_lru_conv")[:, c]
            m["slh"] = g("state_lru_h")[:, c]
            m["ssc"] = g("state_ssd_conv")[:, c]
            m["ssh"] = g("state_ssd_h")[:, c].reshape(DEPTH, 1024, 128)
        in_maps.append({k: np.ascontiguousarray(v) for k, v in m.items()})
    if key not in _PROG_CACHE:
        _PROG_CACHE[key] = build_program(NSP=NSP, TP=TP, SAMPLE=SAMPLE, PS=PS, TS=TS, DEPTH=DEPTH)
    nc = _PROG_CACHE[key]
    res = run_bass_kernel_spmd(nc, in_maps, core_ids=list(range(NCORES)))
    R = res.results
    cat = lambda k, ax: np.concatenate([np.asarray(r[k]) for r in R], axis=ax)
    stk = lambda k, ax: np.stack([np.asarray(r[k]) for r in R], axis=ax)
    B = NCORES * NSP
    outs = [cat("yp", 0)]
    if SAMPLE:
        outs.append(stk("ys", 0))
    outs += [cat("akp", 1).reshape(DEPTH, B, TP, 2, 64), cat("avp", 1).reshape(DEPTH, B, TP, 2, 64), cat("ikp", 1),
             cat("lcp", 1), cat("lhp", 1), cat("scp", 1), cat("shp", 1).reshape(DEPTH, B, 16, 64, 128)]
    if SAMPLE:
        outs += [stk("aks", 1).reshape(DEPTH, NCORES, TS, 2, 64), stk("avs", 1).reshape(DEPTH, NCORES, TS, 2, 64), stk("iks", 1),
                 stk("lcs", 1), stk("lhs", 1), stk("scs", 1), stk("shs", 1).reshape(DEPTH, NCORES, 16, 64, 128)]
    return tuple(np.ascontiguousarray(o, dtype=np.float32) for o in outs)


def kernel(**inputs):
    return _run(inputs, 8, 2, 2048, True, 1024, 64, 2)
```

```python
import numpy as np
from contextlib import ExitStack
import concourse.bass as bass
import concourse.mybir as mybir
from concourse.bass_utils import run_bass_kernel_spmd

F32 = mybir.dt.float32
BF16 = mybir.dt.bfloat16
AF = mybir.ActivationFunctionType
ALU = mybir.AluOpType
AX = mybir.AxisListType


def _region(ap):
    t = ap.tensor
    pat = [(int(s), int(c)) for (s, c) in ap.ap]
    off = int(ap.offset)
    kind = type(t).__name__
    if kind.startswith("DRam"):
        lo = off
        ext = sum((c - 1) * abs(s) for s, c in pat)
        n = 1
        for s, c in pat:
            if s != 0:
                n *= c
        return (t.name, 0, 1, lo, lo + ext + 1, n == ext + 1)
    shp = [int(v) for v in t.shape]
    pstride = 1
    for v in shp[1:]:
        pstride *= v
    p0 = off // pstride
    lo = off % pstride
    npart = pat[0][1]
    ext = sum((c - 1) * abs(s) for s, c in pat[1:])
    n = 1
    for s, c in pat[1:]:
        if s != 0:
            n *= c
    if kind.startswith("PSum"):
        full = (lo == 0 and lo + ext + 1 == pstride and n == ext + 1)
        return (t.name, (p0 // 32) * 32, ((p0 + npart + 31) // 32) * 32, 0, pstride, full and p0 % 32 == 0 and (p0 + npart) % 32 == 0)
    return (t.name, p0, p0 + npart, lo, lo + ext + 1, n == ext + 1)


def _ovl(a, b):
    return a[1] < b[2] and b[1] < a[2] and a[3] < b[4] and b[3] < a[4]


def _covers(a, b):
    return a[5] and a[1] <= b[1] and a[2] >= b[2] and a[3] <= b[3] and a[4] >= b[4]


class _Stop(Exception):
    pass


STOP_AT = [99]
DEBUG_MIX = [False]


CK_OFF = [0]


def drain(gen):
    for _ in gen:
        pass


def interleave_n(*gens):
    live = list(gens)
    while live:
        live = [g for g in live if next(g, "end") != "end"]


def interleave(ga_, gb_):
    a_live = b_live = True
    while a_live or b_live:
        if a_live:
            a_live = next(ga_, "end") != "end"
        if b_live:
            b_live = next(gb_, "end") != "end"


def chk(k):
    if k + CK_OFF[0] > STOP_AT[0]:
        raise _Stop()


class Prog:
    def __init__(self, nc):
        self.nc = nc
        self.stack = ExitStack()
        self.ops = []
        self.wr = {}
        self.rd = {}
        self.finished = False

    def sb(self, name, shape, dt):
        return self.stack.enter_context(self.nc.sbuf_tensor("sb_" + name, list(shape), dt))

    def ps(self, name, shape, dt):
        return self.stack.enter_context(self.nc.psum_tensor("ps_" + name, list(shape), dt))

    def _add(self, eng, fn, outs, ins, is_dma=False):
        idx = len(self.ops)
        deps = {}
        rregs = [_region(a) for a in ins]
        wregs = [_region(a) for a in outs]
        for r in rregs:
            for w in self.wr.get(r[0], ()):
                if _ovl(r, w[1]):
                    deps.setdefault(w[0], set()).add("raw")
        for r in wregs:
            for w in self.wr.get(r[0], ()):
                if _ovl(r, w[1]):
                    deps.setdefault(w[0], set()).add("waw")
            for w in self.rd.get(r[0], ()):
                if _ovl(r, w[1]):
                    deps.setdefault(w[0], set()).add("war")
        for r in wregs:
            lw = self.wr.setdefault(r[0], [])
            lw[:] = [w for w in lw if not _covers(r, w[1])]
            lw.append((idx, r))
            lr = self.rd.get(r[0])
            if lr:
                lr[:] = [w for w in lr if not _covers(r, w[1])]
        for r in rregs:
            self.rd.setdefault(r[0], []).append((idx, r))
        self.ops.append(dict(eng=eng, fn=fn, deps=deps, dma=is_dma, sig=False))
        return idx

    def pe(self, fn, outs, ins):
        return self._add("pe", fn, outs, ins)

    def dve(self, fn, outs, ins):
        return self._add("dve", fn, outs, ins)

    def act(self, fn, outs, ins):
        return self._add("act", fn, outs, ins)

    def pool(self, fn, outs, ins):
        return self._add("pool", fn, outs, ins)

    def dma(self, q, out, in_, **kw):
        return self._add(q, lambda e: e.dma_start(out=out, in_=in_, **kw), [out], [in_], is_dma=True)

    def finish(self):
        nc = self.nc
        ops = self.ops
        engs = ["pe", "dve", "act", "pool", "sp"]
        for i, op in enumerate(ops):
            nd = set()
            for j, kinds in op["deps"].items():
                o = ops[j]
                if o["dma"]:
                    nd.add(j)
                    continue
                if o["eng"] == op["eng"] and not op["dma"]:
                    if op["eng"] == "pe":
                        continue
                nd.add(j)
            op["deps"] = nd
            for j in nd:
                ops[j]["sig"] = True
        esem = {e: self.stack.enter_context(nc.semaphore("s_" + e)) for e in engs}
        nds = {"sp": 40, "pool": 16, "act": 8}
        dsem = {q: [self.stack.enter_context(nc.semaphore("d_%s%d" % (q, k))) for k in range(n)] for q, n in nds.items()}
        dcnt = {q: [0] * n for q, n in nds.items()}
        dnext = {q: 0 for q in nds}
        ecnt = {e: 0 for e in engs}
        for i, op in enumerate(ops):
            if op["dma"]:
                q = op["eng"]
                k = dnext[q] % nds[q]
                dnext[q] += 1
                prev = dcnt[q][k]
                dcnt[q][k] += 16
                op["event"] = (dsem[q][k], dcnt[q][k])
                op["prevev"] = (dsem[q][k], prev) if prev > 0 else None
            elif op["sig"]:
                ecnt[op["eng"]] += 1
                op["event"] = (esem[op["eng"]], ecnt[op["eng"]])
        waited = {e: {} for e in engs}
        per_eng = {e: [] for e in engs}
        for i, op in enumerate(ops):
            e = op["eng"]
            need = {}
            for j in op["deps"]:
                s, v = ops[j]["event"]
                key = id(s)
                if need.get(key, (None, 0))[1] < v:
                    need[key] = (s, v)
            if op["dma"] and op["prevev"] is not None:
                s, v = op["prevev"]
                key = id(s)
                if need.get(key, (None, 0))[1] < v:
                    need[key] = (s, v)
            waits = []
            for key, (s, v) in need.items():
                if waited[e].get(key, 0) >= v:
                    continue
                waited[e][key] = v
                waits.append((s, v))
            op["waits"] = waits
            per_eng[e].append(op)
        self.n_waits = sum(len(o["waits"]) for o in ops)
        final = []
        for q in nds:
            for k in range(nds[q]):
                if dcnt[q][k] > 0:
                    final.append((dsem[q][k], dcnt[q][k]))

        def emit(ename, e):
            for op in per_eng[ename]:
                for (s, v) in op["waits"]:
                    e.wait_ge(s, v)
                ins = op["fn"](e)
                if op["dma"]:
                    ins.then_inc(op["event"][0], 16)
                elif op["sig"]:
                    ins.then_inc(op["event"][0], 1)
            if ename == "sp":
                for (s, v) in final:
                    e.wait_ge(s, v)

        with nc.Block() as block:
            @block.tensor
            def _(e):
                emit("pe", e)

            @block.vector
            def _(e):
                emit("dve", e)

            @block.scalar
            def _(e):
                emit("act", e)

            @block.gpsimd
            def _(e):
                emit("pool", e)

            @block.sync
            def _(e):
                emit("sp", e)
        self.stack.close()
        self.finished = True


D_MODEL = 1024
D_IN = 5204
D_MIX = 2048
EPS = 1e-6
NEG = -30000.0
C_XL, C_GL, C_Q, C_K, C_GA, C_QI, C_KI, C_Z, C_XBC, C_DT = 0, 512, 1024, 1536, 1792, 2304, 2560, 2628, 3652, 5188
NPP = 100
NPB = 1208
NCST = 1152
PB_GQ, PB_GK, PB_DTB, PB_ALOG, PB_D, PB_GSSD, PB_RB15 = 0, 64, 128, 144, 160, 176, 1200
PP_GN, PP_LCW, PP_LCB, PP_LBA, PP_LBX, PP_LAM, PP_SCW, PP_SCB = 0, 8, 24, 28, 32, 36, 40, 88


def build_program(NSP=2, TP=2048, SAMPLE=True, PS=1024, TS=64, DEPTH=2, TG=256, NBIS=16):
    nc = bass.Bass("TRN2", target_bir_lowering=False)
    pg = Prog(nc)
    dt_in = lambda name, shape: nc.dram_tensor(name, list(shape), F32, kind="ExternalInput").ap()
    dt_out = lambda name, shape: nc.dram_tensor(name, list(shape), F32, kind="ExternalOutput").ap()
    dt_tmp = lambda name, shape: nc.dram_tensor(name, list(shape), F32, kind="Internal").ap()
    I = {}
    I["xp"] = dt_in("xp", [NSP, TP, D_MODEL])
    I["w_in"] = dt_in("w_in", [DEPTH, D_MODEL, D_IN])
    I["w_out"] = dt_in("w_out", [DEPTH, D_MIX, D_MODEL])
    I["pp"] = dt_in("pp", [DEPTH, 128, NPP])
    I["pb"] = dt_in("pb", [DEPTH, 128, NPB])
    I["wabd"] = dt_in("wabd", [DEPTH, 128, 2 * 4 * 128])
    I["rb"] = dt_in("rb", [32, 8])
    I["cst"] = dt_in("cst", [128, NCST])
    O = {}
    O["yp"] = dt_out("yp", [NSP, TP, D_MODEL])
    O["akp"] = dt_out("akp", [DEPTH, NSP, TP, 128])
    O["avp"] = dt_out("avp", [DEPTH, NSP, TP, 128])
    O["ikp"] = dt_out("ikp", [DEPTH, NSP, TP, 64])
    O["lcp"] = dt_out("lcp", [DEPTH, NSP, 3, 512])
    O["lhp"] = dt_out("lhp", [DEPTH, NSP, 512])
    O["scp"] = dt_out("scp", [DEPTH, NSP, 3, 1536])
    O["shp"] = dt_out("shp", [DEPTH, NSP, 1024, 128])
    xmid_p = dt_tmp("xmid_p", [NSP, TP, D_MODEL])
    if DEBUG_MIX[0]:
        O["dbg"] = nc.dram_tensor("dbg", [16, 128, TP], BF16, kind="ExternalOutput").ap()
    vecd = dt_tmp("vecd", [8, 384])
    if SAMPLE:
        I["xs"] = dt_in("xs", [TS, D_MODEL])
        I["ck"] = dt_in("ck", [DEPTH, PS, 128])
        I["cv"] = dt_in("cv", [DEPTH, PS, 128])
        I["cki"] = dt_in("cki", [DEPTH, PS, 64])
        I["slc"] = dt_in("slc", [DEPTH, 3, 512])
        I["slh"] = dt_in("slh", [DEPTH, 512])
        I["ssc"] = dt_in("ssc", [DEPTH, 3, 1536])
        I["ssh"] = dt_in("ssh", [DEPTH, 1024, 128])
        O["ys"] = dt_out("ys", [TS, D_MODEL])
        O["aks"] = dt_out("aks", [DEPTH, TS, 128])
        O["avs"] = dt_out("avs", [DEPTH, TS, 128])
        O["iks"] = dt_out("iks", [DEPTH, TS, 64])
        O["lcs"] = dt_out("lcs", [DEPTH, 3, 512])
        O["lhs"] = dt_out("lhs", [DEPTH, 512])
        O["scs"] = dt_out("scs", [DEPTH, 3, 1536])
        O["shs"] = dt_out("shs", [DEPTH, 1024, 128])
        xmid_s = dt_tmp("xmid_s", [TS, D_MODEL])

    LMAX = max(TP, (PS + TS) if SAMPLE else 0)
    LMAX = ((LMAX + 127) // 128) * 128
    NKB = LMAX // 128
    sb, ps = pg.sb, pg.ps
    win = sb("win", [128, 8, D_IN], BF16)
    wout = sb("wout", [128, 16, D_MODEL], BF16)
    ppt = sb("ppt", [128, NPP], F32)
    pbt = sb("pbt", [128, NPB], F32)
    wabd = sb("wabdb", [128, 8, 128], BF16)
    cst = sb("cst", [128, 768], F32)
    ident = cst[:, 0:128]
    tri = cst[:, 128:256]
    astr = cst[:, 256:384]
    dmask = cst[:, 640:768]
    cbf = sb("cbf", [128, 6, 128], BF16)
    trib, astrb = cbf[:, 4, :], cbf[:, 5, :]
    RB = sb("RB", [128, 2, 512], BF16)
    JK = sb("JK", [128, LMAX], mybir.dt.uint8)
    dAs = sb("dAs", [128, 3, 16], BF16)
    identb, j128b, j64b, onesb = cbf[:, 0, :], cbf[:, 1, :], cbf[:, 2, :], cbf[:, 3, :]
    hk = sb("hk", [128, 2, 8, 128], BF16)
    p2row = sb("p2row", [128, NBIS + 1], F32)
    rbt = sb("rbt", [32, 8], F32)
    rb15row = sb("rb15row", [1, 8, 128], BF16)
    vecs = sb("vecs", [8, 384], F32)
    drv = sb("drv", [128, 64], F32)
    c1 = drv[:, 0:4]
    aneg = drv[:, 4:20]
    gq8 = sb("gq8", [128, 64], F32)
    hT = sb("hT", [128, 8, TG], BF16)
    mixT = sb("mixT", [128, 16, TG], BF16)
    khT = sb("khT", [128, LMAX], BF16)
    vtok = sb("vtok", [128, NKB, 128], BF16)
    kiT = sb("kiT", [128, LMAX], BF16)
    qhT = sb("qhT", [128, 4, TG], BF16)
    qiT = sb("qiT", [128, 2, TG], BF16)
    gaT = sb("gaT", [128, 4, TG], BF16)
    lhist = sb("lhist", [128, 4, 3], F32)
    shist = sb("shist", [128, 12, 3], F32)
    lh = sb("lh", [128, 4], F32)
    xbcT = sb("xbcT", [128, 12, TG], BF16)
    sz = sb("sz", [128, 2, 1024], BF16)
    hst = sb("hst", [128, 1024], F32)
    hsb = sb("hsb", [128, 1024], BF16)
    sm = sb("sm", [128, 256], F32)
    smb = sb("smb", [128, 2, 80], F32)
    kiw = sb("kiw", [128, 2, 68], F32)
    dtt = sb("dtt", [128, 2, 16], F32)
    dAt = sb("dAt", [128, 2, 16], F32)
    A32 = sb("A32", [128, 3328], F32)
    AB = sb("AB", [128, 4608], BF16)
    pA = ps("pA", [128, 512], F32)
    pB = ps("pB", [128, 512], F32)
    pC = ps("pC", [128, 512], F32)
    pD = ps("pD", [128, 512], F32)
    pE = ps("pE", [128, 512], F32)
    pF = ps("pF", [128, 512], F32)
    pG = ps("pG", [128, 512], F32)
    pH = ps("pH", [128, 512], F32)

    act, dve, pe, pool, dma = pg.act, pg.dve, pg.pe, pg.pool, pg.dma

    def A_(fn, out, ins):
        return act(fn, [out] if not isinstance(out, list) else out, ins)

    def acopy(out, in_):
        act(lambda e: e.activation(out=out, in_=in_, func=AF.Copy), [out], [in_])

    def afunc(out, in_, func, bias=None, scale=None, accum=None):
        kw = {}
        reads = [in_]
        outs = [out]
        if bias is not None:
            kw["bias"] = bias
            if not isinstance(bias, float):
                reads.append(bias)
        if scale is not None:
            kw["scale"] = scale
            if not isinstance(scale, float):
                reads.append(scale)
        if accum is not None:
            kw["accum_out"] = accum
            outs.append(accum)
        act(lambda e: e.activation(out=out, in_=in_, func=func, **kw), outs, reads)

    def vtt(out, a, b, op):
        dve(lambda e: e.tensor_tensor(out=out, in0=a, in1=b, op=op), [out], [a, b])

    def vts(out, a, s1, op0, s2=None, op1=None, accum=None):
        reads = [a] + [s for s in (s1, s2) if s is not None and not isinstance(s, float)]
        outs = [out] + ([accum] if accum is not None else [])
        kw = {}
        if op1 is not None:
            kw["op1"] = op1
        if accum is not None:
            kw["accum_out"] = accum
        dve(lambda e: e.tensor_scalar(out=out, in0=a, scalar1=s1, scalar2=s2, op0=op0, **kw), outs, reads)

    def vstt(out, a, s, b, op0, op1):
        reads = [a, b] + ([s] if not isinstance(s, float) else [])
        dve(lambda e: e.scalar_tensor_tensor(out=out, in0=a, scalar=s, in1=b, op0=op0, op1=op1), [out], reads)

    def vcopy(out, in_):
        dve(lambda e: e.tensor_copy(out=out, in_=in_), [out], [in_])

    def vscan(out, d0, d1, init):
        reads = [d0, d1] + ([init] if not isinstance(init, float) else [])
        dve(lambda e: e.tensor_tensor_scan(out=out, data0=d0, data1=d1, initial=init, op0=ALU.mult, op1=ALU.add), [out], reads)

    def vreduce(out, in_, op, absval=False):
        if absval:
            dve(lambda e: e.tensor_reduce(out=out, in_=in_, axis=AX.X, op=op, apply_absolute_value=True), [out], [in_])
        else:
            dve(lambda e: e.tensor_reduce(out=out, in_=in_, axis=AX.X, op=op), [out], [in_])

    def vmemset(ap, val):
        dve(lambda e: e.memset(ap, val), [ap], [])

    def ptt(out, a, b, op):
        pool(lambda e: e.tensor_tensor(out=out, in0=a, in1=b, op=op), [out], [a, b])

    def split3(dst, src32, tmpa, tmpb):
        vcopy(dst[:, 0, :], src32)
        vtt(tmpa, src32, dst[:, 0, :], ALU.subtract)
        vcopy(dst[:, 1, :], tmpa)
        vtt(tmpb, tmpa, dst[:, 1, :], ALU.subtract)
        vcopy(dst[:, 2, :], tmpb)

    def rstd_pow(out, ss, n, tmp):
        afunc(tmp, ss, AF.Ln, scale=1.0 / n, bias=EPS)
        afunc(out, tmp, AF.Exp, scale=-0.5)

    def sigm(buf, src, scale=-1.0, bias=None):
        afunc(buf, src, AF.Exp, scale=scale, bias=bias)
        afunc(buf, buf, AF.Ln, bias=1.0)
        afunc(buf, buf, AF.Exp, scale=-1.0)

    def vrecip(out, in_):
        dve(lambda e: e.reciprocal(out=out, in_=in_), [out], [in_])

    def mm(out, lhsT, rhs, start, stop=True):
        pe(lambda e: e.matmul(out, lhsT=lhsT, rhs=rhs, start=start, stop=stop), [out], [lhsT, rhs])

    def tr(out, in_, idt):
        pe(lambda e: e.transpose(out, in_, idt), [out], [in_, idt])

    def bc(ap, axis, n):
        a = ap.unsqueeze(axis)
        shp = list(a.shape)
        shp[axis] = n
        return a.broadcast_to(shp)

    try:
      _body(locals())
    except _Stop:
      pass
    pg.finish()
    return nc


def _body(env):
    globals().update({k: v for k, v in env.items() if not k.startswith("__")})
    dma("sp", cst[:], I["cst"][:, 0:768])
    dma("sp", A32[0:32, 0:384], I["cst"][0:32, 768:1152])
    dma("sp", rbt[:], I["rb"])
    vcopy(cbf[:, 0, :], cst[:, 0:128])
    vcopy(cbf[:, 1, :], cst[:, 384:512])
    vcopy(cbf[:, 2, :], cst[:, 512:640])
    vmemset(cbf[:, 3, :], 1.0)
    vcopy(cbf[:, 4, :], cst[:, 128:256])
    vcopy(cbf[:, 5, :], cst[:, 256:384])
    mm(pA[0:8, 0:384], rbt[:, :], A32[0:32, 0:384], True)
    vcopy(vecs[:], pA[0:8, 0:384])
    vcopy(rbt[0:8, 0:1], vecs[:, 0:1])
    vts(vecs[:], vecs[:], rbt[0:8, 0:1], ALU.subtract)
    dma("sp", vecd, vecs[:])
    hktmp = A32[:, 0:1024].rearrange("p (h s) -> p h s", h=8)
    for k in range(NBIS + 1):
        vmemset(p2row[:, k:k + 1], 2.0 ** -k)
    for si, (TTq, dl) in enumerate([(128, 0), (128, -128)]):
        base = dl - TTq + 256
        src = bass.AP(tensor=vecd.tensor, offset=base, ap=[[1, TTq], [384, 8], [1, 128]])
        dma("sp", hktmp[:TTq], src)
        vcopy(hk[:TTq, si, :, :], hktmp[:TTq])

    CK_OFF[0] = 0
    chk(1)
    seqs = []
    for q in range(NSP):
        seqs.append(dict(kind="p", idx=q, T=TP, P=0))
    if SAMPLE:
        seqs.append(dict(kind="s", idx=0, T=TS, P=PS))

    for l in range(DEPTH):
        for kc in range(8):
            for hf in range(2):
                c0, c1_ = hf * 2602, (hf + 1) * 2602
                dma("pool", win[:, kc, c0:c1_], I["w_in"][l, kc * 128:(kc + 1) * 128, c0:c1_])
        for ec in range(16):
            dma("pool", wout[:, ec, :], I["w_out"][l, ec * 128:(ec + 1) * 128, :])
        dma("sp", ppt[:], I["pp"][l])
        dma("sp", pbt[:], I["pb"][l])
        dma("pool", wabd[:].rearrange("p a b -> p (a b)"), I["wabd"][l])
        afunc(drv[:, 20:24], ppt[:, PP_LAM:PP_LAM + 4], AF.Exp, scale=-1.0)
        afunc(drv[:, 24:28], drv[:, 20:24], AF.Ln, bias=1.0)
        vts(c1, drv[:, 24:28], -8.0, ALU.mult)
        afunc(drv[:, 28:44], pbt[:, PB_ALOG:PB_ALOG + 16], AF.Exp)
        vts(aneg, drv[:, 28:44], -1.0, ALU.mult)
        vts(gq8[:], pbt[:, PB_GQ:PB_GQ + 64], 0.125, ALU.mult)
        vts(drv[:, 48:52], ppt[:, PP_LBA:PP_LBA + 4], -1.0, ALU.mult)
        vts(drv[:, 52:56], ppt[:, PP_LBX:PP_LBX + 4], -1.0, ALU.mult)
        vcopy(rb15row[0:1, :, :], bc(pbt[0:1, PB_RB15:PB_RB15 + 8], 2, 128))

        chk(2)
        for sq in seqs:
            T, P0 = sq["T"], sq["P"]
            isS = sq["kind"] == "s"
            qi_ = sq["idx"]
            L = P0 + T
            ktop = min(256, L // 4)
            if isS:
                xin = I["xs"] if l == 0 else xmid_s
                xout = O["ys"] if l == DEPTH - 1 else xmid_s
                o_ak, o_av, o_ik = O["aks"][l], O["avs"][l], O["iks"][l]
                o_lc, o_lh, o_sc, o_sh = O["lcs"][l], O["lhs"][l], O["scs"][l], O["shs"][l]
            else:
                xin = I["xp"][qi_] if l == 0 else xmid_p[qi_]
                xout = O["yp"][qi_] if l == DEPTH - 1 else xmid_p[qi_]
                o_ak, o_av, o_ik = O["akp"][l, qi_], O["avp"][l, qi_], O["ikp"][l, qi_]
                o_lc, o_lh, o_sc, o_sh = O["lcp"][l, qi_], O["lhp"][l, qi_], O["scp"][l, qi_], O["shp"][l, qi_]
            if isS:
                for g in range(4):
                    dma("sp", lhist[:, g, :], I["slc"][l].rearrange("j (g p) -> p g j", p=128)[:, g, :], allow_slow_non_contiguous=True)
                for g in range(12):
                    dma("sp", shist[:, g, :], I["ssc"][l].rearrange("j (g p) -> p g j", p=128)[:, g, :], allow_slow_non_contiguous=True)
                dma("sp", lh[:], I["slh"][l].rearrange("(g p) -> p g", p=128), allow_slow_non_contiguous=True)
                stt = A32[:, 0:1024].rearrange("p (c n) -> p c n", c=8)
                dma("sp", stt, I["ssh"][l].rearrange("(c p) n -> p c n", p=128))
                sp3 = AB[:, 0:3072].rearrange("p (k n) -> p k n", k=3)
                split3(sp3, A32[:, 0:1024], A32[:, 1024:2048], A32[:, 2048:3072])
                for c in range(8):
                    pbank = pA if c < 4 else pB
                    for k in range(3):
                        mm(pbank[:, (c % 4) * 128:(c % 4 + 1) * 128], sp3[:, k, c * 128:(c + 1) * 128], identb, c % 4 == 0 and k == 0, c % 4 == 3 and k == 2)
                vcopy(hst[:, 0:512], pA[:, :])
                vcopy(hst[:, 512:1024], pB[:, :])
                acopy(hsb[:, :], hst[:, :])
                nb = P0 // 128
                ckb = AB[:, 0:nb * 128].rearrange("p (c n) -> p c n", c=nb)
                kid = AB[:, nb * 128:2 * nb * 128].rearrange("p (c n) -> p c n", c=nb)
                dma("pool", ckb, I["ck"][l].rearrange("(c p) n -> p c n", p=128))
                dma("pool", vtok[:, 0:nb, :], I["cv"][l].rearrange("(c p) n -> p c n", p=128))
                dma("pool", kid[:, :, 0:64], I["cki"][l].rearrange("(c p) n -> p c n", p=128))
                dma("pool", kid[:, :, 64:128], I["cki"][l].rearrange("(c p) n -> p c n", p=128))
                for c in range(nb):
                    mm(pC[:, (c % 4) * 128:(c % 4 + 1) * 128], ckb[:, c, :], identb, c % 4 == 0, c % 4 == 3 or c == nb - 1)
                    mm(pD[:, (c % 4) * 128:(c % 4 + 1) * 128], kid[:, c, :], identb, c % 4 == 0, c % 4 == 3 or c == nb - 1)
                    if c % 4 == 3 or c == nb - 1:
                        c0 = (c // 4) * 4
                        n_ = c - c0 + 1
                        vcopy(khT[:, c0 * 128:(c0 + n_) * 128], pC[:, 0:n_ * 128])
                        acopy(kiT[:, c0 * 128:(c0 + n_) * 128], pD[:, 0:n_ * 128])
            else:
                vmemset(lhist[:], 0.0)
                vmemset(shist[:], 0.0)
                vmemset(lh[:], 0.0)
                vmemset(hst[:], 0.0)
                vmemset(hsb[:], 0.0)

            CK_OFF[0] = 10 if isS else 0
            chk(3)
            ngroups = (T + TG - 1) // TG
            for gi in range(ngroups):
                t0 = gi * TG
                NV = min(TG, T - t0)
                NT = ((NV + 127) // 128) * 128
                TT = 128
                ntile = NT // TT
                def gen_p1(ti):
                    r0 = t0 + ti * TT
                    TV = min(TT, NV - ti * TT)
                    xb_ = (ti % 2) * 1024
                    xt32 = A32[:TT, xb_:xb_ + 1024]
                    if TV < TT:
                        vmemset(A32[TV:TT, xb_:xb_ + 1024], 0.0)
                    dma("sp", A32[:TV, xb_:xb_ + 1024], xin[r0:r0 + TV, :])
                    yield
                    ab_ = (ti % 2) * 2048
                    junk = AB[:TT, ab_:ab_ + 1024]
                    xn = AB[:TT, ab_ + 1024:ab_ + 2048]
                    bk0, bk1 = (pA, pB) if ti % 2 == 0 else (pC, pD)
                    ssq = sm[:TT, ti:ti + 1]
                    afunc(junk, xt32, AF.Square, accum=ssq)
                    yield
                    rstd_pow(sm[:TT, 4 + ti:5 + ti], ssq, D_MODEL, sm[:TT, 2 + ti:3 + ti])
                    yield
                    vts(xn, xt32, sm[:TT, 4 + ti:5 + ti], ALU.mult)
                    yield
                    for kc in range(8):
                        pb_ = bk0 if kc < 4 else bk1
                        mm(pb_[:, (kc % 4) * TT:(kc % 4 + 1) * TT], xn[:, kc * 128:(kc + 1) * 128], identb[:TT, :TT], kc % 4 == 0, kc % 4 == 3)
                    yield
                    for hf in range(2):
                        pb_ = bk0 if hf == 0 else bk1
                        vtt(hT[:, hf * 4:(hf + 1) * 4, ti * TT:(ti + 1) * TT], pb_[:, 0:4 * TT].rearrange("p (c t) -> p c t", c=4),
                            bc(ppt[:, PP_GN + hf * 4:PP_GN + hf * 4 + 4], 2, TT), ALU.mult)
                    yield

                if ntile == 2:
                    interleave(gen_p1(0), gen_p1(1))
                else:
                    drain(gen_p1(0))

                chk(4)
                def proj_fm(pbank, c0, M, prow=0):
                    for kc in range(8):
                        mm(pbank[prow:prow + M, :NT], win[:, kc, c0:c0 + M], hT[:, kc, :NT], kc == 0, kc == 7)

                def conv(pbank, hist, g, wcol, bcol, out32, roff=0, ppt=ppt):
                    raw = A32[:, roff:roff + 3 + NT]
                    vcopy(raw[:, 0:3], hist[:, g, :])
                    acopy(raw[:, 3:3 + NT], pbank[:, :NT])
                    vcopy(hist[:, g, :], raw[:, NV:NV + 3])
                    vts(out32, raw[:, 0:NT], ppt[:, wcol:wcol + 1], ALU.mult, ppt[:, bcol:bcol + 1], ALU.add)
                    for j in range(1, 4):
                        vstt(out32, raw[:, j:j + NT], ppt[:, wcol + j:wcol + j + 1], out32, ALU.mult, ALU.add)
                    return raw

                def gen_lru(g):
                    lbase = (g % 2) * 1664
                    f = lambda k: A32[:, lbase + 260 + k * 256:lbase + 260 + k * 256 + NT]
                    gC, gD = (pC, pD) if g % 2 == 0 else (pG, pH)
                    xc, rr, ii, aa, s_ = [f(k) for k in range(5)]
                    gx, bb, hseq, sg = ii, s_, xc, rr
                    xcb = AB[:, 2048 + (g % 2) * 256:2048 + (g % 2) * 256 + NT]
                    pb_ = pA if g % 2 == 0 else pB
                    proj_fm(pb_, C_XL + g * 128, 128)
                    raw = conv(pb_, lhist, g, PP_LCW + g * 4, PP_LCB + g, xc, lbase)
                    if gi == ngroups - 1:
                        dma("sp", o_lc.rearrange("j (g p) -> p g j", p=128)[:, g, :], raw[:, NV:NV + 3], allow_slow_non_contiguous=True)
                    acopy(xcb, xc)
                    yield
                    mm(gC[:, :NT], wabd[:, g, :], xcb, True)
                    mm(gD[:, :NT], wabd[:, 4 + g, :], xcb, True)
                    sigm(rr, gC[:, :NT], -1.0, drv[:, 48 + g:49 + g])
                    yield
                    sigm(ii, gD[:, :NT], -1.0, drv[:, 52 + g:53 + g])
                    yield
                    afunc(aa, rr, AF.Exp, scale=c1[:, g:g + 1])
                    afunc(s_, aa, AF.Square)
                    afunc(s_, s_, AF.Ln, scale=-1.0, bias=1.0)
                    afunc(s_, s_, AF.Exp, scale=0.5)
                    yield
                    vtt(gx, ii, xc, ALU.mult)
                    vtt(bb, s_, gx, ALU.mult)
                    vscan(hseq, aa, bb, lh[:, g:g + 1])
                    vcopy(lh[:, g:g + 1], hseq[:, NV - 1:NV])
                    yield
                    pg_ = pE if g % 2 == 0 else pF
                    proj_fm(pg_, C_GL + g * 128, 128)
                    sigm(sg, pg_[:, :NT])
                    yield
                    vtt(sg, sg, pg_[:, :NT], ALU.mult)
                    vtt(mixT[:, g, :NT], hseq, sg, ALU.mult)
                    yield

                interleave(gen_lru(0), gen_lru(1))
                interleave(gen_lru(2), gen_lru(3))
                if gi == ngroups - 1:
                    dma("sp", o_lh.rearrange("(g p) -> p g", p=128), lh[:], allow_slow_non_contiguous=True)

                chk(5)
                for g in range(4):
                    pb_ = pA if g % 2 == 0 else pB
                    proj_fm(pb_, C_GA + g * 128, 128)
                    gth = A32[:, 512 + (g % 2) * 256:512 + (g % 2) * 256 + NT]
                    sigm(gth, pb_[:, :NT])
                    vtt(gaT[:, g, :NT], gth, pb_[:, :NT], ALU.mult)
                for g in range(2):
                    pb_ = pE if g % 2 == 0 else pF
                    proj_fm(pb_, C_QI + g * 128, 128)
                    vcopy(qiT[:, g, :NT], pb_[:, :NT])
                proj_fm(pA, C_KI, 64, 0)
                proj_fm(pA, C_KI, 64, 64)
                acopy(kiT[:, P0 + t0:P0 + t0 + NT], pA[:, :NT])

                def gen_conv(g):
                    pb_ = (pC, pD, pE, pF, pG, pH)[g % 6]
                    proj_fm(pb_, C_XBC + g * 128, 128)
                    roff = (g % 6) * 516
                    acc = A32[:, roff + 260:roff + 260 + NT]
                    raw = conv(pb_, shist, g, PP_SCW + g * 4, PP_SCB + g, acc, roff)
                    if gi == ngroups - 1:
                        dma("sp", o_sc.rearrange("j (g p) -> p g j", p=128)[:, g, :], raw[:, NV:NV + 3], allow_slow_non_contiguous=True)
                    yield
                    cth = A32[:, roff:roff + NT]
                    sigm(cth, acc)
                    yield
                    vtt(xbcT[:, g, :NT], cth, acc, ALU.mult)
                    yield

                for g6 in range(0, 12, 6):
                    interleave_n(*[gen_conv(g6 + k) for k in range(6)])

                chk(6)
                def gen_p2b(ti):
                    tsl = slice(ti * TT, (ti + 1) * TT)
                    r0 = t0 + ti * TT
                    TV = min(TT, NV - ti * TT)
                    kb = (P0 + r0) // 128
                    koff = (P0 + r0) % 128
                    qC, qD, qE, qF = (pC, pD, pE, pF) if ti % 2 == 0 else (pA, pB, pG, pH)
                    tb = (ti % 2) * 1536
                    for kc in range(8):
                        mm(qC[:TT, 0:512], hT[:, kc, tsl], win[:, kc, C_Q:C_Q + 512], kc == 0, kc == 7)
                    for (o0, c0, n_) in ((0, C_K, 256), (256, C_KI, 68), (324, C_DT, 16)):
                        for kc in range(8):
                            mm(qD[:TT, o0:o0 + n_], hT[:, kc, tsl], win[:, kc, c0:c0 + n_], kc == 0, kc == 7)
                    yield
                    for hf in range(2):
                        pz = qE if hf == 0 else qF
                        for kc in range(8):
                            mm(pz[:TT, :], hT[:, kc, tsl], win[:, kc, C_Z + hf * 512:C_Z + (hf + 1) * 512], kc == 0, kc == 7)
                        zth = A32[:TT, tb + 512 + hf * 512:tb + 1024 + hf * 512]
                        sigm(zth, pz[:TT, :])
                        vtt(sz[:TT, ti, hf * 512:(hf + 1) * 512], zth, pz[:TT, :], ALU.mult)
                        yield
                    sqt = A32[:TT, tb + 512:tb + 1024]
                    qn = A32[:TT, tb + 1024:tb + 1536]
                    kv32 = A32[:TT, tb + 1536:tb + 1792]
                    qhat = AB[:TT, 2304 + (ti % 2) * 640:2816 + (ti % 2) * 640]
                    khat = AB[:TT, 2816 + (ti % 2) * 640:2944 + (ti % 2) * 640]
                    afunc(sqt, qC[:TT, :], AF.Square)
                    vreduce(smb[:TT, ti % 2, 8:16], sqt.rearrange("p (h d) -> p h d", h=8), ALU.add)
                    rstd_pow(smb[:TT, ti % 2, 24:32], smb[:TT, ti % 2, 8:16], 64, smb[:TT, ti % 2, 16:24])
                    vtt(qn.rearrange("p (h d) -> p h d", h=8), qC[:TT, :].rearrange("p (h d) -> p h d", h=8), bc(smb[:TT, ti % 2, 24:32], 2, 64), ALU.mult)
                    vtt(qhat.rearrange("p (m g d) -> p g m d", m=4, g=2), qn.rearrange("p (g m d) -> p g m d", g=2, m=4),
                        bc(bc(gq8[:TT, :], 1, 4), 1, 2), ALU.mult)
                    yield
                    afunc(sqt[:, 0:128], qD[:TT, 0:128], AF.Square)
                    vreduce(smb[:TT, ti % 2, 32:34], sqt[:, 0:128].rearrange("p (h d) -> p h d", h=2), ALU.add)
                    rstd_pow(smb[:TT, ti % 2, 36:38], smb[:TT, ti % 2, 32:34], 64, smb[:TT, ti % 2, 34:36])
                    vtt(qn[:, 0:128].rearrange("p (h d) -> p h d", h=2), qD[:TT, 0:128].rearrange("p (h d) -> p h d", h=2), bc(smb[:TT, ti % 2, 36:38], 2, 64), ALU.mult)
                    vtt(kv32[:, 0:128].rearrange("p (h d) -> p h d", h=2), qn[:, 0:128].rearrange("p (h d) -> p h d", h=2),
                        bc(pbt[:TT, PB_GK:PB_GK + 64], 1, 2), ALU.mult)
                    acopy(khat, kv32[:, 0:128])
                    acopy(kv32[:, 128:256], qD[:TT, 128:256])
                    acopy(vtok[koff:koff + TT, kb, :], qD[:TT, 128:256])
                    dma("sp", o_ak[r0:r0 + TV, :], kv32[:TV, 0:128])
                    dma("sp", o_av[r0:r0 + TV, :], kv32[:TV, 128:256])
                    vcopy(kiw[:TT, ti, :], qD[:TT, 256:324])
                    dma("sp", o_ik[r0:r0 + TV, :], kiw[:TV, ti, 0:64])
                    yield
                    vtt(smb[:TT, ti % 2, 40:56], qD[:TT, 324:340], pbt[:TT, PB_DTB:PB_DTB + 16], ALU.add)
                    afunc(smb[:TT, ti % 2, 56:72], smb[:TT, ti % 2, 40:56], AF.Exp)
                    afunc(dtt[:TT, ti, :], smb[:TT, ti % 2, 56:72], AF.Ln, bias=1.0)
                    if TV < TT:
                        vmemset(dtt[TV:TT, ti, :], 0.0)
                    vtt(dAt[:TT, ti, :], dtt[:TT, ti, :], aneg[:TT, :], ALU.mult)
                    yield
                    mm(qE[:, 0:TT], khat, identb[:TT, :TT], True)
                    vcopy(khT[:, P0 + r0:P0 + r0 + TT], qE[:, 0:TT])
                    for m in range(4):
                        mm(qF[:, m * TT:(m + 1) * TT], qhat[:, m * 128:(m + 1) * 128], identb[:TT, :TT], m == 0, m == 3)
                    vcopy(qhT[:, :, tsl], qF[:, 0:4 * TT].rearrange("p (c t) -> p c t", c=4))
                    yield

                if ntile == 2:
                    interleave(gen_p2b(0), gen_p2b(1))
                else:
                    drain(gen_p2b(0))

                chk(7)
                for ti in range(ntile):
                    tsl = slice(ti * TT, (ti + 1) * TT)
                    if ti == 1:
                        chk(7.9)
                    dA = dAt[:TT, ti, :]
                    dtv = dtt[:TT, ti, :]
                    split3(dAs[:TT], dA, sm[:TT, 176:192], sm[:TT, 192:208])
                    for k in range(3):
                        mm(pF[:TT, 0:16], trib[:TT, :TT], dAs[:TT, k, :], k == 0, k == 2)
                    for k in range(3):
                        mm(pF[:TT, 16:32], astrb[:TT, :TT], dAs[:TT, k, :], k == 0, k == 2)
                    for k in range(3):
                        mm(pF[:, 32:48], onesb[:TT, :], dAs[:TT, k, :], k == 0, k == 2)
                    ecum = sm[:TT, 80:96]
                    toend = sm[:TT, 96:112]
                    dec = sm[:, 112:128]
                    afunc(sm[:TT, 80:112], pF[:TT, 0:32], AF.Exp)
                    afunc(dec, pF[:, 32:48], AF.Exp)
                    chk(7.1)
                    for c in range(8):
                        pb_ = pC if c < 4 else pD
                        mm(pb_[:TT, (c % 4) * 128:(c % 4 + 1) * 128], xbcT[:, c, tsl], identb, c % 4 == 0, c % 4 == 3)
                    xt_ = AB[:TT, 0:1024]
                    x2_ = AB[:TT, 1024:2048]
                    xD = A32[:TT, 0:1024]
                    for hf in range(2):
                        pTv = (pC if hf == 0 else pD)[:TT, :].rearrange("p (h d) -> p h d", h=8)
                        hs_ = slice(hf * 512, (hf + 1) * 512)
                        vtt(xt_[:, hs_].rearrange("p (h d) -> p h d", h=8), pTv, bc(dtv[:, hf * 8:(hf + 1) * 8], 2, 64), ALU.mult)
                        vtt(xD[:, hs_].rearrange("p (h d) -> p h d", h=8), pTv, bc(pbt[:TT, PB_D + hf * 8:PB_D + hf * 8 + 8], 2, 64), ALU.mult)
                    vtt(x2_.rearrange("p (h d) -> p h d", h=16), xt_.rearrange("p (h d) -> p h d", h=16), bc(toend, 2, 64), ALU.mult)
                    def gen_ssd(g, ti=ti, tsl=tsl, xt_=xt_, x2_=x2_, xD=xD, ecum=ecum, dec=dec):
                        eb = 2048 + g * 1280
                        E = AB[:TT, eb:eb + 8 * TT].rearrange("p (h l) -> p h l", h=8)
                        WT = E
                        G = AB[:TT, eb + 1024:eb + 1024 + TT]
                        bmtok = AB[:TT, eb + 1152:eb + 1280]
                        ycb = AB[:TT, eb:eb + 512]
                        t1 = A32[:TT, 1024 + g * 1024:1536 + g * 1024]
                        yz = A32[:TT, 1536 + g * 1024:2048 + g * 1024]
                        dbk = (pA, pB) if g == 0 else (pG, pH)
                        cbk = pC if g == 0 else pF
                        ydk = pD if g == 0 else pG
                        yok = pE if g == 0 else pH
                        ytk = dbk[0]
                        hpb = 512 // TT
                        for bk in range(8 // hpb):
                            pb_ = dbk[bk % 2]
                            for k in range(2):
                                Rk = RB[:TT, (g * 2 + bk + k) % 2, 0:hpb * TT]
                                h0_ = g * 8 + bk * hpb
                                vtt(Rk.rearrange("p (h l) -> p h l", h=hpb), bc(dAs[:TT, k, h0_:h0_ + hpb], 2, TT), bc(trib[:TT, :TT], 1, hpb), ALU.mult)
                                mm(pb_[:TT, 0:hpb * TT], astrb[:TT, :TT], Rk, k == 0, k == 1)
                            afunc(E[:, bk * hpb:(bk + 1) * hpb, :], pb_[:TT, 0:hpb * TT].rearrange("p (h l) -> p h l", h=hpb), AF.Exp)
                            yield
                        mm(cbk[:TT, :TT], xbcT[:, 8 + g, tsl], xbcT[:, 10 + g, tsl], True)
                        vtt(G, cbk[:TT, :TT], tri[:TT, :TT], ALU.mult)
                        vtt(WT, E, bc(G, 1, 8), ALU.mult)
                        yield
                        for hh in range(8):
                            h_ = g * 8 + hh
                            mm(ydk[:TT, hh * 64:(hh + 1) * 64], WT[:, hh, :], xt_[:, h_ * 64:(h_ + 1) * 64], hh == 0, hh == 7)
                        mm(yok[:TT, :], xbcT[:, 10 + g, tsl], hsb[:, g * 512:(g + 1) * 512], True)
                        yield
                        vtt(t1.rearrange("p (h d) -> p h d", h=8), yok[:TT, :].rearrange("p (h d) -> p h d", h=8), bc(ecum[:, g * 8:(g + 1) * 8], 2, 64), ALU.mult)
                        vtt(t1, t1, ydk[:TT, :], ALU.add)
                        vtt(t1, t1, xD[:, g * 512:(g + 1) * 512], ALU.add)
                        vtt(yz, t1, sz[:TT, ti, g * 512:(g + 1) * 512], ALU.mult)
                        yield
                        afunc(t1, yz, AF.Square, accum=sm[:TT, 128 + g:129 + g])
                        rstd_pow(sm[:TT, 132 + g:133 + g], sm[:TT, 128 + g:129 + g], 512, sm[:TT, 130 + g:131 + g])
                        vts(t1, yz, sm[:TT, 132 + g:133 + g], ALU.mult)
                        vtt(ycb, t1, pbt[:TT, PB_GSSD + g * 512:PB_GSSD + (g + 1) * 512], ALU.mult)
                        yield
                        for c in range(4):
                            mm(ytk[:, c * TT:(c + 1) * TT], ycb[:, c * 128:(c + 1) * 128], identb[:TT, :TT], c == 0, c == 3)
                        vcopy(mixT[:, 8 + g * 4:12 + g * 4, tsl], ytk[:, 0:4 * TT].rearrange("p (c t) -> p c t", c=4))
                        yield
                        mm(cbk[:TT, 0:128], xbcT[:, 8 + g, tsl], identb, True)
                        acopy(bmtok, cbk[:TT, 0:128])
                        mm(cbk[:, :], bmtok, x2_[:, g * 512:(g + 1) * 512], True)
                        hv = hst[:, g * 512:(g + 1) * 512]
                        vtt(hv.rearrange("p (h d) -> p h d", h=8), hv.rearrange("p (h d) -> p h d", h=8), bc(dec[:, g * 8:(g + 1) * 8], 2, 64), ALU.mult)
                        vtt(hv, hv, cbk[:, :], ALU.add)
                        acopy(hsb[:, g * 512:(g + 1) * 512], hv)
                        yield

                    interleave(gen_ssd(0), gen_ssd(1))
                if gi == ngroups - 1:
                    stt = A32[:, 0:1024].rearrange("p (c n) -> p c n", c=8)
                    sp3 = AB[:, 0:3072].rearrange("p (k n) -> p k n", k=3)
                    split3(sp3, hst[:, :], A32[:, 1024:2048], A32[:, 2048:3072])
                    for c in range(8):
                        pbank = pA if c < 4 else pB
                        for k in range(3):
                            mm(pbank[:, (c % 4) * 128:(c % 4 + 1) * 128], sp3[:, k, c * 128:(c + 1) * 128], identb, c % 4 == 0 and k == 0, c % 4 == 3 and k == 2)
                    vcopy(stt[:, 0:4, :], pA[:, :].rearrange("p (c n) -> p c n", c=4))
                    vcopy(stt[:, 4:8, :], pB[:, :].rearrange("p (c n) -> p c n", c=4))
                    dma("sp", o_sh.rearrange("(c p) n -> p c n", p=128), stt)

                chk(8)
                def p4_vars(ti):
                    q0 = P0 + t0 + ti * TT
                    Lk = q0 + TT
                    return (slice(ti * TT, (ti + 1) * TT), q0, Lk, (Lk + 127) // 128, A32[:TT, 0:Lk], AB[:TT, 0:Lk], JK[:TT, 0:Lk])

                def gen_topk(ti):
                    tsl, q0, Lk, nkb, score, negm, junkb = p4_vars(ti)
                    for c0 in range(0, Lk, 512):
                        c1_ = min(Lk, c0 + 512)
                        for hh in range(4):
                            pb_ = pA if hh % 2 == 0 else pB
                            pr = (hh % 2) * 64
                            mm(pb_[:TT, 0:c1_ - c0], qiT[pr:pr + 64, hh // 2, tsl], kiT[pr:pr + 64, c0:c1_], True)
                            rl = A32[:TT, 2048 + (hh % 2) * 512:2560 + (hh % 2) * 512]
                            afunc(rl[:, 0:c1_ - c0], pb_[:TT, 0:c1_ - c0], AF.Relu)
                            if hh == 0:
                                vts(score[:, c0:c1_], rl[:, 0:c1_ - c0], kiw[:TT, ti, 64:65], ALU.mult)
                            else:
                                vstt(score[:, c0:c1_], rl[:, 0:c1_ - c0], kiw[:TT, ti, 64 + hh:65 + hh], score[:, c0:c1_], ALU.mult, ALU.add)
                            yield
                    lo = sm[:TT, 140:141]
                    if Lk > ktop:
                        amax = sm[:TT, 141:142]
                        hw = sm[:TT, 144:144 + NBIS + 1]
                        vreduce(amax, score, ALU.max, absval=True)
                        vtt(score[:, Lk - 128:Lk], score[:, Lk - 128:Lk], dmask[:, :], ALU.add)
                        vts(amax, amax, 1.001, ALU.mult, 1e-3, ALU.add)
                        vts(hw, p2row[:TT, :], amax, ALU.mult)
                        mid = sm[:TT, 142:143]
                        cnt = sm[:TT, 143:144]
                        ind = sm[:TT, 170:171]
                        vmemset(mid, 0.0)
                        yield
                        for it in range(NBIS):
                            vts(junkb, score, mid, ALU.is_gt, 0.0, ALU.add, accum=cnt)
                            vstt(ind, cnt, float(ktop) - 0.5, hw[:, it:it + 1], ALU.is_gt, ALU.mult)
                            vstt(mid, ind, hw[:, it + 1:it + 2], mid, ALU.subtract, ALU.add)
                            yield
                        vtt(lo, mid, hw[:, NBIS:NBIS + 1], ALU.subtract)
                    else:
                        vtt(score[:, Lk - 128:Lk], score[:, Lk - 128:Lk], dmask[:, :], ALU.add)
                        vmemset(lo, NEG / 2)

                def do_mask(ti):
                    tsl, q0, Lk, nkb, score, negm, junkb = p4_vars(ti)
                    lo = sm[:TT, 140:141]
                    if Lk > ktop:
                        c0 = sm[:TT, 171:172]
                        mrem = sm[:TT, 172:173]
                        vts(junkb, score, 0.0, ALU.is_gt, 0.0, ALU.add, accum=c0)
                        vts(mrem, c0, -1.0, ALU.mult, float(ktop), ALU.add)
                        vts(negm, score, 0.0, ALU.is_equal)
                        vscan(negm, onesb[:TT, 0:1].broadcast_to([TT, Lk]), negm, 0.0)
                        vts(negm, negm, mrem, ALU.is_gt)
                        vstt(negm, score, 0.0, negm, ALU.is_equal, ALU.mult)
                        vstt(negm, score, lo, negm, ALU.is_le, ALU.max)
                        vts(negm, negm, NEG, ALU.mult)
                    else:
                        vts(negm, score, lo, ALU.is_le, NEG, ALU.mult)

                def gen_attn(ti):
                    tsl, q0, Lk, nkb, score, negm, junkb = p4_vars(ti)
                    for g in range(2):
                        for jb in range(nkb):
                            S = min(128, Lk - jb * 128)
                            pl = pC if jb % 2 == 0 else pD
                            plv = pl[:S, 0:4 * TT].rearrange("p (m t) -> p m t", m=4)
                            mm(plv, negm[:, jb * 128:jb * 128 + S], bc(identb[:TT, :TT], 1, 4), True, False)
                            dl = jb * 128 - q0
                            mm(plv, khT[g * 64:(g + 1) * 64, jb * 128:jb * 128 + S], qhT[g * 64:(g + 1) * 64, :, tsl], False, dl <= -256)
                            if dl > -256:
                                si = 0 if dl == 0 else 1
                                for m in range(4):
                                    mm(plv[:, m, :], hk[:TT, si, g * 4 + m, :S], j128b[:TT, :TT], False, m == 3)
                            ET = AB[:S, 2048 + (jb % 2) * 512:2048 + (jb % 2) * 512 + 4 * TT].rearrange("p (m t) -> p m t", m=4)
                            afunc(ET, plv, AF.Exp)
                            pov = pE[:, 0:2 * TT].rearrange("p (a t) -> p a t", a=2)
                            pdv = pF[:, 0:2 * TT].rearrange("p (a t) -> p a t", a=2)
                            for par in range(2):
                                mm(pov[par * 64:(par + 1) * 64], vtok[:S, jb, g * 64:(g + 1) * 64], ET[:, par::2, :], jb == 0, jb == nkb - 1)
                                mm(pdv[par * 64:(par + 1) * 64], onesb[:S, 0:64], ET[:, par::2, :], jb == 0, jb == nkb - 1)
                            yield
                        rden = A32[:, 3072:3072 + 2 * TT]
                        vrecip(rden, pF[:, 0:2 * TT])
                        vtt(rden, pE[:, 0:2 * TT], rden, ALU.mult)
                        vtt(mixT[:, 4 + 2 * g:6 + 2 * g, tsl], rden.rearrange("p (a t) -> p a t", a=2), gaT[:, 2 * g:2 * g + 2, tsl], ALU.mult)
                        yield

                def gen_p5(ti):
                    tsl = slice(ti * TT, (ti + 1) * TT)
                    r0 = t0 + ti * TT
                    TV = min(TT, NV - ti * TT)
                    base = (ti % 2) * 1024
                    xr = A32[:TT, base:base + 1024]
                    if TV < TT:
                        vmemset(A32[TV:TT, base:base + 1024], 0.0)
                    dma("sp", A32[:TV, base:base + 1024], xin[r0:r0 + TV, :])
                    yield
                    for hf in range(2):
                        pb_ = pA if hf == 0 else pB
                        for ec in range(16):
                            mm(pb_[:TT, :], mixT[:, ec, tsl], wout[:, ec, hf * 512:(hf + 1) * 512], ec == 0, ec == 15)
                            if ec % 4 == 3:
                                yield
                        vtt(xr[:, hf * 512:(hf + 1) * 512], pb_[:TT, :], xr[:, hf * 512:(hf + 1) * 512], ALU.add)
                        yield
                    dma("sp", xout[r0:r0 + TV, :], xr[:TV])
                    yield

                drain(gen_topk(0))
                do_mask(0)
                if ntile == 2:
                    interleave(gen_attn(0), gen_topk(1))
                    do_mask(1)
                    interleave(gen_attn(1), gen_p5(0))
                else:
                    drain(gen_attn(0))

                chk(9)
                if DEBUG_MIX[0] and not isS and qi_ == 0 and l == 0:
                    dma("sp", O["dbg"][:, :, t0:t0 + NT].rearrange("c p t -> p c t"), mixT[:, :, :NT])
                drain(gen_p5(1 if ntile == 2 else 0))


def _t5_bucket_np(rel):
    import math
    import jax
    import jax.numpy as jnp
    with jax.default_device(jax.devices("cpu")[0]):
        return _t5_bucket_cpu(rel, math, jnp)


def _t5_bucket_cpu(rel, math, jnp):
    rel = jnp.asarray(rel, dtype=jnp.int32)
    half, max_exact = 16, 8
    n = jnp.abs(rel)
    large = max_exact + (jnp.log(jnp.maximum(n, 1).astype(jnp.float32) / max_exact) / math.log(128 / max_exact) * (half - max_exact)).astype(jnp.int32)
    large = jnp.minimum(large, half - 1)
    return np.asarray(jnp.where(rel > 0, half, 0) + jnp.where(n < max_exact, n, large))


def _consts():
    c = np.zeros((128, NCST), np.float32)
    i = np.arange(128)
    c[:, 0:128] = np.eye(128)
    c[:, 128:256] = (i[:, None] <= i[None, :])
    c[:, 256:384] = (i[:, None] > i[None, :])
    c[:, 384:512] = (i[:, None] == 127 - i[None, :])
    c[:64, 512:576] = (i[:64, None] == 63 - i[None, :64])
    c[:, 640:768] = np.where((i[None, :] // 64) > (i[:, None] // 64), NEG, 0.0)
    bk = _t5_bucket_np(np.arange(384) - 255)
    c[0:32, 768:1152] = (np.arange(32)[:, None] == bk[None, :])
    return c


_PROG_CACHE = {}


def _run(inputs, NCORES, NSP, TP, SAMPLE, PS, TS, DEPTH):
    key = (NSP, TP, SAMPLE, PS, TS, DEPTH)
    f32 = np.float32
    g = lambda k: np.ascontiguousarray(np.asarray(inputs[k], dtype=f32))
    w_in, w_out = g("w_in"), g("w_out")
    cst = _consts()
    pp = np.zeros((DEPTH, 128, NPP), f32)
    pb = np.zeros((DEPTH, 128, NPB), f32)
    wabd = np.zeros((DEPTH, 128, 2, 4, 128), f32)
    fm = lambda v, n: v.reshape(n, 128).T
    for l in range(DEPTH):
        pp[l, :, PP_GN:PP_GN + 8] = fm(g("norm_w")[l], 8)
        lcw = g("lru_conv_w")[l]
        pp[l, :, PP_LCW:PP_LCW + 16] = lcw.reshape(4, 4, 128).transpose(2, 1, 0).reshape(128, 16)
        pp[l, :, PP_LCB:PP_LCB + 4] = fm(g("lru_conv_b")[l], 4)
        pp[l, :, PP_LBA:PP_LBA + 4] = fm(g("lru_b_a")[l], 4)
        pp[l, :, PP_LBX:PP_LBX + 4] = fm(g("lru_b_x")[l], 4)
        pp[l, :, PP_LAM:PP_LAM + 4] = fm(g("lru_lambda")[l], 4)
        scw = g("ssd_conv_w")[l]
        pp[l, :, PP_SCW:PP_SCW + 48] = scw.reshape(4, 12, 128).transpose(2, 1, 0).reshape(128, 48)
        pp[l, :, PP_SCB:PP_SCB + 12] = fm(g("ssd_conv_b")[l], 12)
        pb[l, :, PB_GQ:PB_GQ + 64] = g("att_q_norm")[l][None, :]
        pb[l, :, PB_GK:PB_GK + 64] = g("att_k_norm")[l][None, :]
        pb[l, :, PB_DTB:PB_DTB + 16] = g("ssd_dt_bias")[l][None, :]
        pb[l, :, PB_ALOG:PB_ALOG + 16] = g("ssd_a_log")[l][None, :]
        pb[l, :, PB_D:PB_D + 16] = g("ssd_d")[l][None, :]
        pb[l, :, PB_GSSD:PB_GSSD + 1024] = g("ssd_norm")[l][None, :]
        pb[l, :, PB_RB15:PB_RB15 + 8] = g("rel_bias")[15][None, :]
        for a, nm in enumerate(("lru_w_a", "lru_w_x")):
            w = g(nm)[l]
            for gg in range(4):
                wabd[l, 0:64, a, gg, 0:64] = w[2 * gg]
                wabd[l, 64:128, a, gg, 64:128] = w[2 * gg + 1]
    wabd = wabd.reshape(DEPTH, 128, 1024)
    xp = g("x_prompt")
    in_maps = []
    for c in range(NCORES):
        m = {"xp": xp[c * NSP:(c + 1) * NSP], "w_in": w_in, "w_out": w_out, "pp": pp, "pb": pb, "wabd": wabd,
             "rb": g("rel_bias"), "cst": cst}
        if SAMPLE:
            m["xs"] = g("x_sample")[c]
            m["ck"] = g("cache_att_k")[:, c].reshape(DEPTH, PS, 128)
            m["cv"] = g("cache_att_v")[:, c].reshape(DEPTH, PS, 128)
            m["cki"] = g("cache_idx_k")[:, c]
            m["slc"] = g("state_lru_conv")[:, c]
            m["slh"] = g("state_lru_h")[:, c]
            m["ssc"] = g("state_ssd_conv")[:, c]
            m["ssh"] = g("state_ssd_h")[:, c].reshape(DEPTH, 1024, 128)
        in_maps.append({k: np.ascontiguousarray(v) for k, v in m.items()})
    if key not in _PROG_CACHE:
        _PROG_CACHE[key] = build_program(NSP=NSP, TP=TP, SAMPLE=SAMPLE, PS=PS, TS=TS, DEPTH=DEPTH)
    nc = _PROG_CACHE[key]
    res = run_bass_kernel_spmd(nc, in_maps, core_ids=list(range(NCORES)))
    R = res.results
    cat = lambda k, ax: np.concatenate([np.asarray(r[k]) for r in R], axis=ax)
    stk = lambda k, ax: np.stack([np.asarray(r[k]) for r in R], axis=ax)
    B = NCORES * NSP
    outs = [cat("yp", 0)]
    if SAMPLE:
        outs.append(stk("ys", 0))
    outs += [cat("akp", 1).reshape(DEPTH, B, TP, 2, 64), cat("avp", 1).reshape(DEPTH, B, TP, 2, 64), cat("ikp", 1),
             cat("lcp", 1), cat("lhp", 1), cat("scp", 1), cat("shp", 1).reshape(DEPTH, B, 16, 64, 128)]
    if SAMPLE:
        outs += [stk("aks", 1).reshape(DEPTH, NCORES, TS, 2, 64), stk("avs", 1).reshape(DEPTH, NCORES, TS, 2, 64), stk("iks", 1),
                 stk("lcs", 1), stk("lhs", 1), stk("scs", 1), stk("shs", 1).reshape(DEPTH, NCORES, 16, 64, 128)]
    return tuple(np.ascontiguousarray(o, dtype=np.float32) for o in outs)


def kernel(**inputs):
    return _run(inputs, 8, 2, 2048, True, 1024, 64, 2)
```
